# Optimizing a Trainium2 kernel written in Bass

```python
import math
import jax, jax.numpy as jnp
from jax import lax
import numpy as np

D_MODEL = 1024
BATCH = 16
SEQ = 2048
DEPTH = 1

D_MIX = D_MODEL
D_SSM = D_MIX // 2
SSM_GROUP = 16
N_SSM_GROUPS = D_SSM // SSM_GROUP
SSM_STATE = 64
N_DIR = 2
D_FOURIER = D_MIX - D_SSM
N_FOURIER_HEADS = 4
FOURIER_HEAD_DIM = D_FOURIER // N_FOURIER_HEADS
D_FF = 2816
CONV_WIDTH = 3
RMS_EPS = 1e-6
DT_MIN = 1e-3
DT_MAX = 1e-1

kernel_name = "hymba_style_s5_fnet_convffn_encoder"


def rmsnorm(x, g):
    xf = x.astype(jnp.float32)
    y = xf * lax.rsqrt(jnp.mean(xf * xf, axis=-1, keepdims=True) + RMS_EPS)
    return (y * g.astype(jnp.float32)).astype(x.dtype)


def _ssm_combine(left, right):
    a_l, b_l = left
    a_r, b_r = right
    return a_r * a_l, a_r * b_l + b_r


def s5_direction(uf, lam_re, lam_im, log_dt, b_re, b_im, c_re, c_im, reverse):
    f32 = jnp.float32
    lam = lax.complex(lam_re.astype(f32), lam_im.astype(f32))
    dt = jnp.exp(log_dt.astype(f32))[:, None]
    lam_bar = jnp.exp(lam * dt)
    b = lax.complex(b_re.astype(f32), b_im.astype(f32))
    b_bar = ((lam_bar - 1.0) / lam)[..., None] * b
    c = lax.complex(c_re.astype(f32), c_im.astype(f32))
    bu = jnp.einsum('gph,blgh->blgp', b_bar, uf)
    a = jnp.broadcast_to(lam_bar, (1, uf.shape[1]) + lam_bar.shape)
    _, states = lax.associative_scan(_ssm_combine, (a, bu), reverse=reverse, axis=1)
    return jnp.einsum('ghp,blgp->blgh', c, states).real


def s5_mixer(u, lam_re, lam_im, log_dt, b_re, b_im, c_re, c_im, d_skip, w_glu, b_glu):
    bsz, L, _ = u.shape
    uf = u.astype(jnp.float32)
    ug = uf.reshape(bsz, L, N_SSM_GROUPS, SSM_GROUP)
    y = (s5_direction(ug, lam_re[0], lam_im[0], log_dt[0], b_re[0], b_im[0], c_re[0], c_im[0], False)
         + s5_direction(ug, lam_re[1], lam_im[1], log_dt[1], b_re[1], b_im[1], c_re[1], c_im[1], True))
    y = (y.reshape(bsz, L, D_SSM) + d_skip.astype(jnp.float32) * uf).astype(u.dtype)
    z = jax.nn.gelu(y)
    return z * jax.nn.sigmoid(z @ w_glu + b_glu)


def fourier_mixer(v, w_fourier):
    bsz, L, _ = v.shape
    vh = v.astype(jnp.float32).reshape(bsz, L, N_FOURIER_HEADS, FOURIER_HEAD_DIM)
    mixed = jnp.fft.fft2(vh, axes=(1, 3), norm='ortho').real.astype(v.dtype)
    out = jnp.einsum('blhc,hcd->blhd', mixed, w_fourier)
    return out.reshape(bsz, L, D_FOURIER)


def conv_ffn(h, w_up, conv_w, conv_b, w_down):
    L = h.shape[1]
    up = h @ w_up
    half = CONV_WIDTH // 2
    padded = jnp.pad(up, ((0, 0), (half, half), (0, 0)))
    conv = conv_b
    for k in range(CONV_WIDTH):
        conv = conv + conv_w[k] * padded[:, k:k + L]
    gate, val = jnp.split(conv, 2, axis=-1)
    return (jax.nn.gelu(gate) * val) @ w_down


def setup_inputs(seed: int = 0) -> dict:
    key = jax.random.key(seed)
    ks = jax.random.split(key, 24)
    f32 = jnp.float32
    G, P, H = N_SSM_GROUPS, SSM_STATE, SSM_GROUP

    def nrm(k, shape, scale):
        return jax.random.normal(k, shape, f32) * scale

    x = jax.random.normal(ks[0], (BATCH, SEQ, D_MODEL), f32)
    g_mix = 1.0 + nrm(ks[1], (DEPTH, D_MODEL), 0.02)
    w_in = nrm(ks[2], (DEPTH, D_MODEL, D_MIX), D_MODEL ** -0.5)
    ssm_lam_re = -0.5 * jnp.exp(nrm(ks[3], (DEPTH, N_DIR, G, P), 0.05))
    n_idx = jnp.arange(P, dtype=f32)
    ssm_lam_im = math.pi * n_idx + nrm(ks[4], (DEPTH, N_DIR, G, P), 0.01)
    ssm_log_dt = jax.random.uniform(ks[5], (DEPTH, N_DIR, G), f32,
                                    math.log(DT_MIN), math.log(DT_MAX))
    b_scale = (2.0 * H) ** -0.5
    ssm_b_re = nrm(ks[6], (DEPTH, N_DIR, G, P, H), b_scale)
    ssm_b_im = nrm(ks[7], (DEPTH, N_DIR, G, P, H), b_scale)
    c_scale = P ** -0.5
    ssm_c_re = nrm(ks[8], (DEPTH, N_DIR, G, H, P), c_scale)
    ssm_c_im = nrm(ks[9], (DEPTH, N_DIR, G, H, P), c_scale)
    ssm_d = nrm(ks[10], (DEPTH, D_SSM), 1.0)
    w_glu = nrm(ks[11], (DEPTH, D_SSM, D_SSM), D_SSM ** -0.5)
    b_glu = nrm(ks[12], (DEPTH, D_SSM), 0.02)
    w_fourier = nrm(ks[13], (DEPTH, N_FOURIER_HEADS, FOURIER_HEAD_DIM, FOURIER_HEAD_DIM),
                    FOURIER_HEAD_DIM ** -0.5)
    w_out = nrm(ks[14], (DEPTH, D_MIX, D_MODEL), D_MIX ** -0.5)
    g_ffn = 1.0 + nrm(ks[15], (DEPTH, D_MODEL), 0.02)
    w_up = nrm(ks[16], (DEPTH, D_MODEL, 2 * D_FF), D_MODEL ** -0.5)
    conv_w = nrm(ks[17], (DEPTH, CONV_WIDTH, 2 * D_FF), CONV_WIDTH ** -0.5)
    conv_b = nrm(ks[18], (DEPTH, 2 * D_FF), 0.02)
    w_down = nrm(ks[19], (DEPTH, D_FF, D_MODEL), D_FF ** -0.5)
    g_final = 1.0 + nrm(ks[20], (D_MODEL,), 0.02)
    return {
        "x": x, "g_mix": g_mix, "w_in": w_in,
        "ssm_lam_re": ssm_lam_re, "ssm_lam_im": ssm_lam_im, "ssm_log_dt": ssm_log_dt,
        "ssm_b_re": ssm_b_re, "ssm_b_im": ssm_b_im, "ssm_c_re": ssm_c_re, "ssm_c_im": ssm_c_im,
        "ssm_d": ssm_d, "w_glu": w_glu, "b_glu": b_glu, "w_fourier": w_fourier,
        "w_out": w_out, "g_ffn": g_ffn, "w_up": w_up, "conv_w": conv_w, "conv_b": conv_b,
        "w_down": w_down, "g_final": g_final,
    }


def reference(x, g_mix, w_in, ssm_lam_re, ssm_lam_im, ssm_log_dt, ssm_b_re, ssm_b_im,
              ssm_c_re, ssm_c_im, ssm_d, w_glu, b_glu, w_fourier, w_out, g_ffn, w_up,
              conv_w, conv_b, w_down, g_final):
    h = x
    for i in range(DEPTH):
        xn = rmsnorm(h, g_mix[i])
        proj = xn @ w_in[i]
        u = proj[..., :D_SSM]
        v = proj[..., D_SSM:]
        y_ssm = s5_mixer(u, ssm_lam_re[i], ssm_lam_im[i], ssm_log_dt[i], ssm_b_re[i],
                         ssm_b_im[i], ssm_c_re[i], ssm_c_im[i], ssm_d[i], w_glu[i], b_glu[i])
        y_fft = fourier_mixer(v, w_fourier[i])
        h = h + jnp.concatenate([y_ssm, y_fft], axis=-1) @ w_out[i]
        h = h + conv_ffn(rmsnorm(h, g_ffn[i]), w_up[i], conv_w[i], conv_b[i], w_down[i])
    return rmsnorm(h, g_final)
```

```python
import math
from contextlib import ExitStack

import numpy as np
import ml_dtypes
import concourse.bass as bass
import concourse.mybir as mybir
from concourse.bass_utils import run_bass_kernel_spmd

F32 = mybir.dt.float32
BF16 = mybir.dt.bfloat16
AF = mybir.ActivationFunctionType
ALU = mybir.AluOpType

NCORES = 8
L = 2048
DM = 1024
NTT = 16
DFF = 2816
NFT = 22
TC = 16
NCH = L // TC
EPS = 1e-6
FFN_PARTS = [4, 4, 4, 4, 3, 3]
MAGIC = 12582912.0


class _Op:
    __slots__ = ("eng", "fn", "deps", "dma", "idx", "sem", "val", "prev", "has_dep", "phase")


class Prog:
    ENGS = ("pe", "act", "dve", "pool", "sp")

    def __init__(self):
        self.ops = []
        self.lastw = {}
        self.readers = {}
        self.bufbar = {}
        self.phase = "init"
        self.scopes = False

    def op(self, eng, fn, reads=(), writes=(), dma=False):
        psr = [r for r in reads if r[0] == "ps"]
        if psr:
            reads = [r for r in reads if r[0] != "ps"]
            writes = list(writes) + [r for r in psr if r not in writes]
        i = len(self.ops)
        deps = set()
        for r in reads:
            w = self.lastw.get(r)
            if w is None:
                w = self.bufbar.get(r[0])
            if w is not None:
                deps.add(w)
        for w_ in writes:
            lw = self.lastw.get(w_)
            if lw is None:
                lw = self.bufbar.get(w_[0])
            if lw is not None:
                deps.add(lw)
            deps.update(self.readers.get(w_, ()))
        o = _Op()
        o.eng, o.fn, o.deps, o.dma, o.idx = eng, fn, deps, dma, i
        o.phase = self.phase
        self.ops.append(o)
        for r in reads:
            self.readers.setdefault(r, []).append(i)
        for w_ in writes:
            self.lastw[w_] = i
            self.readers[w_] = []
        return i

    def barrier(self, eng, fn, names):
        names = set(names)
        deps = set()
        for r in list(self.lastw.keys()):
            if r[0] in names:
                deps.add(self.lastw.pop(r))
        for r in list(self.readers.keys()):
            if r[0] in names:
                deps.update(self.readers.pop(r))
        for n in names:
            if n in self.bufbar:
                deps.add(self.bufbar[n])
        i = len(self.ops)
        o = _Op()
        o.eng, o.fn, o.deps, o.dma, o.idx = eng, fn, deps, False, i
        o.phase = self.phase
        self.ops.append(o)
        for n in names:
            self.bufbar[n] = i
        return i

    def emit(self, nc, block, csem, rings):
        ops = self.ops
        for o in ops:
            o.has_dep = False
        for o in ops:
            for d in o.deps:
                ops[d].has_dep = True
        cnt = {e: 0 for e in self.ENGS}
        dcnt = {e: 0 for e in self.ENGS}
        for o in ops:
            o.prev = None
            if o.dma:
                ring = rings[o.eng]
                j = dcnt[o.eng]
                dcnt[o.eng] += 1
                R = len(ring)
                o.sem = ring[j % R]
                o.val = 16 * (j // R + 1)
                if j >= R:
                    o.prev = (ring[j % R], 16 * (j // R))
            elif o.has_dep:
                cnt[o.eng] += 1
                o.sem = csem[o.eng]
                o.val = cnt[o.eng]
        self.counts = (cnt, dcnt)

        def run(engname, e):
            known = {}
            cur = [None]
            cur_id = [None]
            for o in ops:
                if o.eng != engname:
                    continue
                if self.scopes and o.phase != cur[0]:
                    if cur[0] is not None:
                        nc.leave_named_scope(cur[0], cur_id[0], False)
                    cur_id[0] = nc.enter_named_scope(o.phase, False)[0]
                    cur[0] = o.phase
                waits = {}
                for d in o.deps:
                    p = ops[d]
                    if engname == "pe" and p.eng == "pe" and not p.dma:
                        continue
                    k = id(p.sem)
                    if k not in waits or waits[k][1] < p.val:
                        waits[k] = (p.sem, p.val)
                if o.prev is not None:
                    k = id(o.prev[0])
                    if k not in waits or waits[k][1] < o.prev[1]:
                        waits[k] = o.prev
                wl = []
                for k, (sm, v) in waits.items():
                    if known.get(k, 0) >= v:
                        continue
                    known[k] = v
                    wl.append((sm, v))
                if o.fn is None:
                    for sm, v in wl:
                        e.wait_ge(sm, v)
                    continue
                for sm, v in wl[:-1]:
                    e.wait_ge(sm, v)
                ins = o.fn(e)
                if wl:
                    ins._wait_ge(wl[-1][0], wl[-1][1])
                if o.dma:
                    ins.then_inc(o.sem, 16)
                elif o.has_dep:
                    ins.then_inc(o.sem, 1)
            if self.scopes and cur[0] is not None:
                nc.leave_named_scope(cur[0], cur_id[0], False)

        block.tensor(lambda e: run("pe", e))
        block.scalar(lambda e: run("act", e))
        block.vector(lambda e: run("dve", e))
        block.gpsimd(lambda e: run("pool", e))
        block.sync(lambda e: run("sp", e))


def _host_consts():
    c = {}
    c["identb"] = np.eye(128, dtype=np.float32).astype(ml_dtypes.bfloat16)
    c["identf"] = np.eye(128, dtype=np.float32)
    n = np.arange(128)
    ang = 2 * np.pi * np.outer(n, n) / 128.0
    c["cdft"] = (np.cos(ang) / math.sqrt(128.0)).astype(np.float32)
    c["sdft"] = (np.sin(ang) / math.sqrt(128.0)).astype(np.float32)
    l = np.arange(L)
    lk = np.outer(l, l) % L
    angL = 2 * np.pi * lk / L
    CL = (np.cos(angL) / math.sqrt(L)).astype(np.float32)
    SL = (np.sin(angL) / math.sqrt(L)).astype(np.float32)

    def lay(M):
        return M.reshape(16, 128, 16, 128).transpose(2, 1, 0, 3)
    c["dftT"] = np.ascontiguousarray(np.stack([lay(CL), lay(SL)], axis=2)).astype(ml_dtypes.bfloat16)
    p = np.arange(128)
    msk = np.zeros((128, 8), np.float32)
    msk[:, 0] = (p < 64)
    msk[:, 1] = (p >= 64)
    msk[:, 2] = ((p // 16) % 2 == 0)
    msk[:, 3] = ((p // 16) % 2 == 1)
    msk[:, 4] = -1.0 * (p < 64)
    msk[:, 5] = -1.0 * (p >= 64)
    c["msk"] = msk
    bd = (p[:, None] // 16 == p[None, :] // 16).astype(np.float32)
    c["bdmask"] = bd
    return c


_CONSTS = None


def build(upto="all", debug=(), scopes=False):
    nc = bass.Bass("TRN2", target_bir_lowering=False)
    P = Prog()
    P.scopes = scopes
    dbg_out = {}

    def din(name, shape, dt=F32):
        return nc.dram_tensor(name, list(shape), dt, kind="ExternalInput").ap()

    x = din("x", [2, L, DM])
    g_mix = din("g_mix", [DM]); w_in = din("w_in", [DM, DM])
    lam_re = din("lam_re", [2, 32, 64]); lam_im = din("lam_im", [2, 32, 64]); log_dt = din("log_dt", [2, 32])
    b_re = din("b_re", [2, 32, 64, 16]); b_im = din("b_im", [2, 32, 64, 16])
    c_re = din("c_re", [2, 32, 16, 64]); c_im = din("c_im", [2, 32, 16, 64])
    ssm_d = din("ssm_d", [512]); w_glu = din("w_glu", [512, 512]); b_glu = din("b_glu", [512])
    w_fourier = din("w_fourier", [4, 128, 128]); w_out = din("w_out", [DM, DM]); g_ffn = din("g_ffn", [DM])
    w_up = din("w_up", [DM, 2 * DFF]); conv_w = din("conv_w", [3, 2 * DFF]); conv_b = din("conv_b", [2 * DFF])
    w_down = din("w_down", [DFF, DM]); g_final = din("g_final", [DM])
    identb_d = din("identb", [128, 128], BF16); identf_d = din("identf", [128, 128])
    cdft_d = din("cdft", [128, 128]); sdft_d = din("sdft", [128, 128])
    dftT_d = din("dftT", [16, 128, 2 * 16 * 128], BF16)
    msk_d = din("msk", [128, 8]); bdmask_d = din("bdmask", [128, 128])
    out = nc.dram_tensor("out", [2, L, DM], F32, kind="ExternalOutput").ap()
    lreim_s = nc.dram_tensor("lreim_s", [4, 2, 128, 16 * 2 * 128], BF16).ap()
    pcp_s = nc.dram_tensor("pcp_s", [4, 2, 128, 4 * 2 * 16 * 32], BF16).ap()
    lag_s = nc.dram_tensor("lag_s", [4, 128, 31 * 128], BF16).ap()

    dbg_shapes = {"u": ([128, 4 * L], BF16), "vT": ([128, 4 * L], BF16), "ycat": ([128, 8 * L], BF16),
                  "h": ([128, NTT * DM], F32), "XT": ([128, 8 * L], BF16), "win": ([128, 8 * DM], BF16), "X": ([128, 2 * 2 * 16 * NCH], BF16)}
    for nm in debug:
        shp, dtp = dbg_shapes[nm]
        dbg_out[nm] = nc.dram_tensor("dbg_" + nm, shp, dtp, kind="ExternalOutput").ap()

    es = ExitStack()
    with es:
        def sb(name, shape, dt):
            return es.enter_context(nc.sbuf_tensor(name, list(shape), dt))

        KB = 1024
        ARENA_B = 144 * KB
        arena = sb("arena", [128, ARENA_B // 2], BF16)

        def av(off_b, size_b, dt=BF16):
            v = arena[:, off_b // 2:(off_b + size_b) // 2]
            return v if dt == BF16 else v.bitcast(dt)

        R0, R1, R2 = 0, 64 * KB, 96 * KB
        h_v = av(R0, 64 * KB, F32).rearrange("p (t d) -> p t d", t=NTT)
        u_v = av(R0, 16 * KB).rearrange("p (f t) -> p f t", f=4)
        vT_v = av(R0 + 16 * KB, 16 * KB).rearrange("p (f t) -> p f t", f=4)
        AB_v = av(R0 + 32 * KB, 32 * KB).rearrange("p (l a c) -> p l a c", l=16, a=2)
        ycat_v = av(R1, 32 * KB).rearrange("p (f t) -> p f t", f=8)
        X_v = av(R2, 16 * KB)
        X5 = X_v.rearrange("p (r d q c) -> p r d q c", r=2, d=2, q=16)
        s5s_v = av(R2 + 16 * KB, 16 * KB).rearrange("p (b k r c) -> p b k r c", b=2, k=16, r=2)
        WOFF = R2 + 32 * KB
        win_v = av(WOFF, 16 * KB).rearrange("p (k c) -> p k c", k=8)
        dft_v = av(WOFF, 16 * KB).rearrange("p (b a l c) -> p b a l c", b=2, a=2, l=16)
        wout_v = av(R2, 16 * KB).rearrange("p (k c) -> p k c", k=8)
        lagb_v = av(R0 + 32 * KB, 16 * KB).rearrange("p (b l c) -> p b l c", b=2, l=32)
        pcpb_v = av(R0 + 48 * KB, 16 * KB).rearrange("p (b q r k c) -> p b q r k c", b=2, q=4, r=2, k=16)
        FB = R1
        maxnp = max(FFN_PARTS)
        wup_b = maxnp * 2 * 128 * 8 * 2
        wup_v = [av(FB + i * wup_b, wup_b).rearrange("p (k g c) -> p k g c", k=8, g=2) for i in range(2)]
        o1 = FB + 2 * wup_b
        wdn_b = maxnp * DM * 2
        wdn_v = [av(o1 + i * wdn_b, wdn_b).rearrange("p (t d) -> p t d", t=maxnp) for i in range(2)]
        o2 = o1 + 2 * wdn_b
        act_b = maxnp * L * 2
        act_v = [av(o2 + i * act_b, act_b).rearrange("p (t n) -> p t n", t=maxnp) for i in range(2)]
        o3 = o2 + 2 * act_b
        assert o3 <= ARENA_B, (o3, ARENA_B)
        FFN_NAMES = ["wup", "wdn", "act", "acc", "gact", "xst", "xsb"]
        SU0 = 0
        su_off = [SU0]

        def su(size_b, dt=F32):
            o = su_off[0]
            su_off[0] += size_b
            assert su_off[0] <= WOFF, su_off[0]
            return av(o, size_b, dt)

        XT = sb("XT", [128, 8, L], BF16)
        xst = sb("xst", [128, 2, DM], F32)
        xsb = sb("xsb", [128, 3, DM], BF16)
        acc_v = xst[:].rearrange("p a (b n) -> p (a b) n", b=2)
        gact4w = xsb[:].rearrange("p a n -> p (a n)")[:, 0:2048].rearrange("p (b n) -> p b n", b=4)
        gact4 = [gact4w[:, i, :] for i in range(4)]
        identb = sb("identb_sb", [128, 128], BF16)
        identf = sb("identf_sb", [128, 128], F32)
        msk = sb("msk_sb", [128, 8], F32)
        bdmask = sb("bdmask_sb", [128, 128], F32)
        g8 = sb("g8", [128, 2, 8], F32)
        gfin = sb("gfin", [128, DM], F32)
        ss = sb("ss", [128, 3, NTT], F32)
        cwsw = sb("cwsw", [128, 4, 256], BF16)
        bglu4 = sb("bglu4", [128, 4], F32)
        cw3 = sb("cw3", [128, 3, 44], F32)
        cb1 = sb("cb1", [128, 44], F32)
        d4 = sb("d4", [128, 4], F32)
        a16 = sb("a16", [128, 2, 2, 16], F32)
        a16n = sb("a16n", [128, 2, 2, 16], F32)
        scur = sb("scur", [128, 2, 2, 2, 16], F32)
        stmp = sb("stmp", [128, 2, 2, 2, 16], F32)
        yfs = sb("yfs", [128, 2, 512], BF16)
        junk = yfs[:].rearrange("p a n -> p (a n)")
        bar = sb("bar", [128, 64], F32)
        wglu_v = sb("wglu_sb", [128, 4, 512], BF16)
        _xt2 = XT[:].rearrange("p k t -> p (k t)")
        wf = _xt2[:, 0:1024].bitcast(F32).rearrange("p (h c) -> p h c", h=4)
        cs = _xt2[:, 1024:1536].bitcast(F32).rearrange("p (a c) -> p a c", a=2)
        ps = es.enter_context(nc.psum_tensor("ps", [128, 8, 512], F32))

        csem = {e: es.enter_context(nc.semaphore("c_" + e)) for e in ("pe", "act", "dve", "pool")}
        rings = {"sp": [es.enter_context(nc.semaphore(f"dsp{i}")) for i in range(24)],
                 "pool": [es.enter_context(nc.semaphore(f"dpl{i}")) for i in range(12)],
                 "act": [], "pe": [], "dve": []}
        nbar = [0]

        def barrier(names):
            i = nbar[0]
            nbar[0] += 1
            P.barrier("pool", lambda e, i=i: e.memset(bar[:, i:i + 1], 0.0), names)

        def dma(q, out_ap, in_ap, reads, writes, slow=False):
            if slow:
                P.op(q, lambda e: e.dma_start(out=out_ap, in_=in_ap, allow_slow_non_contiguous=True), reads, writes, dma=True)
            else:
                P.op(q, lambda e: e.dma_start(out=out_ap, in_=in_ap), reads, writes, dma=True)

        psb = [ps[:, b, :] for b in range(8)]
        psrot = [0]

        def nextbank():
            b = psrot[0] % 8
            psrot[0] += 1
            return b

        def PSR(b, n=1):
            return [("ps", b + i) for i in range(n)]

        sel_lo, sel_hi, ev_m, od_m, nsel_lo, nsel_hi = (msk[:, i:i + 1] for i in range(6))

        dma("sp", identb[:], identb_d, [], [("identb",)])
        dma("sp", identf[:], identf_d, [], [("identf",)])
        dma("sp", msk[:], msk_d, [], [("msk",)])
        dma("sp", bdmask[:], bdmask_d, [], [("bdmask",)])
        dma("sp", g8[:, 0, :], g_mix.rearrange("(k p) -> p k", p=128), [], [("g8", 0)], slow=True)
        dma("sp", g8[:, 1, :], g_ffn.rearrange("(k p) -> p k", p=128), [], [("g8", 1)], slow=True)
        dma("sp", bglu4[:], b_glu.rearrange("(k p) -> p k", p=128), [], [("bglu4",)], slow=True)
        dma("sp", d4[:], ssm_d.rearrange("(k p) -> p k", p=128), [], [("d4",)], slow=True)
        dma("sp", cb1[:], conv_b.rearrange("(k p) -> p k", p=128), [], [("cb1",)], slow=True)
        for k in range(3):
            dma("sp", cw3[:, k, :], conv_w[k].rearrange("(k p) -> p k", p=128), [], [("cw3", k)], slow=True)
        dma("sp", gfin[:], g_final.partition_broadcast(128), [], [("gfin",)])
        for kt in range(4):
            dma("pool", wglu_v[:, kt, :], w_glu[kt * 128:(kt + 1) * 128, :], [], [("wglu", kt)])
        dma("sp", wf[:], w_fourier.rearrange("h c j -> c h j"), [], [("SU", "wf")])
        dma("sp", cs[:, 0, :], cdft_d, [], [("SU", "cs", 0)])
        dma("sp", cs[:, 1, :], sdft_d, [], [("SU", "cs", 1)])

        for hh in range(4):
            b = nextbank()
            for a in range(2):
                P.op("pe", lambda e, hh=hh, a=a, b=b: e.matmul(psb[b][:, a * 128:(a + 1) * 128], cs[:, a, :], wf[:, hh, :],
                                                             start=True, stop=True),
                     [("SU", "cs", a), ("SU", "wf")], [("ps", b)])
            P.op("act", lambda e, hh=hh, b=b: e.activation(out=cwsw[:, hh, 0:128], in_=psb[b][:, 0:128], func=AF.Copy),
                 [("ps", b)], [("cwsw", hh, 0)])
            P.op("act", lambda e, hh=hh, b=b: e.activation(out=cwsw[:, hh, 128:256], in_=psb[b][:, 128:256], func=AF.Copy,
                                                           scale=-1.0),
                 [("ps", b)], [("cwsw", hh, 1)])

        def s5_setup():
            S = lambda *a: ("SU",) + a
            cnt = [0]

            def tile(shape, dt=F32):
                n = 1
                for s_ in shape:
                    n *= s_
                v = su(n * (4 if dt == F32 else 2), dt)
                cnt[0] += 1
                if len(shape) == 1:
                    return v, S(cnt[0])
                names = "abcde"[:len(shape)]
                pat = "p (" + " ".join(names) + ") -> p " + " ".join(names)
                return v.rearrange(pat, **{nm: s_ for nm, s_ in zip(names, shape)}), S(cnt[0])

            import os as _os2
            LEVEL = int(_os2.environ.get("K_LEVEL", "9"))

            def tt(eng, o, a, b_, op, rs, ws):
                P.op(eng, lambda e: e.tensor_tensor(out=o, in0=a, in1=b_, op=op), rs, ws)

            def ts(eng, o, a, s1, op0, rs, ws, s2=None, op1=None):
                if op1 is None:
                    P.op(eng, lambda e: e.tensor_scalar(out=o, in0=a, scalar1=s1, scalar2=None, op0=op0), rs, ws)
                else:
                    P.op(eng, lambda e: e.tensor_scalar(out=o, in0=a, scalar1=s1, scalar2=s2, op0=op0, op1=op1), rs, ws)

            def stt(o, a, sc, b_, op0, op1, rs, ws):
                P.op("dve", lambda e: e.scalar_tensor_tensor(out=o, in0=a, scalar=sc, in1=b_, op0=op0, op1=op1), rs, ws)

            def act(o, a, func, rs, ws, **kw):
                P.op("act", lambda e: e.activation(out=o, in_=a, func=func, **kw), rs, ws)

            M = ("msk",)
            LIN, rLIN = tile([2, 128])
            LDT, rLDT = tile([64])
            for t_, src in ((0, lam_re), (1, lam_im)):
                for hf in range(2):
                    dma("sp", LIN[0:64, t_, hf * 64:(hf + 1) * 64], src.rearrange("d g p -> (d g) p"), [], [rLIN + (t_, hf)])
            dma("sp", LDT, log_dt.rearrange("d g -> (d g)").partition_broadcast(128), [], [rLDT])
            BR2, rBR = tile([2, 32, 16]); BI2, rBI = tile([2, 32, 16])
            for (T_, r_, src) in ((BR2, rBR, b_re), (BI2, rBI, b_im)):
                for hf in range(2):
                    dma("sp", T_[hf * 64:(hf + 1) * 64], src.rearrange("d g p h -> p d g h"), [], [r_ + (hf,)])
            CIN, rCIN = tile([2, 8, 128])
            for t_, src in ((0, c_re), (1, c_im)):
                for hf in range(2):
                    dma("sp", CIN[:, t_, :, hf * 64:(hf + 1) * 64], src.rearrange("d g h p -> (d g h) p").rearrange("(r q) p -> q r p", q=128),
                        [], [rCIN + (t_, hf)])
            CR2, rCR = tile([1024]); CI2, rCI = tile([1024])
            for t_, (T_, r_) in enumerate(((CR2, rCR), (CI2, rCI))):
                for r4 in range(2):
                    b = nextbank()
                    for i in range(4):
                        r = r4 * 4 + i
                        P.op("pe", lambda e, b=b, i=i, t_=t_, r=r: e.transpose(psb[b][:, i * 128:(i + 1) * 128], CIN[:, t_, r, :], identf[:]),
                             [rCIN + (t_, 0), rCIN + (t_, 1), ("identf",)], [("ps", b)])
                    act(T_[:, r4 * 512:(r4 + 1) * 512], psb[b], AF.Copy, [("ps", b)], [r_ + (r4,)])
            LAM, rLAM = tile([2, 64])
            b = nextbank()
            for t_ in range(2):
                P.op("pe", lambda e, b=b, t_=t_: e.transpose(psb[b][:, t_ * 64:(t_ + 1) * 64], LIN[0:64, t_, :], identf[0:64, 0:64]),
                     [rLIN + (t_, 0), rLIN + (t_, 1), ("identf",)], [("ps", b)])
            act(LAM.rearrange("p a b -> p (a b)"), psb[b][:, 0:128], AF.Copy, [("ps", b)], [rLAM])
            LAMR, LAMI = LAM[:, 0, :], LAM[:, 1, :]
            if LEVEL <= 1:
                return
            sm, rsm = tile([20, 64])
            k_ = [0]

            def new():
                i = k_[0]
                k_[0] += 1
                return sm[:, i, :], rsm + (i,)
            dt_, rdt = new(); lr, rlr = new(); li, rli = new(); ea, rea = new(); tA, rtA = new(); nn, rnn = new()
            ri, rri = new(); s1, rs1 = new(); ab, rab = new(); c1, rc1 = new(); den, rden = new(); t0, rt0 = new()
            wr, rwr = new(); qr, rqr = new(); qi, rqi = new(); t1_, rt1_ = new(); t2_, rt2_ = new()
            act(dt_, LDT, AF.Exp, [rLDT], [rdt])
            tt("dve", lr, LAMR, dt_, ALU.mult, [rLAM, rdt], [rlr])
            tt("dve", li, LAMI, dt_, ALU.mult, [rLAM, rdt], [rli])
            act(ea, lr, AF.Exp, [rlr], [rea])
            ts("dve", tA, li, 1.0 / (2 * math.pi), ALU.mult, [rli], [rtA], s2=MAGIC, op1=ALU.add)
            ts("dve", nn, tA, -MAGIC, ALU.add, [rtA], [rnn])
            stt(ri, nn, -2 * math.pi, li, ALU.mult, ALU.add, [rnn, rli], [rri])
            act(s1, ri, AF.Sin, [rri], [rs1])
            act(ab, ri, AF.Abs, [rri], [rab])
            act(c1, ab, AF.Sin, [rab], [rc1], scale=-1.0, bias=math.pi / 2)
            ER, rER = tile([17, 64]); EI, rEI = tile([17, 64])
            P.op("dve", lambda e: e.memset(ER[:, 0, :], 1.0), [], [rER + (0,)])
            P.op("dve", lambda e: e.memset(EI[:, 0, :], 0.0), [], [rEI + (0,)])
            tt("dve", ER[:, 1, :], ea, c1, ALU.mult, [rea, rc1], [rER + (1,)])
            tt("dve", EI[:, 1, :], ea, s1, ALU.mult, [rea, rs1], [rEI + (1,)])
            ta, rta = tile([8, 64]); tb, rtb = tile([8, 64])
            n = 1
            while n < 16:
                lo, hi = 1, n + 1
                src_r = [rER + (i,) for i in range(lo, hi)] + [rEI + (i,) for i in range(lo, hi)]
                ern = ER[:, n:n + 1, :].broadcast_to([128, n, 64]) if n > 1 else ER[:, n:n + 1, :]
                ein = EI[:, n:n + 1, :].broadcast_to([128, n, 64]) if n > 1 else EI[:, n:n + 1, :]
                tt("dve", ta[:, 0:n, :], ER[:, lo:hi, :], ern, ALU.mult, src_r, [rta])
                tt("dve", tb[:, 0:n, :], EI[:, lo:hi, :], ein, ALU.mult, src_r, [rtb])
                tt("dve", ER[:, n + 1:2 * n + 1, :], ta[:, 0:n, :], tb[:, 0:n, :], ALU.subtract, [rta, rtb],
                   [rER + (i,) for i in range(n + 1, 2 * n + 1)])
                tt("dve", ta[:, 0:n, :], ER[:, lo:hi, :], ein, ALU.mult, src_r, [rta])
                tt("dve", tb[:, 0:n, :], EI[:, lo:hi, :], ern, ALU.mult, src_r, [rtb])
                tt("dve", EI[:, n + 1:2 * n + 1, :], ta[:, 0:n, :], tb[:, 0:n, :], ALU.add, [rta, rtb],
                   [rEI + (i,) for i in range(n + 1, 2 * n + 1)])
                n *= 2
            rERall = [rER + (i,) for i in range(17)]
            rEIall = [rEI + (i,) for i in range(17)]
            tt("dve", den, LAMR, LAMR, ALU.mult, [rLAM], [rden])
            tt("dve", t0, LAMI, LAMI, ALU.mult, [rLAM], [rt0])
            tt("dve", den, den, t0, ALU.add, [rden, rt0], [rden])
            P.op("dve", lambda e: e.reciprocal(out=den, in_=den), [rden], [rden])
            ts("dve", wr, ER[:, 1, :], -1.0, ALU.add, [rER + (1,)], [rwr])
            tt("dve", t1_, wr, LAMR, ALU.mult, [rwr, rLAM], [rt1_])
            tt("dve", t2_, EI[:, 1, :], LAMI, ALU.mult, [rEI + (1,), rLAM], [rt2_])
            tt("dve", t1_, t1_, t2_, ALU.add, [rt1_, rt2_], [rt1_])
            tt("dve", qr, t1_, den, ALU.mult, [rt1_, rden], [rqr])
            tt("dve", t1_, EI[:, 1, :], LAMR, ALU.mult, [rEI + (1,), rLAM], [rt1_])
            tt("dve", t2_, wr, LAMI, ALU.mult, [rwr, rLAM], [rt2_])
            tt("dve", t1_, t1_, t2_, ALU.subtract, [rt1_, rt2_], [rt1_])
            tt("dve", qi, t1_, den, ALU.mult, [rt1_, rden], [rqi])
            GR, rGR = tile([16, 64]); GI, rGI = tile([16, 64]); tg, rtg = tile([16, 64]); th, rth = tile([16, 64])
            qrb = qr.unsqueeze(1).broadcast_to([128, 16, 64]); qib = qi.unsqueeze(1).broadcast_to([128, 16, 64])
            tt("dve", tg, ER[:, 0:16, :], qrb, ALU.mult, rERall + [rqr], [rtg])
            tt("dve", th, EI[:, 0:16, :], qib, ALU.mult, rEIall + [rqi], [rth])
            tt("dve", GR, tg, th, ALU.subtract, [rtg, rth], [rGR])
            tt("dve", tg, ER[:, 0:16, :], qib, ALU.mult, rERall + [rqi], [rtg])
            tt("dve", th, EI[:, 0:16, :], qrb, ALU.mult, rEIall + [rqr], [rth])
            tt("dve", GI, tg, th, ALU.add, [rtg, rth], [rGI])
            AL, rAL = tile([16, 64]); BE, rBE = tile([16, 64])
            ts("dve", AL, GR, sel_lo, ALU.mult, [rGR, M], [rAL])
            stt(AL, GI, sel_hi, AL, ALU.mult, ALU.add, [rGI, rAL, M], [rAL])
            ts("dve", BE, GR, sel_hi, ALU.mult, [rGR, M], [rBE])
            stt(BE, GI, nsel_lo, BE, ALU.mult, ALU.add, [rGI, rBE, M], [rBE])
            for i_, (T_, r_) in enumerate(((ER, rER), (EI, rEI))):
                e16 = T_[:, 16, :].rearrange("p (d q two) -> p d q two", d=2, two=2)
                ts("dve", a16[:, i_], e16[:, :, :, 0], sel_lo, ALU.mult, [r_ + (16,), M], [("a16", i_)])
                stt(a16[:, i_], e16[:, :, :, 1], sel_hi, a16[:, i_], ALU.mult, ALU.add, [r_ + (16,), ("a16", i_), M], [("a16", i_)])
            ts("dve", a16n[:, 0], a16[:, 1], -1.0, ALU.mult, [("a16", 1)], [("a16n", 0)])
            ts("dve", a16n[:, 1], a16[:, 1], 1.0, ALU.mult, [("a16", 1)], [("a16n", 1)])
            PC0, rPC0 = tile([1024])
            rCRall = [rCR + (0,), rCR + (1,)]; rCIall = [rCI + (0,), rCI + (1,)]
            ts("dve", PC0, CR2, sel_lo, ALU.mult, rCRall + [M], [rPC0])
            stt(PC0, CI2, nsel_hi, PC0, ALU.mult, ALU.add, rCIall + [rPC0, M], [rPC0])
            if LEVEL <= 2:
                return
            L0, rL0 = tile([4, 128])
            AL4 = AL.rearrange("p k (d g) -> p k d g", d=2); BE4 = BE.rearrange("p k (d g) -> p k d g", d=2)
            ER4 = ER.rearrange("p k (d g) -> p k d g", d=2); EI4 = EI.rearrange("p k (d g) -> p k d g", d=2)
            CR4 = CR2.rearrange("p (d g h) -> p d g h", d=2, g=32); CI4 = CI2.rearrange("p (d g h) -> p d g h", d=2, g=32)
            t1, rt1 = tile([16, 8, 16]); t2, rt2 = tile([16, 8, 16]); t3, rt3 = tile([16, 8, 16])
            t4, rt4 = tile([16, 8, 16])
            t5, rt5 = CIN.rearrange("p a r c -> p (a r c)").rearrange("p (k g h) -> p k g h", k=16, g=8), S("t5alias")
            LGS, rLGS = tile([1, 15, 128], BF16)
            LRS, rLRS = tile([1, 16, 2, 128], BF16)
            PCS, rPCS = tile([1, 4, 2, 16, 32], BF16)
            L0B, rL0B = tile([4, 128], BF16)
            tl0, rtl0 = tile([128])
            it = 0
            for d in range(2):
                for ft in range(4):
                    sl = 0
                    it += 1
                    gs = slice(ft * 8, ft * 8 + 8)
                    bk = lambda v: v.unsqueeze(3).broadcast_to([128, 16, 8, 16])
                    bh = lambda v: v.unsqueeze(1).broadcast_to([128, 16, 8, 16])
                    tt("dve", t1, bk(AL4[:, :, d, gs]), bh(BR2[:, d, gs, :]), ALU.mult, [rAL, rBR + (0,), rBR + (1,)], [rt1])
                    tt("dve", t2, bk(BE4[:, :, d, gs]), bh(BI2[:, d, gs, :]), ALU.mult, [rBE, rBI + (0,), rBI + (1,)], [rt2])
                    tt("dve", t1, t1, t2, ALU.add, [rt1, rt2], [rt1])
                    t1f = t1.rearrange("p k g h -> p k (g h)")
                    for t4_ in range(4):
                        b = nextbank()
                        for i in range(4):
                            tau = t4_ * 4 + i
                            P.op("pe", lambda e, b=b, i=i, tau=tau, d=d, ft=ft, t1f=t1f: e.matmul(
                                psb[b][:, i * 128:(i + 1) * 128], t1f[:, tau, :], PC0[:, d * 512 + ft * 128:d * 512 + (ft + 1) * 128],
                                start=True, stop=True), [rt1, rPC0], [("ps", b)])
                        for i in range(4):
                            tau = t4_ * 4 + i
                            if tau == 0:
                                if d == 0:
                                    tt("dve", L0[:, ft, :], psb[b][:, 0:128], bdmask[:], ALU.mult, [("ps", b), ("bdmask",)], [rL0 + (ft,)])
                                else:
                                    tt("dve", tl0, psb[b][:, 0:128], bdmask[:], ALU.mult, [("ps", b), ("bdmask",)], [rtl0])
                                    tt("dve", L0[:, ft, :], L0[:, ft, :], tl0, ALU.add, [rL0 + (ft,), rtl0], [rL0 + (ft,)])
                                    stt(L0[:, ft, :], identf[:], d4[:, ft:ft + 1], L0[:, ft, :], ALU.mult, ALU.add,
                                        [("identf",), ("d4",), rL0 + (ft,)], [rL0 + (ft,)])
                                    ts("dve", L0B[:, ft, :], L0[:, ft, :], 1.0, ALU.mult, [rL0 + (ft,)], [rL0B + (ft,)])
                                    dma("sp", lag_s[ft][:, 0:128], L0B[:, ft, :], [rL0B + (ft,)], [("lag_s", ft, 0)])
                            else:
                                tt("dve", LGS[:, sl, tau - 1, :], psb[b][:, i * 128:(i + 1) * 128],
                                   bdmask[:], ALU.mult, [("ps", b), ("bdmask",)], [rLGS + (sl, tau)])
                    base = (1 + 15 * d) * 128
                    dma("sp", lag_s[ft][:, base:base + 15 * 128], LGS[:, sl].rearrange("p l c -> p (l c)"),
                        [rLGS + (sl, tau) for tau in range(1, 16)], [("lag_s", ft, 1 + d)])
                    if LEVEL <= 3:
                        continue
                    for t4_ in range(4):
                        b = nextbank()
                        for i in range(4):
                            k = t4_ * 4 + i
                            P.op("pe", lambda e, b=b, i=i, k=k, t1f=t1f: e.transpose(psb[b][:, i * 128:(i + 1) * 128], t1f[:, k, :], identf[:]),
                                 [rt1, ("identf",)], [("ps", b)])
                        src = psb[b].rearrange("p (k r c) -> p k r c", k=4, r=2)
                        dst = LRS[:, sl, t4_ * 4:t4_ * 4 + 4, :, :]
                        if True:
                            P.op("act", lambda e, src=src, dst=dst: e.activation(out=dst[:, :, :, 0:64], in_=src, func=AF.Copy, scale=ev_m),
                                 [("ps", b), M], [rLRS + (sl, t4_, 0)])
                        else:
                            ts("dve", dst[:, :, :, 0:64], src, ev_m, ALU.mult, [("ps", b), M], [rLRS + (sl, t4_, 0)])
                        ts("dve", dst[:, :, :, 64:128], src, od_m, ALU.mult, [("ps", b), M], [rLRS + (sl, t4_, 1)])
                    dma("sp", lreim_s[ft, d], LRS[:, sl].rearrange("p k r c -> p (k r c)"),
                        [rLRS + (sl, a_, b_) for a_ in range(4) for b_ in range(2)], [("lreim_s", ft, d)])
                    if LEVEL <= 4:
                        continue
                    crb = bh(CR4[:, d, gs, :]); cib = bh(CI4[:, d, gs, :])
                    erb = bk(ER4[:, 1:17, d, gs]); eib = bk(EI4[:, 1:17, d, gs])
                    tt("pool", t2, crb, erb, ALU.mult, rCRall + rERall, [rt2])
                    tt("pool", t3, cib, eib, ALU.mult, rCIall + rEIall, [rt3])
                    tt("pool", t2, t2, t3, ALU.subtract, [rt2, rt3], [rt2])
                    tt("dve", t4, crb, eib, ALU.mult, rCRall + rEIall, [rt4])
                    tt("dve", t5, cib, erb, ALU.mult, rCIall + rERall, [rt5])
                    tt("pool", t4, t4, t5, ALU.add, [rt4, rt5], [rt4])
                    for (eng, T_, r_, reim, sels) in (("pool", t2, rt2, 0, (sel_lo, sel_hi)), ("dve", t4, rt4, 1, (nsel_lo, nsel_hi))):
                        for gp in range(2):
                            src = T_.rearrange("p k (q two) h -> p q k two h", two=2)[:, :, :, gp, :]
                            dst = PCS[:, sl, :, reim, :, gp * 16:(gp + 1) * 16]
                            P.op(eng, lambda e, src=src, dst=dst, sc=sels[gp]: e.tensor_scalar(out=dst, in0=src, scalar1=sc, scalar2=0.0,
                                                                                               op0=ALU.mult, op1=ALU.add),
                                 [r_, M], [rPCS + (sl, reim, gp)])
                    dma("sp", pcp_s[ft, d], PCS[:, sl].rearrange("p q r k c -> p (q r k c)"),
                        [rPCS + (sl, a_, b_) for a_ in range(2) for b_ in range(2)], [("pcp_s", ft, d)])

        def norm_stage(src_ap, tt_, ni, src_res, scale_out=True):
            P.op("act", lambda e: e.activation(out=junk, in_=src_ap, func=AF.Square, accum_out=ss[:, ni, tt_:tt_ + 1]),
                 src_res, [("yfs", 0), ("yfs", 1), ("ss", ni, tt_)])
            P.op("act", lambda e: e.activation(out=ss[:, ni, tt_:tt_ + 1], in_=ss[:, ni, tt_:tt_ + 1], func=AF.Sqrt,
                                               scale=1.0 / DM, bias=EPS),
                 [("ss", ni, tt_)], [("ss", ni, tt_)])
            P.op("dve", lambda e: e.reciprocal(out=ss[:, ni, tt_:tt_ + 1], in_=ss[:, ni, tt_:tt_ + 1]),
                 [("ss", ni, tt_)], [("ss", ni, tt_)])
            if not scale_out:
                return
            sl = tt_ % 3
            P.op("dve", lambda e: e.tensor_scalar(out=xsb[:, sl, :], in0=src_ap, scalar1=ss[:, ni, tt_:tt_ + 1], scalar2=None,
                                                  op0=ALU.mult),
                 src_res + [("ss", ni, tt_)], [("xsb", sl)])

        def transpose_stage(tt_, gi):
            sl = tt_ % 3
            b = nextbank()
            pT = psb[b].bitcast(BF16)[:, 0:1024].rearrange("p (k t) -> p k t", k=8)
            for kt in range(8):
                P.op("pe", lambda e, kt=kt: e.transpose(pT[:, kt, :], xsb[:, sl, kt * 128:(kt + 1) * 128], identb[:]),
                     [("xsb", sl), ("identb",)], [("ps", b)])
            P.op("dve", lambda e: e.tensor_tensor(out=XT[:, :, tt_ * 128:(tt_ + 1) * 128], in0=pT,
                                                  in1=g8[:, gi, :].unsqueeze(2).broadcast_to([128, 8, 128]), op=ALU.mult),
                 [("ps", b), ("g8", gi)], [("XT", tt_)])

        uJC = [u_v[:, ft, :].rearrange("p (j c) -> p j c", j=TC) for ft in range(4)]

        def mix_front(s, mode="all"):
            if mode != "a" and s > 0:
                dma("pool", win_v, w_in.rearrange("(k p) c -> p k c", p=128), [], [("win", kt) for kt in range(8)])
            pending = []

            def proj_unit(fc, blk):
                b = nextbank()
                for kt in range(8):
                    P.op("pe", lambda e, fc=fc, kt=kt, b=b, blk=blk: e.matmul(
                        psb[b], win_v[:, kt, fc * 128:(fc + 1) * 128], XT[:, kt, blk * 512:(blk + 1) * 512],
                        start=(kt == 0), stop=(kt == 7)),
                        [("win", kt)] + [("XT", t4) for t4 in range(blk * 4, blk * 4 + 4)], [("ps", b)])
                if fc < 4:
                    dst = uJC[fc][:, :, blk * 32:(blk + 1) * 32]
                    src = psb[b].rearrange("p (c j) -> p j c", j=TC)
                else:
                    dst = vT_v[:, fc % 4, blk * 512:(blk + 1) * 512]
                    src = psb[b]
                rn = ("u" if fc < 4 else "vT", fc % 4, blk)
                P.op("act", lambda e, src=src, dst=dst: e.activation(out=dst, in_=src, func=AF.Copy), [("ps", b)], [rn])

            SK = 2
            if mode == "b":
                for blk in range(4):
                    for fc in range(8):
                        proj_unit(fc, blk)
                return
            for it in range(NTT + SK):
                if it < NTT:
                    sl = it % 2
                    dma("sp", xst[:, sl, :], x[s, it * 128:(it + 1) * 128, :], [], [("xst", sl)])
                    norm_stage(xst[:, sl, :], it, 0, [("xst", sl)])
                t2 = it - SK
                if t2 >= 0:
                    transpose_stage(t2, 0)
                    if t2 % 4 == 3 and mode == "all":
                        pending.extend((fc, t2 // 4) for fc in range(8))
                for _ in range(2):
                    if pending:
                        proj_unit(*pending.pop(0))
            while pending:
                proj_unit(*pending.pop(0))

        def fourier(s):
            for lt in range(16):
                b0 = (nextbank() // 2) * 2
                psrot[0] = b0 + 2
                for hh in range(4):
                    bb = b0 + hh // 2
                    P.op("pe", lambda e, hh=hh, bb=bb, lt=lt: e.matmul(
                        ps[:, bb, (hh % 2) * 256:(hh % 2) * 256 + 256], vT_v[:, hh, lt * 128:(lt + 1) * 128], cwsw[:, hh, :],
                        start=True, stop=True),
                        [("vT", hh, lt // 4), ("cwsw", hh, 0), ("cwsw", hh, 1)], [("ps", bb)])
                for bb in (b0, b0 + 1):
                    src = ps[:, bb, :].rearrange("p (h a c) -> p h a c", h=2, a=2)
                    dstv = AB_v[:, lt, :, (bb - b0) * 256:(bb - b0) * 256 + 256].rearrange("p a (h c) -> p h a c", h=2)
                    P.op("act", lambda e, src=src, dstv=dstv: e.activation(out=dstv, in_=src, func=AF.Copy),
                         [("ps", bb)], [("AB", lt, bb - b0)])
            for kt in range(16):
                sl = kt % 2
                dma("sp", dft_v[:, sl].rearrange("p a l c -> p (a l c)"), dftT_d[kt], [], [("dft", sl)])
                b = nextbank()
                n = 0
                for a in range(2):
                    for lt in range(16):
                        P.op("pe", lambda e, a=a, lt=lt, b=b, sl=sl, n=n: e.matmul(
                            psb[b], dft_v[:, sl, a, lt, :], AB_v[:, lt, a, :], start=(n == 0), stop=(n == 31)),
                            [("dft", sl), ("AB", lt, 0), ("AB", lt, 1)], [("ps", b)])
                        n += 1
                P.op("act", lambda e, b=b, sl=sl: e.activation(out=yfs[:, sl, :], in_=psb[b], func=AF.Copy), [("ps", b)], [("yfs", sl)])
                b2 = nextbank()
                pT = psb[b2].bitcast(BF16)[:, 0:512].rearrange("p (f t) -> p f t", f=4)
                for f in range(4):
                    P.op("pe", lambda e, f=f, sl=sl, pT=pT: e.transpose(pT[:, f, :], yfs[:, sl, f * 128:(f + 1) * 128], identb[:]),
                         [("yfs", sl), ("identb",)], [("ps", b2)])
                P.op("act", lambda e, pT=pT, kt=kt: e.activation(out=ycat_v[:, 4:8, kt * 128:(kt + 1) * 128], in_=pT, func=AF.Copy),
                     [("ps", b2)], [("ycat", 4 + f, kt // 4) for f in range(4)])


        def s5_states(s):
            it = 0
            for ft in range(4):
                for d in range(2):
                    buf = it % 2
                    it += 1
                    dma("sp", s5s_v[:, buf].rearrange("p k r c -> p (k r c)"), lreim_s[ft, d], [("lreim_s", ft, d)], [("s5s", buf)])
                    b4 = (nextbank() // 4) * 4
                    psrot[0] = b4 + 4
                    for reim in range(2):
                        for i in range(TC):
                            k = (TC - 1 - i) if d == 0 else i
                            for q in range(4):
                                b = b4 + q
                                P.op("pe", lambda e, b=b, reim=reim, i=i, k=k, q=q, ft=ft, buf=buf: e.matmul(
                                    psb[b][:, reim * 128:(reim + 1) * 128], s5s_v[32 * q:32 * q + 32, buf, k, reim, :],
                                    uJC[ft][32 * q:32 * q + 32, i, :], start=(i == 0), stop=(i == TC - 1), tile_position=(32 * q, 0)),
                                    [("s5s", buf)] + [("u", ft, bl) for bl in range(4)], [("ps", b)])
                    for q in range(4):
                        Q = ft * 4 + q
                        b = b4 + q
                        P.op("act", lambda e, b=b, d=d, Q=Q: e.activation(out=X5[:, :, d, Q, :], in_=psb[b][:, 0:256].rearrange("p (r c) -> p r c", r=2),
                                                                          func=AF.Copy), [("ps", b)], [("XV",), ("XS",)])

        def s5_scan(s):
            P.op("dve", lambda e: e.memset(scur[:, 0], 0.0), [], [("scur", 0)])
            arb = a16[:, 0].unsqueeze(1).broadcast_to([128, 2, 2, 16])
            for st in range(NCH - 1):
                cu, nx = st % 2, (st + 1) % 2
                col = bass.AP(X5.tensor, X5.offset + st, [list(X5.ap[0]), [4096, 2], [2048 + (NCH - 1) - 2 * st, 2], [128, 16]])
                P.op("dve", lambda e, cu=cu: e.tensor_tensor(out=stmp[:, 0], in0=scur[:, cu], in1=arb, op=ALU.mult),
                     [("scur", cu), ("a16", 0)], [("stmp", 0)])
                P.op("dve", lambda e, cu=cu: e.tensor_tensor(out=stmp[:, 1, 0], in0=scur[:, cu, 1], in1=a16n[:, 0], op=ALU.mult),
                     [("scur", cu), ("a16n", 0)], [("stmp", 1, 0)])
                P.op("dve", lambda e, cu=cu: e.tensor_tensor(out=stmp[:, 1, 1], in0=scur[:, cu, 0], in1=a16n[:, 1], op=ALU.mult),
                     [("scur", cu), ("a16n", 1)], [("stmp", 1, 1)])
                P.op("dve", lambda e: e.tensor_tensor(out=stmp[:, 0], in0=stmp[:, 0], in1=stmp[:, 1], op=ALU.add),
                     [("stmp", 0), ("stmp", 1, 0), ("stmp", 1, 1)], [("stmp", 0)])
                P.op("dve", lambda e, nx=nx, col=col: e.tensor_tensor(out=scur[:, nx], in0=stmp[:, 0], in1=col, op=ALU.add),
                     [("stmp", 0), ("XV",)], [("scur", nx)])
                P.op("pool", lambda e, nx=nx, col=col: e.tensor_copy(out=col, in_=scur[:, nx]), [("scur", nx)], [("XS",)])

        def s5_out(s):
            it = 0
            for ft in range(4):
                lb = ft % 2
                dma("sp", lagb_v[:, lb, 0:31, :].rearrange("p l c -> p (l c)"), lag_s[ft],
                    [("lag_s", ft, 0), ("lag_s", ft, 1), ("lag_s", ft, 2)], [("lagb", lb)])
                B0 = (ft % 2) * 4
                Yb = [ps[:, B0 + jb, :].rearrange("p (j c) -> p j c", j=4) for jb in range(4)]
                for jb in range(4):
                    j0 = 4 * jb
                    mm = [(0, j0, j0 + 4, 0)]
                    for tau in range(1, TC):
                        lo, hi = max(tau, j0), j0 + 4
                        if lo < hi:
                            mm.append((tau, lo, hi, -tau))
                    for tau in range(1, TC):
                        lo, hi = j0, min(TC - tau, j0 + 4)
                        if lo < hi:
                            mm.append((TC - 1 + tau, lo, hi, tau))
                    for n, (lg, lo, hi, sh) in enumerate(mm):
                        P.op("pe", lambda e, lg=lg, lo=lo, hi=hi, sh=sh, jb=jb, j0=j0, n=n, lb=lb, ft=ft, Yb=Yb: e.matmul(
                            Yb[jb][:, lo - j0:hi - j0, :], lagb_v[:, lb, lg, :], uJC[ft][:, lo + sh:hi + sh, :], start=(n == 0), stop=False,
                            skip_group_check=True),
                            [("lagb", lb)] + [("u", ft, bl) for bl in range(4)], [("ps", B0 + jb)])
                for d in range(2):
                    pb_ = it % 2
                    it += 1
                    dma("sp", pcpb_v[:, pb_].rearrange("p q r k c -> p (q r k c)"), pcp_s[ft, d], [("pcp_s", ft, d)], [("pcpb", pb_)])
                    for j in range(TC):
                        kk = j if d == 0 else TC - 1 - j
                        for reim in range(2):
                            for q in range(4):
                                Q = ft * 4 + q
                                if d == 0:
                                    rhs = X5[:, reim, 0, Q, 0:NCH - 1]
                                    oc0 = 1
                                else:
                                    rhs = X5[:, reim, 1, Q, 1:NCH]
                                    oc0 = 0
                                last = (d == 1 and j == TC - 1 and reim == 1)
                                oap = ps[32 * q:32 * q + 32, B0 + j // 4, (j % 4) * 128 + oc0:(j % 4) * 128 + oc0 + NCH - 1]
                                P.op("pe", lambda e, oap=oap, pb_=pb_, q=q, reim=reim, kk=kk, rhs=rhs, last=last: e.matmul(
                                    oap, pcpb_v[:, pb_, q, reim, kk, :], rhs, start=False, stop=last, tile_position=(0, 32 * q),
                                    skip_group_check=True),
                                    [("pcpb", pb_), ("XS",)], [("ps", B0 + j // 4)])
                zJC = ycat_v[:, ft, :].rearrange("p (c j) -> p j c", j=TC)
                for jb in range(4):
                    P.op("act", lambda e, jb=jb, Yb=Yb, zJC=zJC: e.activation(out=zJC[:, 4 * jb:4 * jb + 4, :], in_=Yb[jb], func=AF.Gelu_apprx_tanh),
                         [("ps", B0 + jb)], [("ycat", ft, bl) for bl in range(4)])

        def glu(s):
            for blk in range(4):
                cols = slice(blk * 512, (blk + 1) * 512)
                bs = []
                for m in range(4):
                    b = nextbank()
                    bs.append(b)
                    for kt in range(4):
                        P.op("pe", lambda e, b=b, m=m, kt=kt, cols=cols: e.matmul(psb[b], wglu_v[:, kt, m * 128:(m + 1) * 128], ycat_v[:, kt, cols],
                                                                                  start=(kt == 0), stop=(kt == 3)),
                             [("wglu", kt), ("ycat", kt, blk)], [("ps", b)])
                for m in range(4):
                    sl = m % 2
                    P.op("act", lambda e, b=bs[m], m=m, sl=sl: e.activation(out=xst[:, sl, 0:512], in_=psb[b], func=AF.Sigmoid, bias=bglu4[:, m:m + 1]),
                         [("ps", bs[m]), ("bglu4",)], [("xst", sl)])
                    P.op("pool", lambda e, m=m, sl=sl, cols=cols: e.tensor_tensor(out=ycat_v[:, m, cols], in0=ycat_v[:, m, cols], in1=xst[:, sl, 0:512], op=ALU.mult),
                         [("xst", sl), ("ycat", m, blk)], [("ycat", m, blk)])

        def wout_load(s):
            dma("pool", wout_v, w_out.rearrange("(k p) c -> p k c", p=128), [], [("wout", kt) for kt in range(8)])

        def wout_phase(s):
            SK = 2
            for it in range(NTT + SK):
                if it < NTT:
                    tt_ = it
                    sl = tt_ % 2
                    dma("sp", xst[:, sl, :], x[s, tt_ * 128:(tt_ + 1) * 128, :], [], [("xst", sl)])
                    for hf in range(2):
                        b = nextbank()
                        for kt in range(8):
                            P.op("pe", lambda e, kt=kt, b=b, hf=hf, tt_=tt_: e.matmul(
                                psb[b], ycat_v[:, kt, tt_ * 128:(tt_ + 1) * 128], wout_v[:, kt, hf * 512:(hf + 1) * 512],
                                start=(kt == 0), stop=(kt == 7)),
                                [("ycat", kt, tt_ // 4), ("wout", kt)], [("ps", b)])
                        P.op("dve", lambda e, b=b, hf=hf, tt_=tt_, sl=sl: e.tensor_tensor(
                            out=h_v[:, tt_, hf * 512:(hf + 1) * 512], in0=psb[b], in1=xst[:, sl, hf * 512:(hf + 1) * 512], op=ALU.add),
                            [("ps", b), ("xst", sl)], [("h", tt_, hf)])
                    norm_stage(h_v[:, tt_, :], tt_, 1, [("h", tt_, 0), ("h", tt_, 1)])
                if it - SK >= 0:
                    transpose_stage(it - SK, 1)

        def ffn(s):
            starts = [sum(FFN_PARTS[:i]) for i in range(len(FFN_PARTS))]

            def load(pi):
                npt, t0, wb = FFN_PARTS[pi], starts[pi], pi % 2
                for gv in range(2):
                    c0 = gv * DFF + t0 * 128
                    for kt in range(8):
                        dma("pool", wup_v[wb][:, kt, gv, 0:npt * 128], w_up[kt * 128:(kt + 1) * 128, c0:c0 + npt * 128], [], [("wup", wb, kt, gv)])
                dma("pool", wdn_v[wb][:, 0:npt, :], w_down[t0 * 128:(t0 + npt) * 128, :].rearrange("(t p) d -> p t d", p=128), [], [("wdn", wb)])

            def up_tile(pi, tl):
                t0, wb = starts[pi], pi % 2
                ftile = t0 + tl
                for gv in range(2):
                    B0 = 4 * gv
                    ch = gv * NFT + ftile
                    for blk in range(4):
                        for kt in range(8):
                            P.op("pe", lambda e, B0=B0, blk=blk, kt=kt, wb=wb, gv=gv, tl=tl: e.matmul(
                                ps[:, B0 + blk, :], wup_v[wb][:, kt, gv, tl * 128:(tl + 1) * 128], XT[:, kt, blk * 512:(blk + 1) * 512],
                                start=(kt == 0), stop=(kt == 7)),
                                [("wup", wb, kt, gv)] + [("XT", t4) for t4 in range(blk * 4, blk * 4 + 4)], [("ps", B0 + blk)])
                    row = ps[:, B0:B0 + 4, :].rearrange("p b n -> p (b n)")
                    for blk in range(4):
                        a_ = acc_v[:, gv * 2 + blk % 2, :]
                        ra = ("acc", gv * 2 + blk % 2)
                        c_lo, c_hi = blk * 512, (blk + 1) * 512
                        rd = [("ps", B0 + blk)]
                        P.op("act", lambda e, a_=a_, row=row, c_lo=c_lo, c_hi=c_hi, ch=ch: e.activation(
                            out=a_, in_=row[:, c_lo:c_hi], func=AF.Identity, scale=cw3[:, 1, ch:ch + 1], bias=cb1[:, ch:ch + 1]),
                            rd + [("cw3", 1), ("cb1",)], [ra])
                        lo = 1 if blk == 0 else 0
                        P.op("dve", lambda e, a_=a_, row=row, c_lo=c_lo, c_hi=c_hi, ch=ch, lo=lo: e.scalar_tensor_tensor(
                            out=a_[:, lo:512], in0=row[:, c_lo + lo - 1:c_hi - 1], scalar=cw3[:, 0, ch:ch + 1], in1=a_[:, lo:512],
                            op0=ALU.mult, op1=ALU.add),
                            rd + ([("ps", B0 + blk - 1)] if blk > 0 else []) + [ra, ("cw3", 0)], [ra])
                        hi = 511 if blk == 3 else 512
                        P.op("dve", lambda e, a_=a_, row=row, c_lo=c_lo, c_hi=c_hi, ch=ch, hi=hi: e.scalar_tensor_tensor(
                            out=a_[:, 0:hi], in0=row[:, c_lo + 1:c_lo + hi + 1], scalar=cw3[:, 2, ch:ch + 1], in1=a_[:, 0:hi],
                            op0=ALU.mult, op1=ALU.add),
                            rd + ([("ps", B0 + blk + 1)] if blk < 3 else []) + [ra, ("cw3", 2)], [ra])
                        if gv == 0:
                            P.op("act", lambda e, a_=a_, blk=blk: e.activation(out=gact4[blk], in_=a_, func=AF.Gelu_apprx_tanh),
                                 [ra], [("gact", blk)])
                        else:
                            P.op("pool", lambda e, a_=a_, blk=blk, wb=wb, tl=tl, c_lo=c_lo, c_hi=c_hi: e.tensor_tensor(
                                out=act_v[wb][:, tl, c_lo:c_hi], in0=a_, in1=gact4[blk], op=ALU.mult),
                                [ra, ("gact", blk)], [("act", wb, tl, blk)])

            def down(pi, last=False):
                npt, wb = FFN_PARTS[pi], pi % 2
                ostg = wup_v[(pi + 1) % 2].rearrange("p k g c -> p (k g c)")[:, 0:6144].bitcast(F32).rearrange("p (b n) -> p b n", b=3)
                ores = [("wup", (pi + 1) % 2, kt, gv) for kt in range(8) for gv in range(2)]
                for tt_ in range(NTT):
                    for hf in range(2):
                        b = nextbank()
                        for tl in range(npt):
                            P.op("pe", lambda e, b=b, tl=tl, tt_=tt_, hf=hf, wb=wb, npt=npt: e.matmul(
                                psb[b], act_v[wb][:, tl, tt_ * 128:(tt_ + 1) * 128], wdn_v[wb][:, tl, hf * 512:(hf + 1) * 512],
                                start=(tl == 0), stop=(tl == npt - 1)),
                                [("act", wb, tl, tt_ // 4), ("wdn", wb)], [("ps", b)])
                        P.op("dve", lambda e, b=b, tt_=tt_, hf=hf: e.tensor_tensor(
                            out=h_v[:, tt_, hf * 512:(hf + 1) * 512], in0=psb[b], in1=h_v[:, tt_, hf * 512:(hf + 1) * 512], op=ALU.add),
                            [("ps", b), ("h", tt_, hf)], [("h", tt_, hf)])
                    if last:
                        norm_stage(h_v[:, tt_, :], tt_, 2, [("h", tt_, 0), ("h", tt_, 1)], scale_out=False)
                        ob = tt_ % 3
                        P.op("dve", lambda e, tt_=tt_, ob=ob: e.scalar_tensor_tensor(out=ostg[:, ob, :], in0=h_v[:, tt_, :], scalar=ss[:, 2, tt_:tt_ + 1],
                                                                                    in1=gfin[:], op0=ALU.mult, op1=ALU.mult),
                             [("h", tt_, 0), ("h", tt_, 1), ("ss", 2, tt_), ("gfin",)], ores + [("ostg", ob)])
                        dma("sp", out[s, tt_ * 128:(tt_ + 1) * 128, :], ostg[:, ob, :], [("ostg", ob)], [("out", s, tt_)])

            NP = len(FFN_PARTS)
            load(0)
            for pi in range(NP):
                for tl in range(FFN_PARTS[pi]):
                    up_tile(pi, tl)
                    if tl == 0:
                        if pi > 0:
                            down(pi - 1)
                        if pi + 1 < NP:
                            load(pi + 1)
            down(NP - 1, last=True)

        def final(s):
            for tt_ in range(NTT):
                norm_stage(h_v[:, tt_, :], tt_, 2, [("h", tt_, 0), ("h", tt_, 1)], scale_out=False)
                sl = tt_ % 2
                P.op("dve", lambda e, tt_=tt_, sl=sl: e.scalar_tensor_tensor(out=xst[:, sl, :], in0=h_v[:, tt_, :], scalar=ss[:, 2, tt_:tt_ + 1],
                                                                            in1=gfin[:], op0=ALU.mult, op1=ALU.mult),
                     [("h", tt_, 0), ("h", tt_, 1), ("ss", 2, tt_), ("gfin",)], [("xst", sl)])
                dma("sp", out[s, tt_ * 128:(tt_ + 1) * 128, :], xst[:, sl, :], [("xst", sl)], [("out", s, tt_)])

        def dump(nm, ap2d, reads):
            if nm in dbg_out:
                dma("sp", dbg_out[nm], ap2d, reads, [("dbg", nm)])

        MIXN = ["u", "vT", "AB", "ycat", "XV", "XS", "s5s", "win", "dft", "wout", "lagb", "pcpb"]
        import os as _os
        dma("pool", win_v, w_in.rearrange("(k p) c -> p k c", p=128), [], [("win", kt) for kt in range(8)])
        P.phase = 'setup'
        s5_setup()
        barrier(["SU", "XT"] + [n for n in MIXN if n != "win"])
        nseq = 2 if upto == "all" else 1
        for s in range(nseq):
            P.phase = 'mix_front%d' % s
            mix_front(s, "all")
            if upto == "front":
                dump("u", arena[:, 0:4 * L], [("u", f, b_) for f in range(4) for b_ in range(4)])
                dump("vT", arena[:, 4 * L:8 * L], [("vT", f, b_) for f in range(4) for b_ in range(4)])
                dump("win", arena[:, WOFF // 2:WOFF // 2 + 8 * DM], [("win", k_) for k_ in range(8)])
                dump("XT", XT[:].rearrange("p k t -> p (k t)"), [("XT", t_) for t_ in range(NTT)])
                break
            P.phase = 's5_states%d' % s
            s5_states(s)
            P.phase = 's5_scan%d' % s
            s5_scan(s)
            P.phase = 'fourier%d' % s
            fourier(s)
            barrier(["AB", "dft", "lagb", "pcpb"])
            P.phase = 's5_out%d' % s
            s5_out(s)
            barrier(["XV", "XS", "wout"])
            wout_load(s)
            P.phase = 'glu%d' % s
            glu(s)
            if upto == "mix":
                dump("ycat", arena[:, R1 // 2:R1 // 2 + 8 * L], [("ycat", f, b_) for f in range(8) for b_ in range(4)])
                dump("X", X_v, [("XS",)])
                break
            barrier(["u", "vT", "AB", "lagb", "pcpb", "h", "win", "dft"])
            P.phase = 'wout_phase%d' % s
            wout_phase(s)
            if upto == "wout":
                dump("h", arena[:, 0:32 * KB].bitcast(F32), [("h", t_, hf) for t_ in range(NTT) for hf in range(2)])
                break
            barrier(MIXN + FFN_NAMES)
            P.phase = 'ffn%d' % s
            ffn(s)
            barrier(MIXN + FFN_NAMES + ["h", "ostg"])
        P.op("sp", None, [("out", s_, t_) for s_ in range(nseq) for t_ in range(NTT)] + [("dbg", nm) for nm in dbg_out], [("done",)])

        allsems = list(csem.values()) + rings["sp"] + rings["pool"]
        with nc.Block() as block0:
            def _clr(e):
                for sm in allsems:
                    e.sem_clear(sm)
            block0.sync(_clr)
        with nc.Block() as block:
            P.emit(nc, block, csem, rings)
    return nc, P


def _in_maps(inputs):
    global _CONSTS
    if _CONSTS is None:
        _CONSTS = _host_consts()
    f = lambda a: np.ascontiguousarray(np.asarray(a, dtype=np.float32))
    shared = {
        "g_mix": f(inputs["g_mix"][0]), "w_in": f(inputs["w_in"][0]),
        "lam_re": f(inputs["ssm_lam_re"][0]), "lam_im": f(inputs["ssm_lam_im"][0]), "log_dt": f(inputs["ssm_log_dt"][0]),
        "b_re": f(inputs["ssm_b_re"][0]), "b_im": f(inputs["ssm_b_im"][0]), "c_re": f(inputs["ssm_c_re"][0]), "c_im": f(inputs["ssm_c_im"][0]),
        "ssm_d": f(inputs["ssm_d"][0]), "w_glu": f(inputs["w_glu"][0]), "b_glu": f(inputs["b_glu"][0]),
        "w_fourier": f(inputs["w_fourier"][0]), "w_out": f(inputs["w_out"][0]), "g_ffn": f(inputs["g_ffn"][0]),
        "w_up": f(inputs["w_up"][0]), "conv_w": f(inputs["conv_w"][0]), "conv_b": f(inputs["conv_b"][0]),
        "w_down": f(inputs["w_down"][0]), "g_final": f(inputs["g_final"]),
    }
    c = _CONSTS
    shared.update({"identb": c["identb"], "identf": c["identf"], "cdft": c["cdft"], "sdft": c["sdft"],
                   "dftT": c["dftT"].reshape(16, 128, 2 * 16 * 128), "msk": c["msk"], "bdmask": c["bdmask"]})
    xs = f(inputs["x"])
    maps = []
    for c_ in range(NCORES):
        m = dict(shared)
        m["x"] = xs[2 * c_:2 * c_ + 2]
        maps.append(m)
    return maps


def kernel(**inputs):
    nc, _ = build()
    res = run_bass_kernel_spmd(nc, _in_maps(inputs), core_ids=list(range(NCORES)))
    return np.concatenate([np.asarray(r["out"], dtype=np.float32) for r in res.results], axis=0)
```

```python
import math
from contextlib import ExitStack

import numpy as np
import ml_dtypes
import concourse.bass as bass
import concourse.mybir as mybir
from concourse.bass_utils import run_bass_kernel_spmd

F32 = mybir.dt.float32
BF16 = mybir.dt.bfloat16
AF = mybir.ActivationFunctionType
ALU = mybir.AluOpType

NCORES = 8
L = 2048
DM = 1024
NTT = 16
DFF = 2816
NFT = 22
TC = 16
NCH = L // TC
EPS = 1e-6
FFN_PARTS = [4, 4, 4, 4, 3, 3]
MAGIC = 12582912.0


class _Op:
    __slots__ = ("eng", "fn", "deps", "dma", "idx", "sem", "val", "prev", "has_dep", "phase")


class Prog:
    ENGS = ("pe", "act", "dve", "pool", "sp")

    def __init__(self):
        self.ops = []
        self.lastw = {}
        self.readers = {}
        self.bufbar = {}
        self.phase = "init"
        self.scopes = False

    def op(self, eng, fn, reads=(), writes=(), dma=False):
        psr = [r for r in reads if r[0] == "ps"]
        if psr:
            reads = [r for r in reads if r[0] != "ps"]
            writes = list(writes) + [r for r in psr if r not in writes]
        i = len(self.ops)
        deps = set()
        for r in reads:
            w = self.lastw.get(r)
            if w is None:
                w = self.bufbar.get(r[0])
            if w is not None:
                deps.add(w)
        for w_ in writes:
            lw = self.lastw.get(w_)
            if lw is None:
                lw = self.bufbar.get(w_[0])
            if lw is not None:
                deps.add(lw)
            deps.update(self.readers.get(w_, ()))
        o = _Op()
        o.eng, o.fn, o.deps, o.dma, o.idx = eng, fn, deps, dma, i
        o.phase = self.phase
        self.ops.append(o)
        for r in reads:
            self.readers.setdefault(r, []).append(i)
        for w_ in writes:
            self.lastw[w_] = i
            self.readers[w_] = []
        return i

    def barrier(self, eng, fn, names):
        names = set(names)
        deps = set()
        for r in list(self.lastw.keys()):
            if r[0] in names:
                deps.add(self.lastw.pop(r))
        for r in list(self.readers.keys()):
            if r[0] in names:
                deps.update(self.readers.pop(r))
        for n in names:
            if n in self.bufbar:
                deps.add(self.bufbar[n])
        i = len(self.ops)
        o = _Op()
        o.eng, o.fn, o.deps, o.dma, o.idx = eng, fn, deps, False, i
        o.phase = self.phase
        self.ops.append(o)
        for n in names:
            self.bufbar[n] = i
        return i

    def emit(self, nc, block, csem, rings):
        ops = self.ops
        for o in ops:
            o.has_dep = False
        for o in ops:
            for d in o.deps:
                ops[d].has_dep = True
        cnt = {e: 0 for e in self.ENGS}
        dcnt = {e: 0 for e in self.ENGS}
        for o in ops:
            o.prev = None
            if o.dma:
                ring = rings[o.eng]
                j = dcnt[o.eng]
                dcnt[o.eng] += 1
                R = len(ring)
                o.sem = ring[j % R]
                o.val = 16 * (j // R + 1)
                if j >= R:
                    o.prev = (ring[j % R], 16 * (j // R))
            elif o.has_dep:
                cnt[o.eng] += 1
                o.sem = csem[o.eng]
                o.val = cnt[o.eng]
        self.counts = (cnt, dcnt)

        def run(engname, e):
            known = {}
            cur = [None]
            cur_id = [None]
            for o in ops:
                if o.eng != engname:
                    continue
                if self.scopes and o.phase != cur[0]:
                    if cur[0] is not None:
                        nc.leave_named_scope(cur[0], cur_id[0], False)
                    cur_id[0] = nc.enter_named_scope(o.phase, False)[0]
                    cur[0] = o.phase
                waits = {}
                for d in o.deps:
                    p = ops[d]
                    if engname == "pe" and p.eng == "pe" and not p.dma:
                        continue
                    k = id(p.sem)
                    if k not in waits or waits[k][1] < p.val:
                        waits[k] = (p.sem, p.val)
                if o.prev is not None:
                    k = id(o.prev[0])
                    if k not in waits or waits[k][1] < o.prev[1]:
                        waits[k] = o.prev
                wl = []
                for k, (sm, v) in waits.items():
                    if known.get(k, 0) >= v:
                        continue
                    known[k] = v
                    wl.append((sm, v))
                if o.fn is None:
                    for sm, v in wl:
                        e.wait_ge(sm, v)
                    continue
                for sm, v in wl[:-1]:
                    e.wait_ge(sm, v)
                ins = o.fn(e)
                if wl:
                    ins._wait_ge(wl[-1][0], wl[-1][1])
                if o.dma:
                    ins.then_inc(o.sem, 16)
                elif o.has_dep:
                    ins.then_inc(o.sem, 1)
            if self.scopes and cur[0] is not None:
                nc.leave_named_scope(cur[0], cur_id[0], False)

        block.tensor(lambda e: run("pe", e))
        block.scalar(lambda e: run("act", e))
        block.vector(lambda e: run("dve", e))
        block.gpsimd(lambda e: run("pool", e))
        block.sync(lambda e: run("sp", e))


def _host_consts():
    c = {}
    c["identb"] = np.eye(128, dtype=np.float32).astype(ml_dtypes.bfloat16)
    c["identf"] = np.eye(128, dtype=np.float32)
    n = np.arange(128)
    ang = 2 * np.pi * np.outer(n, n) / 128.0
    c["cdft"] = (np.cos(ang) / math.sqrt(128.0)).astype(np.float32)
    c["sdft"] = (np.sin(ang) / math.sqrt(128.0)).astype(np.float32)
    l = np.arange(L)
    lk = np.outer(l, l) % L
    angL = 2 * np.pi * lk / L
    CL = (np.cos(angL) / math.sqrt(L)).astype(np.float32)
    SL = (np.sin(angL) / math.sqrt(L)).astype(np.float32)

    def lay(M):
        return M.reshape(16, 128, 16, 128).transpose(2, 1, 0, 3)
    c["dftT"] = np.ascontiguousarray(np.stack([lay(CL), lay(SL)], axis=2)).astype(ml_dtypes.bfloat16)
    p = np.arange(128)
    msk = np.zeros((128, 8), np.float32)
    msk[:, 0] = (p < 64)
    msk[:, 1] = (p >= 64)
    msk[:, 2] = ((p // 16) % 2 == 0)
    msk[:, 3] = ((p // 16) % 2 == 1)
    msk[:, 4] = -1.0 * (p < 64)
    msk[:, 5] = -1.0 * (p >= 64)
    c["msk"] = msk
    bd = (p[:, None] // 16 == p[None, :] // 16).astype(np.float32)
    c["bdmask"] = bd
    return c


_CONSTS = None


def build(upto="all", debug=(), scopes=False):
    nc = bass.Bass("TRN2", target_bir_lowering=False)
    P = Prog()
    P.scopes = scopes
    dbg_out = {}

    def din(name, shape, dt=F32):
        return nc.dram_tensor(name, list(shape), dt, kind="ExternalInput").ap()

    x = din("x", [2, L, DM])
    g_mix = din("g_mix", [DM]); w_in = din("w_in", [DM, DM])
    lam_re = din("lam_re", [2, 32, 64]); lam_im = din("lam_im", [2, 32, 64]); log_dt = din("log_dt", [2, 32])
    b_re = din("b_re", [2, 32, 64, 16]); b_im = din("b_im", [2, 32, 64, 16])
    c_re = din("c_re", [2, 32, 16, 64]); c_im = din("c_im", [2, 32, 16, 64])
    ssm_d = din("ssm_d", [512]); w_glu = din("w_glu", [512, 512]); b_glu = din("b_glu", [512])
    w_fourier = din("w_fourier", [4, 128, 128]); w_out = din("w_out", [DM, DM]); g_ffn = din("g_ffn", [DM])
    w_up = din("w_up", [DM, 2 * DFF]); conv_w = din("conv_w", [3, 2 * DFF]); conv_b = din("conv_b", [2 * DFF])
    w_down = din("w_down", [DFF, DM]); g_final = din("g_final", [DM])
    identb_d = din("identb", [128, 128], BF16); identf_d = din("identf", [128, 128])
    cdft_d = din("cdft", [128, 128]); sdft_d = din("sdft", [128, 128])
    dftT_d = din("dftT", [16, 128, 2 * 16 * 128], BF16)
    msk_d = din("msk", [128, 8]); bdmask_d = din("bdmask", [128, 128])
    out = nc.dram_tensor("out", [2, L, DM], F32, kind="ExternalOutput").ap()
    lreim_s = nc.dram_tensor("lreim_s", [4, 2, 128, 16 * 2 * 128], BF16).ap()
    pcp_s = nc.dram_tensor("pcp_s", [4, 2, 128, 4 * 2 * 16 * 32], BF16).ap()
    lag_s = nc.dram_tensor("lag_s", [4, 128, 31 * 128], BF16).ap()

    dbg_shapes = {"u": ([128, 4 * L], BF16), "vT": ([128, 4 * L], BF16), "ycat": ([128, 8 * L], BF16),
                  "h": ([128, NTT * DM], F32), "XT": ([128, 8 * L], BF16), "win": ([128, 8 * DM], BF16), "X": ([128, 2 * 2 * 16 * NCH], BF16)}
    for nm in debug:
        shp, dtp = dbg_shapes[nm]
        dbg_out[nm] = nc.dram_tensor("dbg_" + nm, shp, dtp, kind="ExternalOutput").ap()

    es = ExitStack()
    with es:
        def sb(name, shape, dt):
            return es.enter_context(nc.sbuf_tensor(name, list(shape), dt))

        KB = 1024
        ARENA_B = 144 * KB
        arena = sb("arena", [128, ARENA_B // 2], BF16)

        def av(off_b, size_b, dt=BF16):
            v = arena[:, off_b // 2:(off_b + size_b) // 2]
            return v if dt == BF16 else v.bitcast(dt)

        R0, R1, R2 = 0, 64 * KB, 96 * KB
        h_v = av(R0, 64 * KB, F32).rearrange("p (t d) -> p t d", t=NTT)
        u_v = av(R0, 16 * KB).rearrange("p (f t) -> p f t", f=4)
        vT_v = av(R0 + 16 * KB, 16 * KB).rearrange("p (f t) -> p f t", f=4)
        AB_v = av(R0 + 32 * KB, 32 * KB).rearrange("p (l a c) -> p l a c", l=16, a=2)
        ycat_v = av(R1, 32 * KB).rearrange("p (f t) -> p f t", f=8)
        X_v = av(R2, 16 * KB)
        X5 = X_v.rearrange("p (r d q c) -> p r d q c", r=2, d=2, q=16)
        s5s_v = av(R2 + 16 * KB, 16 * KB).rearrange("p (b k r c) -> p b k r c", b=2, k=16, r=2)
        WOFF = R2 + 32 * KB
        win_v = av(WOFF, 16 * KB).rearrange("p (k c) -> p k c", k=8)
        dft_v = av(WOFF, 16 * KB).rearrange("p (b a l c) -> p b a l c", b=2, a=2, l=16)
        wout_v = av(R2, 16 * KB).rearrange("p (k c) -> p k c", k=8)
        lagb_v = av(R0 + 32 * KB, 16 * KB).rearrange("p (b l c) -> p b l c", b=2, l=32)
        pcpb_v = av(R0 + 48 * KB, 16 * KB).rearrange("p (b q r k c) -> p b q r k c", b=2, q=4, r=2, k=16)
        FB = R1
        maxnp = max(FFN_PARTS)
        wup_b = maxnp * 2 * 128 * 8 * 2
        wup_v = [av(FB + i * wup_b, wup_b).rearrange("p (k g c) -> p k g c", k=8, g=2) for i in range(2)]
        o1 = FB + 2 * wup_b
        wdn_b = maxnp * DM * 2
        wdn_v = [av(o1 + i * wdn_b, wdn_b).rearrange("p (t d) -> p t d", t=maxnp) for i in range(2)]
        o2 = o1 + 2 * wdn_b
        act_b = maxnp * L * 2
        act_v = [av(o2 + i * act_b, act_b).rearrange("p (t n) -> p t n", t=maxnp) for i in range(2)]
        o3 = o2 + 2 * act_b
        assert o3 <= ARENA_B, (o3, ARENA_B)
        FFN_NAMES = ["wup", "wdn", "act", "acc", "gact", "xst", "xsb"]
        SU0 = 0
        su_off = [SU0]

        def su(size_b, dt=F32):
            o = su_off[0]
            su_off[0] += size_b
            assert su_off[0] <= WOFF, su_off[0]
            return av(o, size_b, dt)

        XT = sb("XT", [128, 8, L], BF16)
        xst = sb("xst", [128, 2, DM], F32)
        xsb = sb("xsb", [128, 3, DM], BF16)
        acc_v = xst[:].rearrange("p a (b n) -> p (a b) n", b=2)
        gact4w = xsb[:].rearrange("p a n -> p (a n)")[:, 0:2048].rearrange("p (b n) -> p b n", b=4)
        gact4 = [gact4w[:, i, :] for i in range(4)]
        identb = sb("identb_sb", [128, 128], BF16)
        identf = sb("identf_sb", [128, 128], F32)
        msk = sb("msk_sb", [128, 8], F32)
        bdmask = sb("bdmask_sb", [128, 128], F32)
        g8 = sb("g8", [128, 2, 8], F32)
        gfin = sb("gfin", [128, DM], F32)
        ss = sb("ss", [128, 3, NTT], F32)
        cwsw = sb("cwsw", [128, 4, 256], BF16)
        bglu4 = sb("bglu4", [128, 4], F32)
        cw3 = sb("cw3", [128, 3, 44], F32)
        cb1 = sb("cb1", [128, 44], F32)
        d4 = sb("d4", [128, 4], F32)
        a16 = sb("a16", [128, 2, 2, 16], F32)
        a16n = sb("a16n", [128, 2, 2, 16], F32)
        scur = sb("scur", [128, 2, 2, 2, 16], F32)
        stmp = sb("stmp", [128, 2, 2, 2, 16], F32)
        yfs = sb("yfs", [128, 2, 512], BF16)
        junk = yfs[:].rearrange("p a n -> p (a n)")
        bar = sb("bar", [128, 64], F32)
        wglu_v = sb("wglu_sb", [128, 4, 512], BF16)
        _xt2 = XT[:].rearrange("p k t -> p (k t)")
        wf = _xt2[:, 0:1024].bitcast(F32).rearrange("p (h c) -> p h c", h=4)
        cs = _xt2[:, 1024:1536].bitcast(F32).rearrange("p (a c) -> p a c", a=2)
        ps = es.enter_context(nc.psum_tensor("ps", [128, 8, 512], F32))

        csem = {e: es.enter_context(nc.semaphore("c_" + e)) for e in ("pe", "act", "dve", "pool")}
        rings = {"sp": [es.enter_context(nc.semaphore(f"dsp{i}")) for i in range(24)],
                 "pool": [es.enter_context(nc.semaphore(f"dpl{i}")) for i in range(12)],
                 "act": [], "pe": [], "dve": []}
        nbar = [0]

        def barrier(names):
            i = nbar[0]
            nbar[0] += 1
            P.barrier("pool", lambda e, i=i: e.memset(bar[:, i:i + 1], 0.0), names)

        def dma(q, out_ap, in_ap, reads, writes, slow=False):
            if slow:
                P.op(q, lambda e: e.dma_start(out=out_ap, in_=in_ap, allow_slow_non_contiguous=True), reads, writes, dma=True)
            else:
                P.op(q, lambda e: e.dma_start(out=out_ap, in_=in_ap), reads, writes, dma=True)

        psb = [ps[:, b, :] for b in range(8)]
        psrot = [0]

        def nextbank():
            b = psrot[0] % 8
            psrot[0] += 1
            return b

        def PSR(b, n=1):
            return [("ps", b + i) for i in range(n)]

        sel_lo, sel_hi, ev_m, od_m, nsel_lo, nsel_hi = (msk[:, i:i + 1] for i in range(6))

        dma("sp", identb[:], identb_d, [], [("identb",)])
        dma("sp", identf[:], identf_d, [], [("identf",)])
        dma("sp", msk[:], msk_d, [], [("msk",)])
        dma("sp", bdmask[:], bdmask_d, [], [("bdmask",)])
        dma("sp", g8[:, 0, :], g_mix.rearrange("(k p) -> p k", p=128), [], [("g8", 0)], slow=True)
        dma("sp", g8[:, 1, :], g_ffn.rearrange("(k p) -> p k", p=128), [], [("g8", 1)], slow=True)
        dma("sp", bglu4[:], b_glu.rearrange("(k p) -> p k", p=128), [], [("bglu4",)], slow=True)
        dma("sp", d4[:], ssm_d.rearrange("(k p) -> p k", p=128), [], [("d4",)], slow=True)
        dma("sp", cb1[:], conv_b.rearrange("(k p) -> p k", p=128), [], [("cb1",)], slow=True)
        for k in range(3):
            dma("sp", cw3[:, k, :], conv_w[k].rearrange("(k p) -> p k", p=128), [], [("cw3", k)], slow=True)
        dma("sp", gfin[:], g_final.partition_broadcast(128), [], [("gfin",)])
        for kt in range(4):
            dma("pool", wglu_v[:, kt, :], w_glu[kt * 128:(kt + 1) * 128, :], [], [("wglu", kt)])
        dma("sp", wf[:], w_fourier.rearrange("h c j -> c h j"), [], [("SU", "wf")])
        dma("sp", cs[:, 0, :], cdft_d, [], [("SU", "cs", 0)])
        dma("sp", cs[:, 1, :], sdft_d, [], [("SU", "cs", 1)])

        for hh in range(4):
            b = nextbank()
            for a in range(2):
                P.op("pe", lambda e, hh=hh, a=a, b=b: e.matmul(psb[b][:, a * 128:(a + 1) * 128], cs[:, a, :], wf[:, hh, :],
                                                             start=True, stop=True),
                     [("SU", "cs", a), ("SU", "wf")], [("ps", b)])
            P.op("act", lambda e, hh=hh, b=b: e.activation(out=cwsw[:, hh, 0:128], in_=psb[b][:, 0:128], func=AF.Copy),
                 [("ps", b)], [("cwsw", hh, 0)])
            P.op("act", lambda e, hh=hh, b=b: e.activation(out=cwsw[:, hh, 128:256], in_=psb[b][:, 128:256], func=AF.Copy,
                                                           scale=-1.0),
                 [("ps", b)], [("cwsw", hh, 1)])

        def s5_setup():
            S = lambda *a: ("SU",) + a
            cnt = [0]

            def tile(shape, dt=F32):
                n = 1
                for s_ in shape:
                    n *= s_
                v = su(n * (4 if dt == F32 else 2), dt)
                cnt[0] += 1
                if len(shape) == 1:
                    return v, S(cnt[0])
                names = "abcde"[:len(shape)]
                pat = "p (" + " ".join(names) + ") -> p " + " ".join(names)
                return v.rearrange(pat, **{nm: s_ for nm, s_ in zip(names, shape)}), S(cnt[0])

            import os as _os2
            LEVEL = int(_os2.environ.get("K_LEVEL", "9"))

            def tt(eng, o, a, b_, op, rs, ws):
                P.op(eng, lambda e: e.tensor_tensor(out=o, in0=a, in1=b_, op=op), rs, ws)

            def ts(eng, o, a, s1, op0, rs, ws, s2=None, op1=None):
                if op1 is None:
                    P.op(eng, lambda e: e.tensor_scalar(out=o, in0=a, scalar1=s1, scalar2=None, op0=op0), rs, ws)
                else:
                    P.op(eng, lambda e: e.tensor_scalar(out=o, in0=a, scalar1=s1, scalar2=s2, op0=op0, op1=op1), rs, ws)

            def stt(o, a, sc, b_, op0, op1, rs, ws):
                P.op("dve", lambda e: e.scalar_tensor_tensor(out=o, in0=a, scalar=sc, in1=b_, op0=op0, op1=op1), rs, ws)

            def act(o, a, func, rs, ws, **kw):
                P.op("act", lambda e: e.activation(out=o, in_=a, func=func, **kw), rs, ws)

            M = ("msk",)
            LIN, rLIN = tile([2, 128])
            LDT, rLDT = tile([64])
            for t_, src in ((0, lam_re), (1, lam_im)):
                for hf in range(2):
                    dma("sp", LIN[0:64, t_, hf * 64:(hf + 1) * 64], src.rearrange("d g p -> (d g) p"), [], [rLIN + (t_, hf)])
            dma("sp", LDT, log_dt.rearrange("d g -> (d g)").partition_broadcast(128), [], [rLDT])
            BR2, rBR = tile([2, 32, 16]); BI2, rBI = tile([2, 32, 16])
            for (T_, r_, src) in ((BR2, rBR, b_re), (BI2, rBI, b_im)):
                for hf in range(2):
                    dma("sp", T_[hf * 64:(hf + 1) * 64], src.rearrange("d g p h -> p d g h"), [], [r_ + (hf,)])
            CIN, rCIN = tile([2, 8, 128])
            for t_, src in ((0, c_re), (1, c_im)):
                for hf in range(2):
                    dma("sp", CIN[:, t_, :, hf * 64:(hf + 1) * 64], src.rearrange("d g h p -> (d g h) p").rearrange("(r q) p -> q r p", q=128),
                        [], [rCIN + (t_, hf)])
            CR2, rCR = tile([1024]); CI2, rCI = tile([1024])
            for t_, (T_, r_) in enumerate(((CR2, rCR), (CI2, rCI))):
                for r4 in range(2):
                    b = nextbank()
                    for i in range(4):
                        r = r4 * 4 + i
                        P.op("pe", lambda e, b=b, i=i, t_=t_, r=r: e.transpose(psb[b][:, i * 128:(i + 1) * 128], CIN[:, t_, r, :], identf[:]),
                             [rCIN + (t_, 0), rCIN + (t_, 1), ("identf",)], [("ps", b)])
                    act(T_[:, r4 * 512:(r4 + 1) * 512], psb[b], AF.Copy, [("ps", b)], [r_ + (r4,)])
            LAM, rLAM = tile([2, 64])
            b = nextbank()
            for t_ in range(2):
                P.op("pe", lambda e, b=b, t_=t_: e.transpose(psb[b][:, t_ * 64:(t_ + 1) * 64], LIN[0:64, t_, :], identf[0:64, 0:64]),
                     [rLIN + (t_, 0), rLIN + (t_, 1), ("identf",)], [("ps", b)])
            act(LAM.rearrange("p a b -> p (a b)"), psb[b][:, 0:128], AF.Copy, [("ps", b)], [rLAM])
            LAMR, LAMI = LAM[:, 0, :], LAM[:, 1, :]
            if LEVEL <= 1:
                return
            sm, rsm = tile([20, 64])
            k_ = [0]

            def new():
                i = k_[0]
                k_[0] += 1
                return sm[:, i, :], rsm + (i,)
            dt_, rdt = new(); lr, rlr = new(); li, rli = new(); ea, rea = new(); tA, rtA = new(); nn, rnn = new()
            ri, rri = new(); s1, rs1 = new(); ab, rab = new(); c1, rc1 = new(); den, rden = new(); t0, rt0 = new()
            wr, rwr = new(); qr, rqr = new(); qi, rqi = new(); t1_, rt1_ = new(); t2_, rt2_ = new()
            act(dt_, LDT, AF.Exp, [rLDT], [rdt])
            tt("dve", lr, LAMR, dt_, ALU.mult, [rLAM, rdt], [rlr])
            tt("dve", li, LAMI, dt_, ALU.mult, [rLAM, rdt], [rli])
            act(ea, lr, AF.Exp, [rlr], [rea])
            ts("dve", tA, li, 1.0 / (2 * math.pi), ALU.mult, [rli], [rtA], s2=MAGIC, op1=ALU.add)
            ts("dve", nn, tA, -MAGIC, ALU.add, [rtA], [rnn])
            stt(ri, nn, -2 * math.pi, li, ALU.mult, ALU.add, [rnn, rli], [rri])
            act(s1, ri, AF.Sin, [rri], [rs1])
            act(ab, ri, AF.Abs, [rri], [rab])
            act(c1, ab, AF.Sin, [rab], [rc1], scale=-1.0, bias=math.pi / 2)
            ER, rER = tile([17, 64]); EI, rEI = tile([17, 64])
            P.op("dve", lambda e: e.memset(ER[:, 0, :], 1.0), [], [rER + (0,)])
            P.op("dve", lambda e: e.memset(EI[:, 0, :], 0.0), [], [rEI + (0,)])
            tt("dve", ER[:, 1, :], ea, c1, ALU.mult, [rea, rc1], [rER + (1,)])
            tt("dve", EI[:, 1, :], ea, s1, ALU.mult, [rea, rs1], [rEI + (1,)])
            ta, rta = tile([8, 64]); tb, rtb = tile([8, 64])
            n = 1
            while n < 16:
                lo, hi = 1, n + 1
                src_r = [rER + (i,) for i in range(lo, hi)] + [rEI + (i,) for i in range(lo, hi)]
                ern = ER[:, n:n + 1, :].broadcast_to([128, n, 64]) if n > 1 else ER[:, n:n + 1, :]
                ein = EI[:, n:n + 1, :].broadcast_to([128, n, 64]) if n > 1 else EI[:, n:n + 1, :]
                tt("dve", ta[:, 0:n, :], ER[:, lo:hi, :], ern, ALU.mult, src_r, [rta])
                tt("dve", tb[:, 0:n, :], EI[:, lo:hi, :], ein, ALU.mult, src_r, [rtb])
                tt("dve", ER[:, n + 1:2 * n + 1, :], ta[:, 0:n, :], tb[:, 0:n, :], ALU.subtract, [rta, rtb],
                   [rER + (i,) for i in range(n + 1, 2 * n + 1)])
                tt("dve", ta[:, 0:n, :], ER[:, lo:hi, :], ein, ALU.mult, src_r, [rta])
                tt("dve", tb[:, 0:n, :], EI[:, lo:hi, :], ern, ALU.mult, src_r, [rtb])
                tt("dve", EI[:, n + 1:2 * n + 1, :], ta[:, 0:n, :], tb[:, 0:n, :], ALU.add, [rta, rtb],
                   [rEI + (i,) for i in range(n + 1, 2 * n + 1)])
                n *= 2
            rERall = [rER + (i,) for i in range(17)]
            rEIall = [rEI + (i,) for i in range(17)]
            tt("dve", den, LAMR, LAMR, ALU.mult, [rLAM], [rden])
            tt("dve", t0, LAMI, LAMI, ALU.mult, [rLAM], [rt0])
            tt("dve", den, den, t0, ALU.add, [rden, rt0], [rden])
            P.op("dve", lambda e: e.reciprocal(out=den, in_=den), [rden], [rden])
            ts("dve", wr, ER[:, 1, :], -1.0, ALU.add, [rER + (1,)], [rwr])
            tt("dve", t1_, wr, LAMR, ALU.mult, [rwr, rLAM], [rt1_])
            tt("dve", t2_, EI[:, 1, :], LAMI, ALU.mult, [rEI + (1,), rLAM], [rt2_])
            tt("dve", t1_, t1_, t2_, ALU.add, [rt1_, rt2_], [rt1_])
            tt("dve", qr, t1_, den, ALU.mult, [rt1_, rden], [rqr])
            tt("dve", t1_, EI[:, 1, :], LAMR, ALU.mult, [rEI + (1,), rLAM], [rt1_])
            tt("dve", t2_, wr, LAMI, ALU.mult, [rwr, rLAM], [rt2_])
            tt("dve", t1_, t1_, t2_, ALU.subtract, [rt1_, rt2_], [rt1_])
            tt("dve", qi, t1_, den, ALU.mult, [rt1_, rden], [rqi])
            GR, rGR = tile([16, 64]); GI, rGI = tile([16, 64]); tg, rtg = tile([16, 64]); th, rth = tile([16, 64])
            qrb = qr.unsqueeze(1).broadcast_to([128, 16, 64]); qib = qi.unsqueeze(1).broadcast_to([128, 16, 64])
            tt("dve", tg, ER[:, 0:16, :], qrb, ALU.mult, rERall + [rqr], [rtg])
            tt("dve", th, EI[:, 0:16, :], qib, ALU.mult, rEIall + [rqi], [rth])
            tt("dve", GR, tg, th, ALU.subtract, [rtg, rth], [rGR])
            tt("dve", tg, ER[:, 0:16, :], qib, ALU.mult, rERall + [rqi], [rtg])
            tt("dve", th, EI[:, 0:16, :], qrb, ALU.mult, rEIall + [rqr], [rth])
            tt("dve", GI, tg, th, ALU.add, [rtg, rth], [rGI])
            AL, rAL = tile([16, 64]); BE, rBE = tile([16, 64])
            ts("dve", AL, GR, sel_lo, ALU.mult, [rGR, M], [rAL])
            stt(AL, GI, sel_hi, AL, ALU.mult, ALU.add, [rGI, rAL, M], [rAL])
            ts("dve", BE, GR, sel_hi, ALU.mult, [rGR, M], [rBE])
            stt(BE, GI, nsel_lo, BE, ALU.mult, ALU.add, [rGI, rBE, M], [rBE])
            for i_, (T_, r_) in enumerate(((ER, rER), (EI, rEI))):
                e16 = T_[:, 16, :].rearrange("p (d q two) -> p d q two", d=2, two=2)
                ts("dve", a16[:, i_], e16[:, :, :, 0], sel_lo, ALU.mult, [r_ + (16,), M], [("a16", i_)])
                stt(a16[:, i_], e16[:, :, :, 1], sel_hi, a16[:, i_], ALU.mult, ALU.add, [r_ + (16,), ("a16", i_), M], [("a16", i_)])
            ts("dve", a16n[:, 0], a16[:, 1], -1.0, ALU.mult, [("a16", 1)], [("a16n", 0)])
            ts("dve", a16n[:, 1], a16[:, 1], 1.0, ALU.mult, [("a16", 1)], [("a16n", 1)])
            PC0, rPC0 = tile([1024])
            rCRall = [rCR + (0,), rCR + (1,)]; rCIall = [rCI + (0,), rCI + (1,)]
            ts("dve", PC0, CR2, sel_lo, ALU.mult, rCRall + [M], [rPC0])
            stt(PC0, CI2, nsel_hi, PC0, ALU.mult, ALU.add, rCIall + [rPC0, M], [rPC0])
            if LEVEL <= 2:
                return
            L0, rL0 = tile([4, 128])
            AL4 = AL.rearrange("p k (d g) -> p k d g", d=2); BE4 = BE.rearrange("p k (d g) -> p k d g", d=2)
            ER4 = ER.rearrange("p k (d g) -> p k d g", d=2); EI4 = EI.rearrange("p k (d g) -> p k d g", d=2)
            CR4 = CR2.rearrange("p (d g h) -> p d g h", d=2, g=32); CI4 = CI2.rearrange("p (d g h) -> p d g h", d=2, g=32)
            t1, rt1 = tile([16, 8, 16]); t2, rt2 = tile([16, 8, 16]); t3, rt3 = tile([16, 8, 16])
            t4, rt4 = tile([16, 8, 16])
            t5, rt5 = CIN.rearrange("p a r c -> p (a r c)").rearrange("p (k g h) -> p k g h", k=16, g=8), S("t5alias")
            LGS, rLGS = tile([1, 15, 128], BF16)
            LRS, rLRS = tile([1, 16, 2, 128], BF16)
            PCS, rPCS = tile([1, 4, 2, 16, 32], BF16)
            L0B, rL0B = tile([4, 128], BF16)
            tl0, rtl0 = tile([128])
            it = 0
            for d in range(2):
                for ft in range(4):
                    sl = 0
                    it += 1
                    gs = slice(ft * 8, ft * 8 + 8)
                    bk = lambda v: v.unsqueeze(3).broadcast_to([128, 16, 8, 16])
                    bh = lambda v: v.unsqueeze(1).broadcast_to([128, 16, 8, 16])
                    tt("dve", t1, bk(AL4[:, :, d, gs]), bh(BR2[:, d, gs, :]), ALU.mult, [rAL, rBR + (0,), rBR + (1,)], [rt1])
                    tt("dve", t2, bk(BE4[:, :, d, gs]), bh(BI2[:, d, gs, :]), ALU.mult, [rBE, rBI + (0,), rBI + (1,)], [rt2])
                    tt("dve", t1, t1, t2, ALU.add, [rt1, rt2], [rt1])
                    t1f = t1.rearrange("p k g h -> p k (g h)")
                    for t4_ in range(4):
                        b = nextbank()
                        for i in range(4):
                            tau = t4_ * 4 + i
                            P.op("pe", lambda e, b=b, i=i, tau=tau, d=d, ft=ft, t1f=t1f: e.matmul(
                                psb[b][:, i * 128:(i + 1) * 128], t1f[:, tau, :], PC0[:, d * 512 + ft * 128:d * 512 + (ft + 1) * 128],
                                start=True, stop=True), [rt1, rPC0], [("ps", b)])
                        for i in range(4):
                            tau = t4_ * 4 + i
                            if tau == 0:
                                if d == 0:
                                    tt("dve", L0[:, ft, :], psb[b][:, 0:128], bdmask[:], ALU.mult, [("ps", b), ("bdmask",)], [rL0 + (ft,)])
                                else:
                                    tt("dve", tl0, psb[b][:, 0:128], bdmask[:], ALU.mult, [("ps", b), ("bdmask",)], [rtl0])
                                    tt("dve", L0[:, ft, :], L0[:, ft, :], tl0, ALU.add, [rL0 + (ft,), rtl0], [rL0 + (ft,)])
                                    stt(L0[:, ft, :], identf[:], d4[:, ft:ft + 1], L0[:, ft, :], ALU.mult, ALU.add,
                                        [("identf",), ("d4",), rL0 + (ft,)], [rL0 + (ft,)])
                                    ts("dve", L0B[:, ft, :], L0[:, ft, :], 1.0, ALU.mult, [rL0 + (ft,)], [rL0B + (ft,)])
                                    dma("sp", lag_s[ft][:, 0:128], L0B[:, ft, :], [rL0B + (ft,)], [("lag_s", ft, 0)])
                            else:
                                tt("dve", LGS[:, sl, tau - 1, :], psb[b][:, i * 128:(i + 1) * 128],
                                   bdmask[:], ALU.mult, [("ps", b), ("bdmask",)], [rLGS + (sl, tau)])
                    base = (1 + 15 * d) * 128
                    dma("sp", lag_s[ft][:, base:base + 15 * 128], LGS[:, sl].rearrange("p l c -> p (l c)"),
                        [rLGS + (sl, tau) for tau in range(1, 16)], [("lag_s", ft, 1 + d)])
                    if LEVEL <= 3:
                        continue
                    for t4_ in range(4):
                        b = nextbank()
                        for i in range(4):
                            k = t4_ * 4 + i
                            P.op("pe", lambda e, b=b, i=i, k=k, t1f=t1f: e.transpose(psb[b][:, i * 128:(i + 1) * 128], t1f[:, k, :], identf[:]),
                                 [rt1, ("identf",)], [("ps", b)])
                        src = psb[b].rearrange("p (k r c) -> p k r c", k=4, r=2)
                        dst = LRS[:, sl, t4_ * 4:t4_ * 4 + 4, :, :]
                        if True:
                            P.op("act", lambda e, src=src, dst=dst: e.activation(out=dst[:, :, :, 0:64], in_=src, func=AF.Copy, scale=ev_m),
                                 [("ps", b), M], [rLRS + (sl, t4_, 0)])
                        else:
                            ts("dve", dst[:, :, :, 0:64], src, ev_m, ALU.mult, [("ps", b), M], [rLRS + (sl, t4_, 0)])
                        ts("dve", dst[:, :, :, 64:128], src, od_m, ALU.mult, [("ps", b), M], [rLRS + (sl, t4_, 1)])
                    dma("sp", lreim_s[ft, d], LRS[:, sl].rearrange("p k r c -> p (k r c)"),
                        [rLRS + (sl, a_, b_) for a_ in range(4) for b_ in range(2)], [("lreim_s", ft, d)])
                    if LEVEL <= 4:
                        continue
                    crb = bh(CR4[:, d, gs, :]); cib = bh(CI4[:, d, gs, :])
                    erb = bk(ER4[:, 1:17, d, gs]); eib = bk(EI4[:, 1:17, d, gs])
                    tt("pool", t2, crb, erb, ALU.mult, rCRall + rERall, [rt2])
                    tt("pool", t3, cib, eib, ALU.mult, rCIall + rEIall, [rt3])
                    tt("pool", t2, t2, t3, ALU.subtract, [rt2, rt3], [rt2])
                    tt("dve", t4, crb, eib, ALU.mult, rCRall + rEIall, [rt4])
                    tt("dve", t5, cib, erb, ALU.mult, rCIall + rERall, [rt5])
                    tt("pool", t4, t4, t5, ALU.add, [rt4, rt5], [rt4])
                    for (eng, T_, r_, reim, sels) in (("pool", t2, rt2, 0, (sel_lo, sel_hi)), ("dve", t4, rt4, 1, (nsel_lo, nsel_hi))):
                        for gp in range(2):
                            src = T_.rearrange("p k (q two) h -> p q k two h", two=2)[:, :, :, gp, :]
                            dst = PCS[:, sl, :, reim, :, gp * 16:(gp + 1) * 16]
                            P.op(eng, lambda e, src=src, dst=dst, sc=sels[gp]: e.tensor_scalar(out=dst, in0=src, scalar1=sc, scalar2=0.0,
                                                                                               op0=ALU.mult, op1=ALU.add),
                                 [r_, M], [rPCS + (sl, reim, gp)])
                    dma("sp", pcp_s[ft, d], PCS[:, sl].rearrange("p q r k c -> p (q r k c)"),
                        [rPCS + (sl, a_, b_) for a_ in range(2) for b_ in range(2)], [("pcp_s", ft, d)])

        def norm_stage(src_ap, tt_, ni, src_res, scale_out=True):
            P.op("act", lambda e: e.activation(out=junk, in_=src_ap, func=AF.Square, accum_out=ss[:, ni, tt_:tt_ + 1]),
                 src_res, [("yfs", 0), ("yfs", 1), ("ss", ni, tt_)])
            P.op("act", lambda e: e.activation(out=ss[:, ni, tt_:tt_ + 1], in_=ss[:, ni, tt_:tt_ + 1], func=AF.Sqrt,
                                               scale=1.0 / DM, bias=EPS),
                 [("ss", ni, tt_)], [("ss", ni, tt_)])
            P.op("dve", lambda e: e.reciprocal(out=ss[:, ni, tt_:tt_ + 1], in_=ss[:, ni, tt_:tt_ + 1]),
                 [("ss", ni, tt_)], [("ss", ni, tt_)])
            if not scale_out:
                return
            sl = tt_ % 3
            P.op("dve", lambda e: e.tensor_scalar(out=xsb[:, sl, :], in0=src_ap, scalar1=ss[:, ni, tt_:tt_ + 1], scalar2=None,
                                                  op0=ALU.mult),
                 src_res + [("ss", ni, tt_)], [("xsb", sl)])

        def transpose_stage(tt_, gi):
            sl = tt_ % 3
            b = nextbank()
            pT = psb[b].bitcast(BF16)[:, 0:1024].rearrange("p (k t) -> p k t", k=8)
            for kt in range(8):
                P.op("pe", lambda e, kt=kt: e.transpose(pT[:, kt, :], xsb[:, sl, kt * 128:(kt + 1) * 128], identb[:]),
                     [("xsb", sl), ("identb",)], [("ps", b)])
            P.op("dve", lambda e: e.tensor_tensor(out=XT[:, :, tt_ * 128:(tt_ + 1) * 128], in0=pT,
                                                  in1=g8[:, gi, :].unsqueeze(2).broadcast_to([128, 8, 128]), op=ALU.mult),
                 [("ps", b), ("g8", gi)], [("XT", tt_)])

        uJC = [u_v[:, ft, :].rearrange("p (j c) -> p j c", j=TC) for ft in range(4)]

        def mix_front(s, mode="all"):
            if mode != "a" and s > 0:
                dma("pool", win_v, w_in.rearrange("(k p) c -> p k c", p=128), [], [("win", kt) for kt in range(8)])
            pending = []

            def proj_unit(fc, blk):
                b = nextbank()
                for kt in range(8):
                    P.op("pe", lambda e, fc=fc, kt=kt, b=b, blk=blk: e.matmul(
                        psb[b], win_v[:, kt, fc * 128:(fc + 1) * 128], XT[:, kt, blk * 512:(blk + 1) * 512],
                        start=(kt == 0), stop=(kt == 7)),
                        [("win", kt)] + [("XT", t4) for t4 in range(blk * 4, blk * 4 + 4)], [("ps", b)])
                if fc < 4:
                    dst = uJC[fc][:, :, blk * 32:(blk + 1) * 32]
                    src = psb[b].rearrange("p (c j) -> p j c", j=TC)
                else:
                    dst = vT_v[:, fc % 4, blk * 512:(blk + 1) * 512]
                    src = psb[b]
                rn = ("u" if fc < 4 else "vT", fc % 4, blk)
                P.op("act", lambda e, src=src, dst=dst: e.activation(out=dst, in_=src, func=AF.Copy), [("ps", b)], [rn])

            SK = 2
            if mode == "b":
                for blk in range(4):
                    for fc in range(8):
                        proj_unit(fc, blk)
                return
            for it in range(NTT + SK):
                if it < NTT:
                    sl = it % 2
                    dma("sp", xst[:, sl, :], x[s, it * 128:(it + 1) * 128, :], [], [("xst", sl)])
                    norm_stage(xst[:, sl, :], it, 0, [("xst", sl)])
                t2 = it - SK
                if t2 >= 0:
                    transpose_stage(t2, 0)
                    if t2 % 4 == 3 and mode == "all":
                        pending.extend((fc, t2 // 4) for fc in range(8))
                for _ in range(2):
                    if pending:
                        proj_unit(*pending.pop(0))
            while pending:
                proj_unit(*pending.pop(0))

        def fourier(s):
            for lt in range(16):
                b0 = (nextbank() // 2) * 2
                psrot[0] = b0 + 2
                for hh in range(4):
                    bb = b0 + hh // 2
                    P.op("pe", lambda e, hh=hh, bb=bb, lt=lt: e.matmul(
                        ps[:, bb, (hh % 2) * 256:(hh % 2) * 256 + 256], vT_v[:, hh, lt * 128:(lt + 1) * 128], cwsw[:, hh, :],
                        start=True, stop=True),
                        [("vT", hh, lt // 4), ("cwsw", hh, 0), ("cwsw", hh, 1)], [("ps", bb)])
                for bb in (b0, b0 + 1):
                    src = ps[:, bb, :].rearrange("p (h a c) -> p h a c", h=2, a=2)
                    dstv = AB_v[:, lt, :, (bb - b0) * 256:(bb - b0) * 256 + 256].rearrange("p a (h c) -> p h a c", h=2)
                    P.op("act", lambda e, src=src, dstv=dstv: e.activation(out=dstv, in_=src, func=AF.Copy),
                         [("ps", bb)], [("AB", lt, bb - b0)])
            for kt in range(16):
                sl = kt % 2
                dma("sp", dft_v[:, sl].rearrange("p a l c -> p (a l c)"), dftT_d[kt], [], [("dft", sl)])
                b = nextbank()
                n = 0
                for a in range(2):
                    for lt in range(16):
                        P.op("pe", lambda e, a=a, lt=lt, b=b, sl=sl, n=n: e.matmul(
                            psb[b], dft_v[:, sl, a, lt, :], AB_v[:, lt, a, :], start=(n == 0), stop=(n == 31)),
                            [("dft", sl), ("AB", lt, 0), ("AB", lt, 1)], [("ps", b)])
                        n += 1
                P.op("act", lambda e, b=b, sl=sl: e.activation(out=yfs[:, sl, :], in_=psb[b], func=AF.Copy), [("ps", b)], [("yfs", sl)])
                b2 = nextbank()
                pT = psb[b2].bitcast(BF16)[:, 0:512].rearrange("p (f t) -> p f t", f=4)
                for f in range(4):
                    P.op("pe", lambda e, f=f, sl=sl, pT=pT: e.transpose(pT[:, f, :], yfs[:, sl, f * 128:(f + 1) * 128], identb[:]),
                         [("yfs", sl), ("identb",)], [("ps", b2)])
                P.op("act", lambda e, pT=pT, kt=kt: e.activation(out=ycat_v[:, 4:8, kt * 128:(kt + 1) * 128], in_=pT, func=AF.Copy),
                     [("ps", b2)], [("ycat", 4 + f, kt // 4) for f in range(4)])


        def s5_states(s):
            it = 0
            for ft in range(4):
                for d in range(2):
                    buf = it % 2
                    it += 1
                    dma("sp", s5s_v[:, buf].rearrange("p k r c -> p (k r c)"), lreim_s[ft, d], [("lreim_s", ft, d)], [("s5s", buf)])
                    b4 = (nextbank() // 4) * 4
                    psrot[0] = b4 + 4
                    for reim in range(2):
                        for i in range(TC):
                            k = (TC - 1 - i) if d == 0 else i
                            for q in range(4):
                                b = b4 + q
                                P.op("pe", lambda e, b=b, reim=reim, i=i, k=k, q=q, ft=ft, buf=buf: e.matmul(
                                    psb[b][:, reim * 128:(reim + 1) * 128], s5s_v[32 * q:32 * q + 32, buf, k, reim, :],
                                    uJC[ft][32 * q:32 * q + 32, i, :], start=(i == 0), stop=(i == TC - 1), tile_position=(32 * q, 0)),
                                    [("s5s", buf)] + [("u", ft, bl) for bl in range(4)], [("ps", b)])
                    for q in range(4):
                        Q = ft * 4 + q
                        b = b4 + q
                        P.op("act", lambda e, b=b, d=d, Q=Q: e.activation(out=X5[:, :, d, Q, :], in_=psb[b][:, 0:256].rearrange("p (r c) -> p r c", r=2),
                                                                          func=AF.Copy), [("ps", b)], [("XV",), ("XS",)])

        def s5_scan(s):
            P.op("dve", lambda e: e.memset(scur[:, 0], 0.0), [], [("scur", 0)])
            arb = a16[:, 0].unsqueeze(1).broadcast_to([128, 2, 2, 16])
            for st in range(NCH - 1):
                cu, nx = st % 2, (st + 1) % 2
                col = bass.AP(X5.tensor, X5.offset + st, [list(X5.ap[0]), [4096, 2], [2048 + (NCH - 1) - 2 * st, 2], [128, 16]])
                P.op("dve", lambda e, cu=cu: e.tensor_tensor(out=stmp[:, 0], in0=scur[:, cu], in1=arb, op=ALU.mult),
                     [("scur", cu), ("a16", 0)], [("stmp", 0)])
                P.op("dve", lambda e, cu=cu: e.tensor_tensor(out=stmp[:, 1, 0], in0=scur[:, cu, 1], in1=a16n[:, 0], op=ALU.mult),
                     [("scur", cu), ("a16n", 0)], [("stmp", 1, 0)])
                P.op("dve", lambda e, cu=cu: e.tensor_tensor(out=stmp[:, 1, 1], in0=scur[:, cu, 0], in1=a16n[:, 1], op=ALU.mult),
                     [("scur", cu), ("a16n", 1)], [("stmp", 1, 1)])
                P.op("dve", lambda e: e.tensor_tensor(out=stmp[:, 0], in0=stmp[:, 0], in1=stmp[:, 1], op=ALU.add),
                     [("stmp", 0), ("stmp", 1, 0), ("stmp", 1, 1)], [("stmp", 0)])
                P.op("dve", lambda e, nx=nx, col=col: e.tensor_tensor(out=scur[:, nx], in0=stmp[:, 0], in1=col, op=ALU.add),
                     [("stmp", 0), ("XV",)], [("scur", nx)])
                P.op("pool", lambda e, nx=nx, col=col: e.tensor_copy(out=col, in_=scur[:, nx]), [("scur", nx)], [("XS",)])

        def s5_out(s):
            it = 0
            for ft in range(4):
                lb = ft % 2
                dma("sp", lagb_v[:, lb, 0:31, :].rearrange("p l c -> p (l c)"), lag_s[ft],
                    [("lag_s", ft, 0), ("lag_s", ft, 1), ("lag_s", ft, 2)], [("lagb", lb)])
                B0 = (ft % 2) * 4
                Yb = [ps[:, B0 + jb, :].rearrange("p (j c) -> p j c", j=4) for jb in range(4)]
                for jb in range(4):
                    j0 = 4 * jb
                    mm = [(0, j0, j0 + 4, 0)]
                    for tau in range(1, TC):
                        lo, hi = max(tau, j0), j0 + 4
                        if lo < hi:
                            mm.append((tau, lo, hi, -tau))
                    for tau in range(1, TC):
                        lo, hi = j0, min(TC - tau, j0 + 4)
                        if lo < hi:
                            mm.append((TC - 1 + tau, lo, hi, tau))
                    for n, (lg, lo, hi, sh) in enumerate(mm):
                        P.op("pe", lambda e, lg=lg, lo=lo, hi=hi, sh=sh, jb=jb, j0=j0, n=n, lb=lb, ft=ft, Yb=Yb: e.matmul(
                            Yb[jb][:, lo - j0:hi - j0, :], lagb_v[:, lb, lg, :], uJC[ft][:, lo + sh:hi + sh, :], start=(n == 0), stop=False,
                            skip_group_check=True),
                            [("lagb", lb)] + [("u", ft, bl) for bl in range(4)], [("ps", B0 + jb)])
                for d in range(2):
                    pb_ = it % 2
                    it += 1
                    dma("sp", pcpb_v[:, pb_].rearrange("p q r k c -> p (q r k c)"), pcp_s[ft, d], [("pcp_s", ft, d)], [("pcpb", pb_)])
                    for j in range(TC):
                        kk = j if d == 0 else TC - 1 - j
                        for reim in range(2):
                            for q in range(4):
                                Q = ft * 4 + q
                                if d == 0:
                                    rhs = X5[:, reim, 0, Q, 0:NCH - 1]
                                    oc0 = 1
                                else:
                                    rhs = X5[:, reim, 1, Q, 1:NCH]
                                    oc0 = 0
                                last = (d == 1 and j == TC - 1 and reim == 1)
                                oap = ps[32 * q:32 * q + 32, B0 + j // 4, (j % 4) * 128 + oc0:(j % 4) * 128 + oc0 + NCH - 1]
                                P.op("pe", lambda e, oap=oap, pb_=pb_, q=q, reim=reim, kk=kk, rhs=rhs, last=last: e.matmul(
                                    oap, pcpb_v[:, pb_, q, reim, kk, :], rhs, start=False, stop=last, tile_position=(0, 32 * q),
                                    skip_group_check=True),
                                    [("pcpb", pb_), ("XS",)], [("ps", B0 + j // 4)])
                zJC = ycat_v[:, ft, :].rearrange("p (c j) -> p j c", j=TC)
                for jb in range(4):
                    P.op("act", lambda e, jb=jb, Yb=Yb, zJC=zJC: e.activation(out=zJC[:, 4 * jb:4 * jb + 4, :], in_=Yb[jb], func=AF.Gelu_apprx_tanh),
                         [("ps", B0 + jb)], [("ycat", ft, bl) for bl in range(4)])

        def glu(s):
            for blk in range(4):
                cols = slice(blk * 512, (blk + 1) * 512)
                bs = []
                for m in range(4):
                    b = nextbank()
                    bs.append(b)
                    for kt in range(4):
                        P.op("pe", lambda e, b=b, m=m, kt=kt, cols=cols: e.matmul(psb[b], wglu_v[:, kt, m * 128:(m + 1) * 128], ycat_v[:, kt, cols],
                                                                                  start=(kt == 0), stop=(kt == 3)),
                             [("wglu", kt), ("ycat", kt, blk)], [("ps", b)])
                for m in range(4):
                    sl = m % 2
                    P.op("act", lambda e, b=bs[m], m=m, sl=sl: e.activation(out=xst[:, sl, 0:512], in_=psb[b], func=AF.Sigmoid, bias=bglu4[:, m:m + 1]),
                         [("ps", bs[m]), ("bglu4",)], [("xst", sl)])
                    P.op("pool", lambda e, m=m, sl=sl, cols=cols: e.tensor_tensor(out=ycat_v[:, m, cols], in0=ycat_v[:, m, cols], in1=xst[:, sl, 0:512], op=ALU.mult),
                         [("xst", sl), ("ycat", m, blk)], [("ycat", m, blk)])

        def wout_load(s):
            dma("pool", wout_v, w_out.rearrange("(k p) c -> p k c", p=128), [], [("wout", kt) for kt in range(8)])

        def wout_phase(s):
            SK = 2
            for it in range(NTT + SK):
                if it < NTT:
                    tt_ = it
                    sl = tt_ % 2
                    dma("sp", xst[:, sl, :], x[s, tt_ * 128:(tt_ + 1) * 128, :], [], [("xst", sl)])
                    for hf in range(2):
                        b = nextbank()
                        for kt in range(8):
                            P.op("pe", lambda e, kt=kt, b=b, hf=hf, tt_=tt_: e.matmul(
                                psb[b], ycat_v[:, kt, tt_ * 128:(tt_ + 1) * 128], wout_v[:, kt, hf * 512:(hf + 1) * 512],
                                start=(kt == 0), stop=(kt == 7)),
                                [("ycat", kt, tt_ // 4), ("wout", kt)], [("ps", b)])
                        P.op("dve", lambda e, b=b, hf=hf, tt_=tt_, sl=sl: e.tensor_tensor(
                            out=h_v[:, tt_, hf * 512:(hf + 1) * 512], in0=psb[b], in1=xst[:, sl, hf * 512:(hf + 1) * 512], op=ALU.add),
                            [("ps", b), ("xst", sl)], [("h", tt_, hf)])
                    norm_stage(h_v[:, tt_, :], tt_, 1, [("h", tt_, 0), ("h", tt_, 1)])
                if it - SK >= 0:
                    transpose_stage(it - SK, 1)

        def ffn(s):
            starts = [sum(FFN_PARTS[:i]) for i in range(len(FFN_PARTS))]

            def load_wup(pi):
                npt, t0, wb = FFN_PARTS[pi], starts[pi], pi % 2
                for gv in range(2):
                    c0 = gv * DFF + t0 * 128
                    for kt in range(8):
                        dma("pool", wup_v[wb][:, kt, gv, 0:npt * 128], w_up[kt * 128:(kt + 1) * 128, c0:c0 + npt * 128], [], [("wup", wb, kt, gv)])

            def load_wdn(pi):
                npt, t0, wb = FFN_PARTS[pi], starts[pi], pi % 2
                dma("pool", wdn_v[wb][:, 0:npt, :], w_down[t0 * 128:(t0 + npt) * 128, :].rearrange("(t p) d -> p t d", p=128), [], [("wdn", wb)])

            def up_tile(pi, tl):
                t0, wb = starts[pi], pi % 2
                ftile = t0 + tl
                for gv in range(2):
                    B0 = 4 * gv
                    ch = gv * NFT + ftile
                    for blk in range(4):
                        for kt in range(8):
                            P.op("pe", lambda e, B0=B0, blk=blk, kt=kt, wb=wb, gv=gv, tl=tl: e.matmul(
                                ps[:, B0 + blk, :], wup_v[wb][:, kt, gv, tl * 128:(tl + 1) * 128], XT[:, kt, blk * 512:(blk + 1) * 512],
                                start=(kt == 0), stop=(kt == 7)),
                                [("wup", wb, kt, gv)] + [("XT", t4) for t4 in range(blk * 4, blk * 4 + 4)], [("ps", B0 + blk)])
                    row = ps[:, B0:B0 + 4, :].rearrange("p b n -> p (b n)")
                    for blk in range(4):
                        a_ = acc_v[:, gv * 2 + blk % 2, :]
                        ra = ("acc", gv * 2 + blk % 2)
                        c_lo, c_hi = blk * 512, (blk + 1) * 512
                        rd = [("ps", B0 + blk)]
                        P.op("act", lambda e, a_=a_, row=row, c_lo=c_lo, c_hi=c_hi, ch=ch: e.activation(
                            out=a_, in_=row[:, c_lo:c_hi], func=AF.Identity, scale=cw3[:, 1, ch:ch + 1], bias=cb1[:, ch:ch + 1]),
                            rd + [("cw3", 1), ("cb1",)], [ra])
                        lo = 1 if blk == 0 else 0
                        P.op("dve", lambda e, a_=a_, row=row, c_lo=c_lo, c_hi=c_hi, ch=ch, lo=lo: e.scalar_tensor_tensor(
                            out=a_[:, lo:512], in0=row[:, c_lo + lo - 1:c_hi - 1], scalar=cw3[:, 0, ch:ch + 1], in1=a_[:, lo:512],
                            op0=ALU.mult, op1=ALU.add),
                            rd + ([("ps", B0 + blk - 1)] if blk > 0 else []) + [ra, ("cw3", 0)], [ra])
                        hi = 511 if blk == 3 else 512
                        P.op("dve", lambda e, a_=a_, row=row, c_lo=c_lo, c_hi=c_hi, ch=ch, hi=hi: e.scalar_tensor_tensor(
                            out=a_[:, 0:hi], in0=row[:, c_lo + 1:c_lo + hi + 1], scalar=cw3[:, 2, ch:ch + 1], in1=a_[:, 0:hi],
                            op0=ALU.mult, op1=ALU.add),
                            rd + ([("ps", B0 + blk + 1)] if blk < 3 else []) + [ra, ("cw3", 2)], [ra])
                        if gv == 0:
                            P.op("act", lambda e, a_=a_, blk=blk: e.activation(out=gact4[blk], in_=a_, func=AF.Gelu_apprx_tanh),
                                 [ra], [("gact", blk)])
                        else:
                            P.op("pool", lambda e, a_=a_, blk=blk, wb=wb, tl=tl, c_lo=c_lo, c_hi=c_hi: e.tensor_tensor(
                                out=act_v[wb][:, tl, c_lo:c_hi], in0=a_, in1=gact4[blk], op=ALU.mult),
                                [ra, ("gact", blk)], [("act", wb, tl, blk)])

            def down(pis, last=False):
                tiles = [(pi % 2, tl) for pi in pis for tl in range(FFN_PARTS[pi])]
                ostg = wup_v[0].rearrange("p k g c -> p (k g c)")[:, 0:6144].bitcast(F32).rearrange("p (b n) -> p b n", b=3)
                ores = [("wup", 0, kt, gv) for kt in range(8) for gv in range(2)]
                for tt_ in range(NTT):
                    for hf in range(2):
                        b = nextbank()
                        for n_, (wb, tl) in enumerate(tiles):
                            P.op("pe", lambda e, b=b, tl=tl, tt_=tt_, hf=hf, wb=wb, n_=n_: e.matmul(
                                psb[b], act_v[wb][:, tl, tt_ * 128:(tt_ + 1) * 128], wdn_v[wb][:, tl, hf * 512:(hf + 1) * 512],
                                start=(n_ == 0), stop=(n_ == len(tiles) - 1)),
                                [("act", wb, tl, tt_ // 4), ("wdn", wb)], [("ps", b)])
                        P.op("dve", lambda e, b=b, tt_=tt_, hf=hf: e.tensor_tensor(
                            out=h_v[:, tt_, hf * 512:(hf + 1) * 512], in0=psb[b], in1=h_v[:, tt_, hf * 512:(hf + 1) * 512], op=ALU.add),
                            [("ps", b), ("h", tt_, hf)], [("h", tt_, hf)])
                    if last:
                        norm_stage(h_v[:, tt_, :], tt_, 2, [("h", tt_, 0), ("h", tt_, 1)], scale_out=False)
                        ob = tt_ % 3
                        P.op("dve", lambda e, tt_=tt_, ob=ob: e.scalar_tensor_tensor(out=ostg[:, ob, :], in0=h_v[:, tt_, :], scalar=ss[:, 2, tt_:tt_ + 1],
                                                                                    in1=gfin[:], op0=ALU.mult, op1=ALU.mult),
                             [("h", tt_, 0), ("h", tt_, 1), ("ss", 2, tt_), ("gfin",)], ores + [("ostg", ob)])
                        dma("sp", out[s, tt_ * 128:(tt_ + 1) * 128, :], ostg[:, ob, :], [("ostg", ob)], [("out", s, tt_)])

            NP = len(FFN_PARTS)
            assert NP % 2 == 0
            load_wup(0); load_wup(1); load_wdn(0); load_wdn(1)
            for pi in range(NP):
                for tl in range(FFN_PARTS[pi]):
                    up_tile(pi, tl)
                    if tl == 0:
                        if pi % 2 == 0 and pi >= 2:
                            load_wup(pi + 1); load_wdn(pi); load_wdn(pi + 1)
                        if pi % 2 == 1 and pi + 1 < NP:
                            load_wup(pi + 1)
                if pi % 2 == 1:
                    down([pi - 1, pi], last=(pi == NP - 1))

        def final(s):
            for tt_ in range(NTT):
                norm_stage(h_v[:, tt_, :], tt_, 2, [("h", tt_, 0), ("h", tt_, 1)], scale_out=False)
                sl = tt_ % 2
                P.op("dve", lambda e, tt_=tt_, sl=sl: e.scalar_tensor_tensor(out=xst[:, sl, :], in0=h_v[:, tt_, :], scalar=ss[:, 2, tt_:tt_ + 1],
                                                                            in1=gfin[:], op0=ALU.mult, op1=ALU.mult),
                     [("h", tt_, 0), ("h", tt_, 1), ("ss", 2, tt_), ("gfin",)], [("xst", sl)])
                dma("sp", out[s, tt_ * 128:(tt_ + 1) * 128, :], xst[:, sl, :], [("xst", sl)], [("out", s, tt_)])

        def dump(nm, ap2d, reads):
            if nm in dbg_out:
                dma("sp", dbg_out[nm], ap2d, reads, [("dbg", nm)])

        MIXN = ["u", "vT", "AB", "ycat", "XV", "XS", "s5s", "win", "dft", "wout", "lagb", "pcpb"]
        import os as _os
        dma("pool", win_v, w_in.rearrange("(k p) c -> p k c", p=128), [], [("win", kt) for kt in range(8)])
        P.phase = 'setup'
        s5_setup()
        barrier(["SU", "XT"] + [n for n in MIXN if n != "win"])
        nseq = 2 if upto == "all" else 1
        for s in range(nseq):
            P.phase = 'mix_front%d' % s
            mix_front(s, "all")
            if upto == "front":
                dump("u", arena[:, 0:4 * L], [("u", f, b_) for f in range(4) for b_ in range(4)])
                dump("vT", arena[:, 4 * L:8 * L], [("vT", f, b_) for f in range(4) for b_ in range(4)])
                dump("win", arena[:, WOFF // 2:WOFF // 2 + 8 * DM], [("win", k_) for k_ in range(8)])
                dump("XT", XT[:].rearrange("p k t -> p (k t)"), [("XT", t_) for t_ in range(NTT)])
                break
            P.phase = 's5_states%d' % s
            s5_states(s)
            P.phase = 's5_scan%d' % s
            s5_scan(s)
            P.phase = 'fourier%d' % s
            fourier(s)
            barrier(["AB", "dft", "lagb", "pcpb"])
            P.phase = 's5_out%d' % s
            s5_out(s)
            barrier(["XV", "XS", "wout"])
            wout_load(s)
            P.phase = 'glu%d' % s
            glu(s)
            if upto == "mix":
                dump("ycat", arena[:, R1 // 2:R1 // 2 + 8 * L], [("ycat", f, b_) for f in range(8) for b_ in range(4)])
                dump("X", X_v, [("XS",)])
                break
            barrier(["u", "vT", "AB", "lagb", "pcpb", "h", "win", "dft"])
            P.phase = 'wout_phase%d' % s
            wout_phase(s)
            if upto == "wout":
                dump("h", arena[:, 0:32 * KB].bitcast(F32), [("h", t_, hf) for t_ in range(NTT) for hf in range(2)])
                break
            barrier(MIXN + FFN_NAMES)
            P.phase = 'ffn%d' % s
            ffn(s)
            barrier(MIXN + FFN_NAMES + ["h", "ostg"])
        P.op("sp", None, [("out", s_, t_) for s_ in range(nseq) for t_ in range(NTT)] + [("dbg", nm) for nm in dbg_out], [("done",)])

        allsems = list(csem.values()) + rings["sp"] + rings["pool"]
        with nc.Block() as block0:
            def _clr(e):
                for sm in allsems:
                    e.sem_clear(sm)
            block0.sync(_clr)
        with nc.Block() as block:
            P.emit(nc, block, csem, rings)
    return nc, P


def _in_maps(inputs):
    global _CONSTS
    if _CONSTS is None:
        _CONSTS = _host_consts()
    f = lambda a: np.ascontiguousarray(np.asarray(a, dtype=np.float32))
    shared = {
        "g_mix": f(inputs["g_mix"][0]), "w_in": f(inputs["w_in"][0]),
        "lam_re": f(inputs["ssm_lam_re"][0]), "lam_im": f(inputs["ssm_lam_im"][0]), "log_dt": f(inputs["ssm_log_dt"][0]),
        "b_re": f(inputs["ssm_b_re"][0]), "b_im": f(inputs["ssm_b_im"][0]), "c_re": f(inputs["ssm_c_re"][0]), "c_im": f(inputs["ssm_c_im"][0]),
        "ssm_d": f(inputs["ssm_d"][0]), "w_glu": f(inputs["w_glu"][0]), "b_glu": f(inputs["b_glu"][0]),
        "w_fourier": f(inputs["w_fourier"][0]), "w_out": f(inputs["w_out"][0]), "g_ffn": f(inputs["g_ffn"][0]),
        "w_up": f(inputs["w_up"][0]), "conv_w": f(inputs["conv_w"][0]), "conv_b": f(inputs["conv_b"][0]),
        "w_down": f(inputs["w_down"][0]), "g_final": f(inputs["g_final"]),
    }
    c = _CONSTS
    shared.update({"identb": c["identb"], "identf": c["identf"], "cdft": c["cdft"], "sdft": c["sdft"],
                   "dftT": c["dftT"].reshape(16, 128, 2 * 16 * 128), "msk": c["msk"], "bdmask": c["bdmask"]})
    xs = f(inputs["x"])
    maps = []
    for c_ in range(NCORES):
        m = dict(shared)
        m["x"] = xs[2 * c_:2 * c_ + 2]
        maps.append(m)
    return maps


def kernel(**inputs):
    nc, _ = build()
    res = run_bass_kernel_spmd(nc, _in_maps(inputs), core_ids=list(range(NCORES)))
    return np.concatenate([np.asarray(r["out"], dtype=np.float32) for r in res.results], axis=0)
```

```python
import math
from contextlib import ExitStack

import numpy as np
import ml_dtypes
import concourse.bass as bass
import concourse.mybir as mybir
from concourse.bass_utils import run_bass_kernel_spmd

F32 = mybir.dt.float32
BF16 = mybir.dt.bfloat16
AF = mybir.ActivationFunctionType
ALU = mybir.AluOpType

NCORES = 8
L = 2048
DM = 1024
NTT = 16
DFF = 2816
NFT = 22
TC = 16
NCH = L // TC
EPS = 1e-6
FFN_PARTS = [4, 4, 4, 4, 3, 3]
MAGIC = 12582912.0


class _Op:
    __slots__ = ("eng", "fn", "deps", "dma", "idx", "sem", "val", "prev", "has_dep", "phase")


class Prog:
    ENGS = ("pe", "act", "dve", "pool", "sp")

    def __init__(self):
        self.ops = []
        self.lastw = {}
        self.readers = {}
        self.bufbar = {}
        self.phase = "init"
        self.scopes = False

    def op(self, eng, fn, reads=(), writes=(), dma=False):
        psr = [r for r in reads if r[0] == "ps"]
        if psr:
            reads = [r for r in reads if r[0] != "ps"]
            writes = list(writes) + [r for r in psr if r not in writes]
        i = len(self.ops)
        deps = set()
        for r in reads:
            w = self.lastw.get(r)
            if w is None:
                w = self.bufbar.get(r[0])
            if w is not None:
                deps.add(w)
        for w_ in writes:
            lw = self.lastw.get(w_)
            if lw is None:
                lw = self.bufbar.get(w_[0])
            if lw is not None:
                deps.add(lw)
            deps.update(self.readers.get(w_, ()))
        o = _Op()
        o.eng, o.fn, o.deps, o.dma, o.idx = eng, fn, deps, dma, i
        o.phase = self.phase
        self.ops.append(o)
        for r in reads:
            self.readers.setdefault(r, []).append(i)
        for w_ in writes:
            self.lastw[w_] = i
            self.readers[w_] = []
        return i

    def barrier(self, eng, fn, names):
        names = set(names)
        deps = set()
        for r in list(self.lastw.keys()):
            if r[0] in names:
                deps.add(self.lastw.pop(r))
        for r in list(self.readers.keys()):
            if r[0] in names:
                deps.update(self.readers.pop(r))
        for n in names:
            if n in self.bufbar:
                deps.add(self.bufbar[n])
        i = len(self.ops)
        o = _Op()
        o.eng, o.fn, o.deps, o.dma, o.idx = eng, fn, deps, False, i
        o.phase = self.phase
        self.ops.append(o)
        for n in names:
            self.bufbar[n] = i
        return i

    def emit(self, nc, block, csem, rings):
        ops = self.ops
        for o in ops:
            o.has_dep = False
        for o in ops:
            for d in o.deps:
                ops[d].has_dep = True
        cnt = {e: 0 for e in self.ENGS}
        dcnt = {e: 0 for e in self.ENGS}
        for o in ops:
            o.prev = None
            if o.dma:
                ring = rings[o.eng]
                j = dcnt[o.eng]
                dcnt[o.eng] += 1
                R = len(ring)
                o.sem = ring[j % R]
                o.val = 16 * (j // R + 1)
                if j >= R:
                    o.prev = (ring[j % R], 16 * (j // R))
            elif o.has_dep:
                cnt[o.eng] += 1
                o.sem = csem[o.eng]
                o.val = cnt[o.eng]
        self.counts = (cnt, dcnt)

        def run(engname, e):
            known = {}
            cur = [None]
            cur_id = [None]
            for o in ops:
                if o.eng != engname:
                    continue
                if self.scopes and o.phase != cur[0]:
                    if cur[0] is not None:
                        nc.leave_named_scope(cur[0], cur_id[0], False)
                    cur_id[0] = nc.enter_named_scope(o.phase, False)[0]
                    cur[0] = o.phase
                waits = {}
                for d in o.deps:
                    p = ops[d]
                    if engname == "pe" and p.eng == "pe" and not p.dma:
                        continue
                    k = id(p.sem)
                    if k not in waits or waits[k][1] < p.val:
                        waits[k] = (p.sem, p.val)
                if o.prev is not None:
                    k = id(o.prev[0])
                    if k not in waits or waits[k][1] < o.prev[1]:
                        waits[k] = o.prev
                wl = []
                for k, (sm, v) in waits.items():
                    if known.get(k, 0) >= v:
                        continue
                    known[k] = v
                    wl.append((sm, v))
                if o.fn is None:
                    for sm, v in wl:
                        e.wait_ge(sm, v)
                    continue
                for sm, v in wl[:-1]:
                    e.wait_ge(sm, v)
                ins = o.fn(e)
                if wl:
                    ins._wait_ge(wl[-1][0], wl[-1][1])
                if o.dma:
                    ins.then_inc(o.sem, 16)
                elif o.has_dep:
                    ins.then_inc(o.sem, 1)
            if self.scopes and cur[0] is not None:
                nc.leave_named_scope(cur[0], cur_id[0], False)

        block.tensor(lambda e: run("pe", e))
        block.scalar(lambda e: run("act", e))
        block.vector(lambda e: run("dve", e))
        block.gpsimd(lambda e: run("pool", e))
        block.sync(lambda e: run("sp", e))


def _host_consts():
    c = {}
    c["identb"] = np.eye(128, dtype=np.float32).astype(ml_dtypes.bfloat16)
    c["identf"] = np.eye(128, dtype=np.float32)
    n = np.arange(128)
    ang = 2 * np.pi * np.outer(n, n) / 128.0
    c["cdft"] = (np.cos(ang) / math.sqrt(128.0)).astype(np.float32)
    c["sdft"] = (np.sin(ang) / math.sqrt(128.0)).astype(np.float32)
    l = np.arange(L)
    lk = np.outer(l, l) % L
    angL = 2 * np.pi * lk / L
    CL = (np.cos(angL) / math.sqrt(L)).astype(np.float32)
    SL = (np.sin(angL) / math.sqrt(L)).astype(np.float32)

    def lay(M):
        return M.reshape(16, 128, 16, 128).transpose(2, 1, 0, 3)
    c["dftT"] = np.ascontiguousarray(np.stack([lay(CL), lay(SL)], axis=2)).astype(ml_dtypes.bfloat16)
    p = np.arange(128)
    msk = np.zeros((128, 8), np.float32)
    msk[:, 0] = (p < 64)
    msk[:, 1] = (p >= 64)
    msk[:, 2] = ((p // 16) % 2 == 0)
    msk[:, 3] = ((p // 16) % 2 == 1)
    msk[:, 4] = -1.0 * (p < 64)
    msk[:, 5] = -1.0 * (p >= 64)
    c["msk"] = msk
    bd = (p[:, None] // 16 == p[None, :] // 16).astype(np.float32)
    c["bdmask"] = bd
    return c


_CONSTS = None


def build(upto="all", debug=(), scopes=False):
    nc = bass.Bass("TRN2", target_bir_lowering=False)
    P = Prog()
    P.scopes = scopes
    dbg_out = {}

    def din(name, shape, dt=F32):
        return nc.dram_tensor(name, list(shape), dt, kind="ExternalInput").ap()

    x = din("x", [2, L, DM])
    g_mix = din("g_mix", [DM]); w_in = din("w_in", [DM, DM])
    lam_re = din("lam_re", [2, 32, 64]); lam_im = din("lam_im", [2, 32, 64]); log_dt = din("log_dt", [2, 32])
    b_re = din("b_re", [2, 32, 64, 16]); b_im = din("b_im", [2, 32, 64, 16])
    c_re = din("c_re", [2, 32, 16, 64]); c_im = din("c_im", [2, 32, 16, 64])
    ssm_d = din("ssm_d", [512]); w_glu = din("w_glu", [512, 512]); b_glu = din("b_glu", [512])
    w_fourier = din("w_fourier", [4, 128, 128]); w_out = din("w_out", [DM, DM]); g_ffn = din("g_ffn", [DM])
    w_up = din("w_up", [DM, 2 * DFF]); conv_w = din("conv_w", [3, 2 * DFF]); conv_b = din("conv_b", [2 * DFF])
    w_down = din("w_down", [DFF, DM]); g_final = din("g_final", [DM])
    identb_d = din("identb", [128, 128], BF16); identf_d = din("identf", [128, 128])
    cdft_d = din("cdft", [128, 128]); sdft_d = din("sdft", [128, 128])
    dftT_d = din("dftT", [16, 128, 2 * 16 * 128], BF16)
    msk_d = din("msk", [128, 8]); bdmask_d = din("bdmask", [128, 128])
    out = nc.dram_tensor("out", [2, L, DM], F32, kind="ExternalOutput").ap()
    lreim_s = nc.dram_tensor("lreim_s", [4, 2, 128, 16 * 2 * 128], BF16).ap()
    pcp_s = nc.dram_tensor("pcp_s", [4, 2, 128, 4 * 2 * 16 * 32], BF16).ap()
    lag_s = nc.dram_tensor("lag_s", [4, 128, 31 * 128], BF16).ap()

    dbg_shapes = {"u": ([128, 4 * L], BF16), "vT": ([128, 4 * L], BF16), "ycat": ([128, 8 * L], BF16),
                  "h": ([128, NTT * DM], F32), "XT": ([128, 8 * L], BF16), "win": ([128, 8 * DM], BF16), "X": ([128, 2 * 2 * 16 * NCH], BF16)}
    for nm in debug:
        shp, dtp = dbg_shapes[nm]
        dbg_out[nm] = nc.dram_tensor("dbg_" + nm, shp, dtp, kind="ExternalOutput").ap()

    es = ExitStack()
    with es:
        def sb(name, shape, dt):
            return es.enter_context(nc.sbuf_tensor(name, list(shape), dt))

        KB = 1024
        ARENA_B = 144 * KB
        arena = sb("arena", [128, ARENA_B // 2], BF16)

        def av(off_b, size_b, dt=BF16):
            v = arena[:, off_b // 2:(off_b + size_b) // 2]
            return v if dt == BF16 else v.bitcast(dt)

        R0, R1, R2 = 0, 64 * KB, 96 * KB
        h_v = av(R0, 64 * KB, F32).rearrange("p (t d) -> p t d", t=NTT)
        u_v = av(R0, 16 * KB).rearrange("p (f t) -> p f t", f=4)
        vT_v = av(R0 + 16 * KB, 16 * KB).rearrange("p (f t) -> p f t", f=4)
        AB_v = av(R0 + 32 * KB, 32 * KB).rearrange("p (l a c) -> p l a c", l=16, a=2)
        ycat_v = av(R1, 32 * KB).rearrange("p (f t) -> p f t", f=8)
        X_v = av(R2, 16 * KB)
        X5 = X_v.rearrange("p (r d q c) -> p r d q c", r=2, d=2, q=16)
        s5s_v = av(R2 + 16 * KB, 16 * KB).rearrange("p (b k r c) -> p b k r c", b=2, k=16, r=2)
        WOFF = R2 + 32 * KB
        win_v = av(WOFF, 16 * KB).rearrange("p (k c) -> p k c", k=8)
        dft_v = av(WOFF, 16 * KB).rearrange("p (b a l c) -> p b a l c", b=2, a=2, l=16)
        wout_v = av(R2, 16 * KB).rearrange("p (k c) -> p k c", k=8)
        lagb_v = av(R0 + 32 * KB, 16 * KB).rearrange("p (b l c) -> p b l c", b=2, l=32)
        pcpb_v = av(R0 + 48 * KB, 16 * KB).rearrange("p (b q r k c) -> p b q r k c", b=2, q=4, r=2, k=16)
        FB = R1
        maxnp = max(FFN_PARTS)
        wup_b = maxnp * 2 * 128 * 8 * 2
        wup_v = [av(FB + i * wup_b, wup_b).rearrange("p (k g c) -> p k g c", k=8, g=2) for i in range(2)]
        o1 = FB + 2 * wup_b
        wdn_b = maxnp * DM * 2
        wdn_v = [av(o1 + i * wdn_b, wdn_b).rearrange("p (t d) -> p t d", t=maxnp) for i in range(2)]
        o2 = o1 + 2 * wdn_b
        act_b = maxnp * L * 2
        act_v = [av(o2 + i * act_b, act_b).rearrange("p (t n) -> p t n", t=maxnp) for i in range(2)]
        o3 = o2 + 2 * act_b
        assert o3 <= ARENA_B, (o3, ARENA_B)
        FFN_NAMES = ["wup", "wdn", "act", "acc", "gact", "xst", "xsb"]
        SU0 = 0
        su_off = [SU0]

        def su(size_b, dt=F32):
            o = su_off[0]
            su_off[0] += size_b
            assert su_off[0] <= WOFF, su_off[0]
            return av(o, size_b, dt)

        XT = sb("XT", [128, 8, L], BF16)
        xst = sb("xst", [128, 2, DM], F32)
        xsb = sb("xsb", [128, 3, DM], BF16)
        acc_v = xst[:].rearrange("p a (b n) -> p (a b) n", b=2)
        gact4w = xsb[:].rearrange("p a n -> p (a n)")[:, 0:2048].rearrange("p (b n) -> p b n", b=4)
        gact4 = [gact4w[:, i, :] for i in range(4)]
        identb = sb("identb_sb", [128, 128], BF16)
        identf = sb("identf_sb", [128, 128], F32)
        msk = sb("msk_sb", [128, 8], F32)
        bdmask = sb("bdmask_sb", [128, 128], F32)
        g8 = sb("g8", [128, 2, 8], F32)
        gfin = sb("gfin", [128, DM], F32)
        ss = sb("ss", [128, 3, NTT], F32)
        cwsw = sb("cwsw", [128, 4, 256], BF16)
        bglu4 = sb("bglu4", [128, 4], F32)
        cw3 = sb("cw3", [128, 3, 44], F32)
        cb1 = sb("cb1", [128, 44], F32)
        d4 = sb("d4", [128, 4], F32)
        a16 = sb("a16", [128, 2, 2, 16], F32)
        a16n = sb("a16n", [128, 2, 2, 16], F32)
        scur = sb("scur", [128, 2, 2, 2, 16], F32)
        stmp = sb("stmp", [128, 2, 2, 2, 16], F32)
        yfs = sb("yfs", [128, 2, 512], BF16)
        junk = yfs[:].rearrange("p a n -> p (a n)")
        bar = sb("bar", [128, 64], F32)
        wglu_v = sb("wglu_sb", [128, 4, 512], BF16)
        _xt2 = XT[:].rearrange("p k t -> p (k t)")
        wf = _xt2[:, 0:1024].bitcast(F32).rearrange("p (h c) -> p h c", h=4)
        cs = _xt2[:, 1024:1536].bitcast(F32).rearrange("p (a c) -> p a c", a=2)
        ps = es.enter_context(nc.psum_tensor("ps", [128, 8, 512], F32))

        csem = {e: es.enter_context(nc.semaphore("c_" + e)) for e in ("pe", "act", "dve", "pool")}
        rings = {"sp": [es.enter_context(nc.semaphore(f"dsp{i}")) for i in range(24)],
                 "pool": [es.enter_context(nc.semaphore(f"dpl{i}")) for i in range(12)],
                 "act": [], "pe": [], "dve": []}
        nbar = [0]

        def barrier(names):
            i = nbar[0]
            nbar[0] += 1
            P.barrier("pool", lambda e, i=i: e.memset(bar[:, i:i + 1], 0.0), names)

        def dma(q, out_ap, in_ap, reads, writes, slow=False):
            if slow:
                P.op(q, lambda e: e.dma_start(out=out_ap, in_=in_ap, allow_slow_non_contiguous=True), reads, writes, dma=True)
            else:
                P.op(q, lambda e: e.dma_start(out=out_ap, in_=in_ap), reads, writes, dma=True)

        psb = [ps[:, b, :] for b in range(8)]
        psrot = [0]

        def nextbank():
            b = psrot[0] % 8
            psrot[0] += 1
            return b

        def PSR(b, n=1):
            return [("ps", b + i) for i in range(n)]

        sel_lo, sel_hi, ev_m, od_m, nsel_lo, nsel_hi = (msk[:, i:i + 1] for i in range(6))

        dma("sp", identb[:], identb_d, [], [("identb",)])
        dma("sp", identf[:], identf_d, [], [("identf",)])
        dma("sp", msk[:], msk_d, [], [("msk",)])
        dma("sp", bdmask[:], bdmask_d, [], [("bdmask",)])
        dma("sp", g8[:, 0, :], g_mix.rearrange("(k p) -> p k", p=128), [], [("g8", 0)], slow=True)
        dma("sp", g8[:, 1, :], g_ffn.rearrange("(k p) -> p k", p=128), [], [("g8", 1)], slow=True)
        dma("sp", bglu4[:], b_glu.rearrange("(k p) -> p k", p=128), [], [("bglu4",)], slow=True)
        dma("sp", d4[:], ssm_d.rearrange("(k p) -> p k", p=128), [], [("d4",)], slow=True)
        dma("sp", cb1[:], conv_b.rearrange("(k p) -> p k", p=128), [], [("cb1",)], slow=True)
        for k in range(3):
            dma("sp", cw3[:, k, :], conv_w[k].rearrange("(k p) -> p k", p=128), [], [("cw3", k)], slow=True)
        dma("sp", gfin[:], g_final.partition_broadcast(128), [], [("gfin",)])
        for kt in range(4):
            dma("pool", wglu_v[:, kt, :], w_glu[kt * 128:(kt + 1) * 128, :], [], [("wglu", kt)])
        dma("sp", wf[:], w_fourier.rearrange("h c j -> c h j"), [], [("SU", "wf")])
        dma("sp", cs[:, 0, :], cdft_d, [], [("SU", "cs", 0)])
        dma("sp", cs[:, 1, :], sdft_d, [], [("SU", "cs", 1)])

        for hh in range(4):
            b = nextbank()
            for a in range(2):
                P.op("pe", lambda e, hh=hh, a=a, b=b: e.matmul(psb[b][:, a * 128:(a + 1) * 128], cs[:, a, :], wf[:, hh, :],
                                                             start=True, stop=True),
                     [("SU", "cs", a), ("SU", "wf")], [("ps", b)])
            P.op("act", lambda e, hh=hh, b=b: e.activation(out=cwsw[:, hh, 0:128], in_=psb[b][:, 0:128], func=AF.Copy),
                 [("ps", b)], [("cwsw", hh, 0)])
            P.op("act", lambda e, hh=hh, b=b: e.activation(out=cwsw[:, hh, 128:256], in_=psb[b][:, 128:256], func=AF.Copy,
                                                           scale=-1.0),
                 [("ps", b)], [("cwsw", hh, 1)])

        def s5_setup():
            S = lambda *a: ("SU",) + a
            cnt = [0]

            def tile(shape, dt=F32):
                n = 1
                for s_ in shape:
                    n *= s_
                v = su(n * (4 if dt == F32 else 2), dt)
                cnt[0] += 1
                if len(shape) == 1:
                    return v, S(cnt[0])
                names = "abcde"[:len(shape)]
                pat = "p (" + " ".join(names) + ") -> p " + " ".join(names)
                return v.rearrange(pat, **{nm: s_ for nm, s_ in zip(names, shape)}), S(cnt[0])

            import os as _os2
            LEVEL = int(_os2.environ.get("K_LEVEL", "9"))

            def tt(eng, o, a, b_, op, rs, ws):
                P.op(eng, lambda e: e.tensor_tensor(out=o, in0=a, in1=b_, op=op), rs, ws)

            def ts(eng, o, a, s1, op0, rs, ws, s2=None, op1=None):
                if op1 is None:
                    P.op(eng, lambda e: e.tensor_scalar(out=o, in0=a, scalar1=s1, scalar2=None, op0=op0), rs, ws)
                else:
                    P.op(eng, lambda e: e.tensor_scalar(out=o, in0=a, scalar1=s1, scalar2=s2, op0=op0, op1=op1), rs, ws)

            def stt(o, a, sc, b_, op0, op1, rs, ws):
                P.op("dve", lambda e: e.scalar_tensor_tensor(out=o, in0=a, scalar=sc, in1=b_, op0=op0, op1=op1), rs, ws)

            def act(o, a, func, rs, ws, **kw):
                P.op("act", lambda e: e.activation(out=o, in_=a, func=func, **kw), rs, ws)

            M = ("msk",)
            LIN, rLIN = tile([2, 128])
            LDT, rLDT = tile([64])
            for t_, src in ((0, lam_re), (1, lam_im)):
                for hf in range(2):
                    dma("sp", LIN[0:64, t_, hf * 64:(hf + 1) * 64], src.rearrange("d g p -> (d g) p"), [], [rLIN + (t_, hf)])
            dma("sp", LDT, log_dt.rearrange("d g -> (d g)").partition_broadcast(128), [], [rLDT])
            BR2, rBR = tile([2, 32, 16]); BI2, rBI = tile([2, 32, 16])
            for (T_, r_, src) in ((BR2, rBR, b_re), (BI2, rBI, b_im)):
                for hf in range(2):
                    dma("sp", T_[hf * 64:(hf + 1) * 64], src.rearrange("d g p h -> p d g h"), [], [r_ + (hf,)])
            CIN, rCIN = tile([2, 8, 128])
            for t_, src in ((0, c_re), (1, c_im)):
                for hf in range(2):
                    dma("sp", CIN[:, t_, :, hf * 64:(hf + 1) * 64], src.rearrange("d g h p -> (d g h) p").rearrange("(r q) p -> q r p", q=128),
                        [], [rCIN + (t_, hf)])
            CR2, rCR = tile([1024]); CI2, rCI = tile([1024])
            for t_, (T_, r_) in enumerate(((CR2, rCR), (CI2, rCI))):
                for r4 in range(2):
                    b = nextbank()
                    for i in range(4):
                        r = r4 * 4 + i
                        P.op("pe", lambda e, b=b, i=i, t_=t_, r=r: e.transpose(psb[b][:, i * 128:(i + 1) * 128], CIN[:, t_, r, :], identf[:]),
                             [rCIN + (t_, 0), rCIN + (t_, 1), ("identf",)], [("ps", b)])
                    act(T_[:, r4 * 512:(r4 + 1) * 512], psb[b], AF.Copy, [("ps", b)], [r_ + (r4,)])
            LAM, rLAM = tile([2, 64])
            b = nextbank()
            for t_ in range(2):
                P.op("pe", lambda e, b=b, t_=t_: e.transpose(psb[b][:, t_ * 64:(t_ + 1) * 64], LIN[0:64, t_, :], identf[0:64, 0:64]),
                     [rLIN + (t_, 0), rLIN + (t_, 1), ("identf",)], [("ps", b)])
            act(LAM.rearrange("p a b -> p (a b)"), psb[b][:, 0:128], AF.Copy, [("ps", b)], [rLAM])
            LAMR, LAMI = LAM[:, 0, :], LAM[:, 1, :]
            if LEVEL <= 1:
                return
            sm, rsm = tile([20, 64])
            k_ = [0]

            def new():
                i = k_[0]
                k_[0] += 1
                return sm[:, i, :], rsm + (i,)
            dt_, rdt = new(); lr, rlr = new(); li, rli = new(); ea, rea = new(); tA, rtA = new(); nn, rnn = new()
            ri, rri = new(); s1, rs1 = new(); ab, rab = new(); c1, rc1 = new(); den, rden = new(); t0, rt0 = new()
            wr, rwr = new(); qr, rqr = new(); qi, rqi = new(); t1_, rt1_ = new(); t2_, rt2_ = new()
            act(dt_, LDT, AF.Exp, [rLDT], [rdt])
            tt("dve", lr, LAMR, dt_, ALU.mult, [rLAM, rdt], [rlr])
            tt("dve", li, LAMI, dt_, ALU.mult, [rLAM, rdt], [rli])
            act(ea, lr, AF.Exp, [rlr], [rea])
            ts("dve", tA, li, 1.0 / (2 * math.pi), ALU.mult, [rli], [rtA], s2=MAGIC, op1=ALU.add)
            ts("dve", nn, tA, -MAGIC, ALU.add, [rtA], [rnn])
            stt(ri, nn, -2 * math.pi, li, ALU.mult, ALU.add, [rnn, rli], [rri])
            act(s1, ri, AF.Sin, [rri], [rs1])
            act(ab, ri, AF.Abs, [rri], [rab])
            act(c1, ab, AF.Sin, [rab], [rc1], scale=-1.0, bias=math.pi / 2)
            ER, rER = tile([17, 64]); EI, rEI = tile([17, 64])
            P.op("dve", lambda e: e.memset(ER[:, 0, :], 1.0), [], [rER + (0,)])
            P.op("dve", lambda e: e.memset(EI[:, 0, :], 0.0), [], [rEI + (0,)])
            tt("dve", ER[:, 1, :], ea, c1, ALU.mult, [rea, rc1], [rER + (1,)])
            tt("dve", EI[:, 1, :], ea, s1, ALU.mult, [rea, rs1], [rEI + (1,)])
            ta, rta = tile([8, 64]); tb, rtb = tile([8, 64])
            n = 1
            while n < 16:
                lo, hi = 1, n + 1
                src_r = [rER + (i,) for i in range(lo, hi)] + [rEI + (i,) for i in range(lo, hi)]
                ern = ER[:, n:n + 1, :].broadcast_to([128, n, 64]) if n > 1 else ER[:, n:n + 1, :]
                ein = EI[:, n:n + 1, :].broadcast_to([128, n, 64]) if n > 1 else EI[:, n:n + 1, :]
                tt("dve", ta[:, 0:n, :], ER[:, lo:hi, :], ern, ALU.mult, src_r, [rta])
                tt("dve", tb[:, 0:n, :], EI[:, lo:hi, :], ein, ALU.mult, src_r, [rtb])
                tt("dve", ER[:, n + 1:2 * n + 1, :], ta[:, 0:n, :], tb[:, 0:n, :], ALU.subtract, [rta, rtb],
                   [rER + (i,) for i in range(n + 1, 2 * n + 1)])
                tt("dve", ta[:, 0:n, :], ER[:, lo:hi, :], ein, ALU.mult, src_r, [rta])
                tt("dve", tb[:, 0:n, :], EI[:, lo:hi, :], ern, ALU.mult, src_r, [rtb])
                tt("dve", EI[:, n + 1:2 * n + 1, :], ta[:, 0:n, :], tb[:, 0:n, :], ALU.add, [rta, rtb],
                   [rEI + (i,) for i in range(n + 1, 2 * n + 1)])
                n *= 2
            rERall = [rER + (i,) for i in range(17)]
            rEIall = [rEI + (i,) for i in range(17)]
            tt("dve", den, LAMR, LAMR, ALU.mult, [rLAM], [rden])
            tt("dve", t0, LAMI, LAMI, ALU.mult, [rLAM], [rt0])
            tt("dve", den, den, t0, ALU.add, [rden, rt0], [rden])
            P.op("dve", lambda e: e.reciprocal(out=den, in_=den), [rden], [rden])
            ts("dve", wr, ER[:, 1, :], -1.0, ALU.add, [rER + (1,)], [rwr])
            tt("dve", t1_, wr, LAMR, ALU.mult, [rwr, rLAM], [rt1_])
            tt("dve", t2_, EI[:, 1, :], LAMI, ALU.mult, [rEI + (1,), rLAM], [rt2_])
            tt("dve", t1_, t1_, t2_, ALU.add, [rt1_, rt2_], [rt1_])
            tt("dve", qr, t1_, den, ALU.mult, [rt1_, rden], [rqr])
            tt("dve", t1_, EI[:, 1, :], LAMR, ALU.mult, [rEI + (1,), rLAM], [rt1_])
            tt("dve", t2_, wr, LAMI, ALU.mult, [rwr, rLAM], [rt2_])
            tt("dve", t1_, t1_, t2_, ALU.subtract, [rt1_, rt2_], [rt1_])
            tt("dve", qi, t1_, den, ALU.mult, [rt1_, rden], [rqi])
            GR, rGR = tile([16, 64]); GI, rGI = tile([16, 64]); tg, rtg = tile([16, 64]); th, rth = tile([16, 64])
            qrb = qr.unsqueeze(1).broadcast_to([128, 16, 64]); qib = qi.unsqueeze(1).broadcast_to([128, 16, 64])
            tt("dve", tg, ER[:, 0:16, :], qrb, ALU.mult, rERall + [rqr], [rtg])
            tt("dve", th, EI[:, 0:16, :], qib, ALU.mult, rEIall + [rqi], [rth])
            tt("dve", GR, tg, th, ALU.subtract, [rtg, rth], [rGR])
            tt("dve", tg, ER[:, 0:16, :], qib, ALU.mult, rERall + [rqi], [rtg])
            tt("dve", th, EI[:, 0:16, :], qrb, ALU.mult, rEIall + [rqr], [rth])
            tt("dve", GI, tg, th, ALU.add, [rtg, rth], [rGI])
            AL, rAL = tile([16, 64]); BE, rBE = tile([16, 64])
            ts("dve", AL, GR, sel_lo, ALU.mult, [rGR, M], [rAL])
            stt(AL, GI, sel_hi, AL, ALU.mult, ALU.add, [rGI, rAL, M], [rAL])
            ts("dve", BE, GR, sel_hi, ALU.mult, [rGR, M], [rBE])
            stt(BE, GI, nsel_lo, BE, ALU.mult, ALU.add, [rGI, rBE, M], [rBE])
            for i_, (T_, r_) in enumerate(((ER, rER), (EI, rEI))):
                e16 = T_[:, 16, :].rearrange("p (d q two) -> p d q two", d=2, two=2)
                ts("dve", a16[:, i_], e16[:, :, :, 0], sel_lo, ALU.mult, [r_ + (16,), M], [("a16", i_)])
                stt(a16[:, i_], e16[:, :, :, 1], sel_hi, a16[:, i_], ALU.mult, ALU.add, [r_ + (16,), ("a16", i_), M], [("a16", i_)])
            ts("dve", a16n[:, 0], a16[:, 1], -1.0, ALU.mult, [("a16", 1)], [("a16n", 0)])
            ts("dve", a16n[:, 1], a16[:, 1], 1.0, ALU.mult, [("a16", 1)], [("a16n", 1)])
            PC0, rPC0 = tile([1024])
            rCRall = [rCR + (0,), rCR + (1,)]; rCIall = [rCI + (0,), rCI + (1,)]
            ts("dve", PC0, CR2, sel_lo, ALU.mult, rCRall + [M], [rPC0])
            stt(PC0, CI2, nsel_hi, PC0, ALU.mult, ALU.add, rCIall + [rPC0, M], [rPC0])
            if LEVEL <= 2:
                return
            L0, rL0 = tile([4, 128])
            AL4 = AL.rearrange("p k (d g) -> p k d g", d=2); BE4 = BE.rearrange("p k (d g) -> p k d g", d=2)
            ER4 = ER.rearrange("p k (d g) -> p k d g", d=2); EI4 = EI.rearrange("p k (d g) -> p k d g", d=2)
            CR4 = CR2.rearrange("p (d g h) -> p d g h", d=2, g=32); CI4 = CI2.rearrange("p (d g h) -> p d g h", d=2, g=32)
            t1, rt1 = tile([16, 8, 16]); t2, rt2 = tile([16, 8, 16]); t3, rt3 = tile([16, 8, 16])
            t4, rt4 = tile([16, 8, 16])
            t5, rt5 = CIN.rearrange("p a r c -> p (a r c)").rearrange("p (k g h) -> p k g h", k=16, g=8), S("t5alias")
            LGS, rLGS = tile([1, 15, 128], BF16)
            LRS, rLRS = tile([1, 16, 2, 128], BF16)
            PCS, rPCS = tile([1, 4, 2, 16, 32], BF16)
            L0B, rL0B = tile([4, 128], BF16)
            tl0, rtl0 = tile([128])
            it = 0
            for d in range(2):
                for ft in range(4):
                    sl = 0
                    it += 1
                    gs = slice(ft * 8, ft * 8 + 8)
                    bk = lambda v: v.unsqueeze(3).broadcast_to([128, 16, 8, 16])
                    bh = lambda v: v.unsqueeze(1).broadcast_to([128, 16, 8, 16])
                    tt("dve", t1, bk(AL4[:, :, d, gs]), bh(BR2[:, d, gs, :]), ALU.mult, [rAL, rBR + (0,), rBR + (1,)], [rt1])
                    tt("pool", t2, bk(BE4[:, :, d, gs]), bh(BI2[:, d, gs, :]), ALU.mult, [rBE, rBI + (0,), rBI + (1,)], [rt2])
                    tt("dve", t1, t1, t2, ALU.add, [rt1, rt2], [rt1])
                    t1f = t1.rearrange("p k g h -> p k (g h)")
                    for t4_ in range(4):
                        b = nextbank()
                        for i in range(4):
                            tau = t4_ * 4 + i
                            P.op("pe", lambda e, b=b, i=i, tau=tau, d=d, ft=ft, t1f=t1f: e.matmul(
                                psb[b][:, i * 128:(i + 1) * 128], t1f[:, tau, :], PC0[:, d * 512 + ft * 128:d * 512 + (ft + 1) * 128],
                                start=True, stop=True), [rt1, rPC0], [("ps", b)])
                        for i in range(4):
                            tau = t4_ * 4 + i
                            if tau == 0:
                                if d == 0:
                                    tt("dve", L0[:, ft, :], psb[b][:, 0:128], bdmask[:], ALU.mult, [("ps", b), ("bdmask",)], [rL0 + (ft,)])
                                else:
                                    tt("dve", tl0, psb[b][:, 0:128], bdmask[:], ALU.mult, [("ps", b), ("bdmask",)], [rtl0])
                                    tt("dve", L0[:, ft, :], L0[:, ft, :], tl0, ALU.add, [rL0 + (ft,), rtl0], [rL0 + (ft,)])
                                    stt(L0[:, ft, :], identf[:], d4[:, ft:ft + 1], L0[:, ft, :], ALU.mult, ALU.add,
                                        [("identf",), ("d4",), rL0 + (ft,)], [rL0 + (ft,)])
                                    ts("dve", L0B[:, ft, :], L0[:, ft, :], 1.0, ALU.mult, [rL0 + (ft,)], [rL0B + (ft,)])
                                    dma("sp", lag_s[ft][:, 0:128], L0B[:, ft, :], [rL0B + (ft,)], [("lag_s", ft, 0)])
                            else:
                                tt("dve", LGS[:, sl, tau - 1, :], psb[b][:, i * 128:(i + 1) * 128],
                                   bdmask[:], ALU.mult, [("ps", b), ("bdmask",)], [rLGS + (sl, tau)])
                    base = (1 + 15 * d) * 128
                    dma("sp", lag_s[ft][:, base:base + 15 * 128], LGS[:, sl].rearrange("p l c -> p (l c)"),
                        [rLGS + (sl, tau) for tau in range(1, 16)], [("lag_s", ft, 1 + d)])
                    if LEVEL <= 3:
                        continue
                    for t4_ in range(4):
                        b = nextbank()
                        for i in range(4):
                            k = t4_ * 4 + i
                            P.op("pe", lambda e, b=b, i=i, k=k, t1f=t1f: e.transpose(psb[b][:, i * 128:(i + 1) * 128], t1f[:, k, :], identf[:]),
                                 [rt1, ("identf",)], [("ps", b)])
                        src = psb[b].rearrange("p (k r c) -> p k r c", k=4, r=2)
                        dst = LRS[:, sl, t4_ * 4:t4_ * 4 + 4, :, :]
                        if True:
                            P.op("act", lambda e, src=src, dst=dst: e.activation(out=dst[:, :, :, 0:64], in_=src, func=AF.Copy, scale=ev_m),
                                 [("ps", b), M], [rLRS + (sl, t4_, 0)])
                        else:
                            ts("dve", dst[:, :, :, 0:64], src, ev_m, ALU.mult, [("ps", b), M], [rLRS + (sl, t4_, 0)])
                        ts("dve", dst[:, :, :, 64:128], src, od_m, ALU.mult, [("ps", b), M], [rLRS + (sl, t4_, 1)])
                    dma("sp", lreim_s[ft, d], LRS[:, sl].rearrange("p k r c -> p (k r c)"),
                        [rLRS + (sl, a_, b_) for a_ in range(4) for b_ in range(2)], [("lreim_s", ft, d)])
                    if LEVEL <= 4:
                        continue
                    crb = bh(CR4[:, d, gs, :]); cib = bh(CI4[:, d, gs, :])
                    erb = bk(ER4[:, 1:17, d, gs]); eib = bk(EI4[:, 1:17, d, gs])
                    tt("pool", t2, crb, erb, ALU.mult, rCRall + rERall, [rt2])
                    tt("pool", t3, cib, eib, ALU.mult, rCIall + rEIall, [rt3])
                    tt("pool", t2, t2, t3, ALU.subtract, [rt2, rt3], [rt2])
                    tt("dve", t4, crb, eib, ALU.mult, rCRall + rEIall, [rt4])
                    tt("dve", t5, cib, erb, ALU.mult, rCIall + rERall, [rt5])
                    tt("pool", t4, t4, t5, ALU.add, [rt4, rt5], [rt4])
                    for (eng, T_, r_, reim, sels) in (("pool", t2, rt2, 0, (sel_lo, sel_hi)), ("dve", t4, rt4, 1, (nsel_lo, nsel_hi))):
                        for gp in range(2):
                            src = T_.rearrange("p k (q two) h -> p q k two h", two=2)[:, :, :, gp, :]
                            dst = PCS[:, sl, :, reim, :, gp * 16:(gp + 1) * 16]
                            P.op(eng, lambda e, src=src, dst=dst, sc=sels[gp]: e.tensor_scalar(out=dst, in0=src, scalar1=sc, scalar2=0.0,
                                                                                               op0=ALU.mult, op1=ALU.add),
                                 [r_, M], [rPCS + (sl, reim, gp)])
                    dma("sp", pcp_s[ft, d], PCS[:, sl].rearrange("p q r k c -> p (q r k c)"),
                        [rPCS + (sl, a_, b_) for a_ in range(2) for b_ in range(2)], [("pcp_s", ft, d)])

        def norm_stage(src_ap, tt_, ni, src_res, scale_out=True):
            P.op("act", lambda e: e.activation(out=junk, in_=src_ap, func=AF.Square, accum_out=ss[:, ni, tt_:tt_ + 1]),
                 src_res, [("yfs", 0), ("yfs", 1), ("ss", ni, tt_)])
            P.op("act", lambda e: e.activation(out=ss[:, ni, tt_:tt_ + 1], in_=ss[:, ni, tt_:tt_ + 1], func=AF.Sqrt,
                                               scale=1.0 / DM, bias=EPS),
                 [("ss", ni, tt_)], [("ss", ni, tt_)])
            P.op("dve", lambda e: e.reciprocal(out=ss[:, ni, tt_:tt_ + 1], in_=ss[:, ni, tt_:tt_ + 1]),
                 [("ss", ni, tt_)], [("ss", ni, tt_)])
            if not scale_out:
                return
            sl = tt_ % 3
            P.op("dve", lambda e: e.tensor_scalar(out=xsb[:, sl, :], in0=src_ap, scalar1=ss[:, ni, tt_:tt_ + 1], scalar2=None,
                                                  op0=ALU.mult),
                 src_res + [("ss", ni, tt_)], [("xsb", sl)])

        def transpose_stage(tt_, gi):
            sl = tt_ % 3
            b = nextbank()
            pT = psb[b].bitcast(BF16)[:, 0:1024].rearrange("p (k t) -> p k t", k=8)
            for kt in range(8):
                P.op("pe", lambda e, kt=kt: e.transpose(pT[:, kt, :], xsb[:, sl, kt * 128:(kt + 1) * 128], identb[:]),
                     [("xsb", sl), ("identb",)], [("ps", b)])
            P.op("dve", lambda e: e.tensor_tensor(out=XT[:, :, tt_ * 128:(tt_ + 1) * 128], in0=pT,
                                                  in1=g8[:, gi, :].unsqueeze(2).broadcast_to([128, 8, 128]), op=ALU.mult),
                 [("ps", b), ("g8", gi)], [("XT", tt_)])

        uJC = [u_v[:, ft, :].rearrange("p (j c) -> p j c", j=TC) for ft in range(4)]

        def mix_front(s, mode="all"):
            if mode != "a" and s > 0:
                dma("pool", win_v, w_in.rearrange("(k p) c -> p k c", p=128), [], [("win", kt) for kt in range(8)])
            pending = []

            def proj_unit(fc, blk):
                b = nextbank()
                for kt in range(8):
                    P.op("pe", lambda e, fc=fc, kt=kt, b=b, blk=blk: e.matmul(
                        psb[b], win_v[:, kt, fc * 128:(fc + 1) * 128], XT[:, kt, blk * 512:(blk + 1) * 512],
                        start=(kt == 0), stop=(kt == 7)),
                        [("win", kt)] + [("XT", t4) for t4 in range(blk * 4, blk * 4 + 4)], [("ps", b)])
                if fc < 4:
                    dst = uJC[fc][:, :, blk * 32:(blk + 1) * 32]
                    src = psb[b].rearrange("p (c j) -> p j c", j=TC)
                else:
                    dst = vT_v[:, fc % 4, blk * 512:(blk + 1) * 512]
                    src = psb[b]
                rn = ("u" if fc < 4 else "vT", fc % 4, blk)
                P.op("act", lambda e, src=src, dst=dst: e.activation(out=dst, in_=src, func=AF.Copy), [("ps", b)], [rn])

            SK = 2
            if mode == "b":
                for blk in range(4):
                    for fc in range(8):
                        proj_unit(fc, blk)
                return
            for it in range(NTT + SK):
                if it < NTT:
                    sl = it % 2
                    dma("sp", xst[:, sl, :], x[s, it * 128:(it + 1) * 128, :], [], [("xst", sl)])
                    norm_stage(xst[:, sl, :], it, 0, [("xst", sl)])
                t2 = it - SK
                if t2 >= 0:
                    transpose_stage(t2, 0)
                    if t2 % 4 == 3 and mode == "all":
                        pending.extend((fc, t2 // 4) for fc in range(8))
                for _ in range(2):
                    if pending:
                        proj_unit(*pending.pop(0))
            while pending:
                proj_unit(*pending.pop(0))

        def fourier(s):
            for lt in range(16):
                b0 = (nextbank() // 2) * 2
                psrot[0] = b0 + 2
                for hh in range(4):
                    bb = b0 + hh // 2
                    P.op("pe", lambda e, hh=hh, bb=bb, lt=lt: e.matmul(
                        ps[:, bb, (hh % 2) * 256:(hh % 2) * 256 + 256], vT_v[:, hh, lt * 128:(lt + 1) * 128], cwsw[:, hh, :],
                        start=True, stop=True),
                        [("vT", hh, lt // 4), ("cwsw", hh, 0), ("cwsw", hh, 1)], [("ps", bb)])
                for bb in (b0, b0 + 1):
                    src = ps[:, bb, :].rearrange("p (h a c) -> p h a c", h=2, a=2)
                    dstv = AB_v[:, lt, :, (bb - b0) * 256:(bb - b0) * 256 + 256].rearrange("p a (h c) -> p h a c", h=2)
                    P.op("act", lambda e, src=src, dstv=dstv: e.activation(out=dstv, in_=src, func=AF.Copy),
                         [("ps", bb)], [("AB", lt, bb - b0)])
            for kt in range(16):
                sl = kt % 2
                dma("sp", dft_v[:, sl].rearrange("p a l c -> p (a l c)"), dftT_d[kt], [], [("dft", sl)])
                b = nextbank()
                n = 0
                for a in range(2):
                    for lt in range(16):
                        P.op("pe", lambda e, a=a, lt=lt, b=b, sl=sl, n=n: e.matmul(
                            psb[b], dft_v[:, sl, a, lt, :], AB_v[:, lt, a, :], start=(n == 0), stop=(n == 31)),
                            [("dft", sl), ("AB", lt, 0), ("AB", lt, 1)], [("ps", b)])
                        n += 1
                P.op("act", lambda e, b=b, sl=sl: e.activation(out=yfs[:, sl, :], in_=psb[b], func=AF.Copy), [("ps", b)], [("yfs", sl)])
                b2 = nextbank()
                pT = psb[b2].bitcast(BF16)[:, 0:512].rearrange("p (f t) -> p f t", f=4)
                for f in range(4):
                    P.op("pe", lambda e, f=f, sl=sl, pT=pT: e.transpose(pT[:, f, :], yfs[:, sl, f * 128:(f + 1) * 128], identb[:]),
                         [("yfs", sl), ("identb",)], [("ps", b2)])
                P.op("act", lambda e, pT=pT, kt=kt: e.activation(out=ycat_v[:, 4:8, kt * 128:(kt + 1) * 128], in_=pT, func=AF.Copy),
                     [("ps", b2)], [("ycat", 4 + f, kt // 4) for f in range(4)])


        def s5_states(s):
            it = 0
            for ft in range(4):
                for d in range(2):
                    buf = it % 2
                    it += 1
                    dma("sp", s5s_v[:, buf].rearrange("p k r c -> p (k r c)"), lreim_s[ft, d], [("lreim_s", ft, d)], [("s5s", buf)])
                    b4 = (nextbank() // 4) * 4
                    psrot[0] = b4 + 4
                    for reim in range(2):
                        for i in range(TC):
                            k = (TC - 1 - i) if d == 0 else i
                            for q in range(4):
                                b = b4 + q
                                P.op("pe", lambda e, b=b, reim=reim, i=i, k=k, q=q, ft=ft, buf=buf: e.matmul(
                                    psb[b][:, reim * 128:(reim + 1) * 128], s5s_v[32 * q:32 * q + 32, buf, k, reim, :],
                                    uJC[ft][32 * q:32 * q + 32, i, :], start=(i == 0), stop=(i == TC - 1), tile_position=(32 * q, 0)),
                                    [("s5s", buf)] + [("u", ft, bl) for bl in range(4)], [("ps", b)])
                    for q in range(4):
                        Q = ft * 4 + q
                        b = b4 + q
                        P.op("act", lambda e, b=b, d=d, Q=Q: e.activation(out=X5[:, :, d, Q, :], in_=psb[b][:, 0:256].rearrange("p (r c) -> p r c", r=2),
                                                                          func=AF.Copy), [("ps", b)], [("XV",), ("XS",)])

        def s5_scan(s):
            P.op("dve", lambda e: e.memset(scur[:, 0], 0.0), [], [("scur", 0)])
            arb = a16[:, 0].unsqueeze(1).broadcast_to([128, 2, 2, 16])
            for st in range(NCH - 1):
                cu, nx = st % 2, (st + 1) % 2
                col = bass.AP(X5.tensor, X5.offset + st, [list(X5.ap[0]), [4096, 2], [2048 + (NCH - 1) - 2 * st, 2], [128, 16]])
                P.op("dve", lambda e, cu=cu: e.tensor_tensor(out=stmp[:, 0], in0=scur[:, cu], in1=arb, op=ALU.mult),
                     [("scur", cu), ("a16", 0)], [("stmp", 0)])
                P.op("dve", lambda e, cu=cu: e.tensor_tensor(out=stmp[:, 1, 0], in0=scur[:, cu, 1], in1=a16n[:, 0], op=ALU.mult),
                     [("scur", cu), ("a16n", 0)], [("stmp", 1, 0)])
                P.op("dve", lambda e, cu=cu: e.tensor_tensor(out=stmp[:, 1, 1], in0=scur[:, cu, 0], in1=a16n[:, 1], op=ALU.mult),
                     [("scur", cu), ("a16n", 1)], [("stmp", 1, 1)])
                P.op("dve", lambda e: e.tensor_tensor(out=stmp[:, 0], in0=stmp[:, 0], in1=stmp[:, 1], op=ALU.add),
                     [("stmp", 0), ("stmp", 1, 0), ("stmp", 1, 1)], [("stmp", 0)])
                P.op("dve", lambda e, nx=nx, col=col: e.tensor_tensor(out=scur[:, nx], in0=stmp[:, 0], in1=col, op=ALU.add),
                     [("stmp", 0), ("XV",)], [("scur", nx)])
                P.op("pool", lambda e, nx=nx, col=col: e.tensor_copy(out=col, in_=scur[:, nx]), [("scur", nx)], [("XS",)])

        def s5_out(s):
            it = 0
            for ft in range(4):
                lb = ft % 2
                dma("sp", lagb_v[:, lb, 0:31, :].rearrange("p l c -> p (l c)"), lag_s[ft],
                    [("lag_s", ft, 0), ("lag_s", ft, 1), ("lag_s", ft, 2)], [("lagb", lb)])
                B0 = (ft % 2) * 4
                Yb = [ps[:, B0 + jb, :].rearrange("p (j c) -> p j c", j=4) for jb in range(4)]
                for jb in range(4):
                    j0 = 4 * jb
                    mm = [(0, j0, j0 + 4, 0)]
                    for tau in range(1, TC):
                        lo, hi = max(tau, j0), j0 + 4
                        if lo < hi:
                            mm.append((tau, lo, hi, -tau))
                    for tau in range(1, TC):
                        lo, hi = j0, min(TC - tau, j0 + 4)
                        if lo < hi:
                            mm.append((TC - 1 + tau, lo, hi, tau))
                    for n, (lg, lo, hi, sh) in enumerate(mm):
                        P.op("pe", lambda e, lg=lg, lo=lo, hi=hi, sh=sh, jb=jb, j0=j0, n=n, lb=lb, ft=ft, Yb=Yb: e.matmul(
                            Yb[jb][:, lo - j0:hi - j0, :], lagb_v[:, lb, lg, :], uJC[ft][:, lo + sh:hi + sh, :], start=(n == 0), stop=False,
                            skip_group_check=True),
                            [("lagb", lb)] + [("u", ft, bl) for bl in range(4)], [("ps", B0 + jb)])
                for d in range(2):
                    pb_ = it % 2
                    it += 1
                    dma("sp", pcpb_v[:, pb_].rearrange("p q r k c -> p (q r k c)"), pcp_s[ft, d], [("pcp_s", ft, d)], [("pcpb", pb_)])
                    for j in range(TC):
                        kk = j if d == 0 else TC - 1 - j
                        for reim in range(2):
                            for q in range(4):
                                Q = ft * 4 + q
                                if d == 0:
                                    rhs = X5[:, reim, 0, Q, 0:NCH - 1]
                                    oc0 = 1
                                else:
                                    rhs = X5[:, reim, 1, Q, 1:NCH]
                                    oc0 = 0
                                last = (d == 1 and j == TC - 1 and reim == 1)
                                oap = ps[32 * q:32 * q + 32, B0 + j // 4, (j % 4) * 128 + oc0:(j % 4) * 128 + oc0 + NCH - 1]
                                P.op("pe", lambda e, oap=oap, pb_=pb_, q=q, reim=reim, kk=kk, rhs=rhs, last=last: e.matmul(
                                    oap, pcpb_v[:, pb_, q, reim, kk, :], rhs, start=False, stop=last, tile_position=(0, 32 * q),
                                    skip_group_check=True),
                                    [("pcpb", pb_), ("XS",)], [("ps", B0 + j // 4)])
                zJC = ycat_v[:, ft, :].rearrange("p (c j) -> p j c", j=TC)
                for jb in range(4):
                    P.op("act", lambda e, jb=jb, Yb=Yb, zJC=zJC: e.activation(out=zJC[:, 4 * jb:4 * jb + 4, :], in_=Yb[jb], func=AF.Gelu_apprx_tanh),
                         [("ps", B0 + jb)], [("ycat", ft, bl) for bl in range(4)])

        def glu(s):
            for blk in range(4):
                cols = slice(blk * 512, (blk + 1) * 512)
                bs = []
                for m in range(4):
                    b = nextbank()
                    bs.append(b)
                    for kt in range(4):
                        P.op("pe", lambda e, b=b, m=m, kt=kt, cols=cols: e.matmul(psb[b], wglu_v[:, kt, m * 128:(m + 1) * 128], ycat_v[:, kt, cols],
                                                                                  start=(kt == 0), stop=(kt == 3)),
                             [("wglu", kt), ("ycat", kt, blk)], [("ps", b)])
                for m in range(4):
                    sl = m % 2
                    P.op("act", lambda e, b=bs[m], m=m, sl=sl: e.activation(out=xst[:, sl, 0:512], in_=psb[b], func=AF.Sigmoid, bias=bglu4[:, m:m + 1]),
                         [("ps", bs[m]), ("bglu4",)], [("xst", sl)])
                    P.op("pool", lambda e, m=m, sl=sl, cols=cols: e.tensor_tensor(out=ycat_v[:, m, cols], in0=ycat_v[:, m, cols], in1=xst[:, sl, 0:512], op=ALU.mult),
                         [("xst", sl), ("ycat", m, blk)], [("ycat", m, blk)])

        def wout_load(s):
            dma("pool", wout_v, w_out.rearrange("(k p) c -> p k c", p=128), [], [("wout", kt) for kt in range(8)])

        def wout_phase(s):
            SK = 2
            for it in range(NTT + SK):
                if it < NTT:
                    tt_ = it
                    sl = tt_ % 2
                    dma("sp", xst[:, sl, :], x[s, tt_ * 128:(tt_ + 1) * 128, :], [], [("xst", sl)])
                    for hf in range(2):
                        b = nextbank()
                        for kt in range(8):
                            P.op("pe", lambda e, kt=kt, b=b, hf=hf, tt_=tt_: e.matmul(
                                psb[b], ycat_v[:, kt, tt_ * 128:(tt_ + 1) * 128], wout_v[:, kt, hf * 512:(hf + 1) * 512],
                                start=(kt == 0), stop=(kt == 7)),
                                [("ycat", kt, tt_ // 4), ("wout", kt)], [("ps", b)])
                        P.op("dve", lambda e, b=b, hf=hf, tt_=tt_, sl=sl: e.tensor_tensor(
                            out=h_v[:, tt_, hf * 512:(hf + 1) * 512], in0=psb[b], in1=xst[:, sl, hf * 512:(hf + 1) * 512], op=ALU.add),
                            [("ps", b), ("xst", sl)], [("h", tt_, hf)])
                    norm_stage(h_v[:, tt_, :], tt_, 1, [("h", tt_, 0), ("h", tt_, 1)])
                if it - SK >= 0:
                    transpose_stage(it - SK, 1)

        def ffn(s):
            starts = [sum(FFN_PARTS[:i]) for i in range(len(FFN_PARTS))]

            def load_wup(pi):
                npt, t0, wb = FFN_PARTS[pi], starts[pi], pi % 2
                for gv in range(2):
                    c0 = gv * DFF + t0 * 128
                    for kt in range(8):
                        dma("pool", wup_v[wb][:, kt, gv, 0:npt * 128], w_up[kt * 128:(kt + 1) * 128, c0:c0 + npt * 128], [], [("wup", wb, kt, gv)])

            def load_wdn(pi):
                npt, t0, wb = FFN_PARTS[pi], starts[pi], pi % 2
                dma("pool", wdn_v[wb][:, 0:npt, :], w_down[t0 * 128:(t0 + npt) * 128, :].rearrange("(t p) d -> p t d", p=128), [], [("wdn", wb)])

            def up_tile(pi, tl):
                t0, wb = starts[pi], pi % 2
                ftile = t0 + tl
                for gv in range(2):
                    B0 = 4 * gv
                    ch = gv * NFT + ftile
                    for blk in range(4):
                        for kt in range(8):
                            P.op("pe", lambda e, B0=B0, blk=blk, kt=kt, wb=wb, gv=gv, tl=tl: e.matmul(
                                ps[:, B0 + blk, :], wup_v[wb][:, kt, gv, tl * 128:(tl + 1) * 128], XT[:, kt, blk * 512:(blk + 1) * 512],
                                start=(kt == 0), stop=(kt == 7)),
                                [("wup", wb, kt, gv)] + [("XT", t4) for t4 in range(blk * 4, blk * 4 + 4)], [("ps", B0 + blk)])
                    row = ps[:, B0:B0 + 4, :].rearrange("p b n -> p (b n)")
                    for blk in range(4):
                        a_ = acc_v[:, blk, :]
                        c_lo, c_hi = blk * 512, (blk + 1) * 512
                        P.op("act", lambda e, a_=a_, row=row, c_lo=c_lo, c_hi=c_hi, ch=ch: e.activation(
                            out=a_, in_=row[:, c_lo:c_hi], func=AF.Identity, scale=cw3[:, 1, ch:ch + 1], bias=cb1[:, ch:ch + 1]),
                            [("ps", B0 + blk), ("cw3", 1), ("cb1",)], [("acc", blk)])
                    for blk in range(4):
                        a_ = acc_v[:, blk, :]
                        ra = ("acc", blk)
                        c_lo, c_hi = blk * 512, (blk + 1) * 512
                        rd = [("ps", B0 + blk)]
                        lo = 1 if blk == 0 else 0
                        P.op("dve", lambda e, a_=a_, row=row, c_lo=c_lo, c_hi=c_hi, ch=ch, lo=lo: e.scalar_tensor_tensor(
                            out=a_[:, lo:512], in0=row[:, c_lo + lo - 1:c_hi - 1], scalar=cw3[:, 0, ch:ch + 1], in1=a_[:, lo:512],
                            op0=ALU.mult, op1=ALU.add),
                            rd + ([("ps", B0 + blk - 1)] if blk > 0 else []) + [ra, ("cw3", 0)], [ra])
                        hi = 511 if blk == 3 else 512
                        P.op("dve", lambda e, a_=a_, row=row, c_lo=c_lo, c_hi=c_hi, ch=ch, hi=hi: e.scalar_tensor_tensor(
                            out=a_[:, 0:hi], in0=row[:, c_lo + 1:c_lo + hi + 1], scalar=cw3[:, 2, ch:ch + 1], in1=a_[:, 0:hi],
                            op0=ALU.mult, op1=ALU.add),
                            rd + ([("ps", B0 + blk + 1)] if blk < 3 else []) + [ra, ("cw3", 2)], [ra])
                        if gv == 0:
                            P.op("act", lambda e, a_=a_, blk=blk: e.activation(out=gact4[blk], in_=a_, func=AF.Gelu_apprx_tanh),
                                 [ra], [("gact", blk)])
                        else:
                            P.op("pool", lambda e, a_=a_, blk=blk, wb=wb, tl=tl, c_lo=c_lo, c_hi=c_hi: e.tensor_tensor(
                                out=act_v[wb][:, tl, c_lo:c_hi], in0=a_, in1=gact4[blk], op=ALU.mult),
                                [ra, ("gact", blk)], [("act", wb, tl, blk)])

            def down(pis, last=False):
                tiles = [(pi % 2, tl) for pi in pis for tl in range(FFN_PARTS[pi])]
                ostg = wup_v[0].rearrange("p k g c -> p (k g c)")[:, 0:6144].bitcast(F32).rearrange("p (b n) -> p b n", b=3)
                ores = [("wup", 0, kt, gv) for kt in range(8) for gv in range(2)]
                for tt_ in range(NTT):
                    for hf in range(2):
                        b = nextbank()
                        for n_, (wb, tl) in enumerate(tiles):
                            P.op("pe", lambda e, b=b, tl=tl, tt_=tt_, hf=hf, wb=wb, n_=n_: e.matmul(
                                psb[b], act_v[wb][:, tl, tt_ * 128:(tt_ + 1) * 128], wdn_v[wb][:, tl, hf * 512:(hf + 1) * 512],
                                start=(n_ == 0), stop=(n_ == len(tiles) - 1)),
                                [("act", wb, tl, tt_ // 4), ("wdn", wb)], [("ps", b)])
                        P.op("dve", lambda e, b=b, tt_=tt_, hf=hf: e.tensor_tensor(
                            out=h_v[:, tt_, hf * 512:(hf + 1) * 512], in0=psb[b], in1=h_v[:, tt_, hf * 512:(hf + 1) * 512], op=ALU.add),
                            [("ps", b), ("h", tt_, hf)], [("h", tt_, hf)])
                    if last:
                        norm_stage(h_v[:, tt_, :], tt_, 2, [("h", tt_, 0), ("h", tt_, 1)], scale_out=False)
                        ob = tt_ % 3
                        P.op("dve", lambda e, tt_=tt_, ob=ob: e.scalar_tensor_tensor(out=ostg[:, ob, :], in0=h_v[:, tt_, :], scalar=ss[:, 2, tt_:tt_ + 1],
                                                                                    in1=gfin[:], op0=ALU.mult, op1=ALU.mult),
                             [("h", tt_, 0), ("h", tt_, 1), ("ss", 2, tt_), ("gfin",)], ores + [("ostg", ob)])
                        dma("sp", out[s, tt_ * 128:(tt_ + 1) * 128, :], ostg[:, ob, :], [("ostg", ob)], [("out", s, tt_)])

            NP = len(FFN_PARTS)
            assert NP % 2 == 0
            load_wup(0); load_wup(1); load_wdn(0); load_wdn(1)
            for pi in range(NP):
                for tl in range(FFN_PARTS[pi]):
                    up_tile(pi, tl)
                    if tl == 0:
                        if pi % 2 == 0 and pi >= 2:
                            load_wup(pi + 1); load_wdn(pi); load_wdn(pi + 1)
                        if pi % 2 == 1 and pi + 1 < NP:
                            load_wup(pi + 1)
                if pi % 2 == 1:
                    down([pi - 1, pi], last=(pi == NP - 1))

        def final(s):
            for tt_ in range(NTT):
                norm_stage(h_v[:, tt_, :], tt_, 2, [("h", tt_, 0), ("h", tt_, 1)], scale_out=False)
                sl = tt_ % 2
                P.op("dve", lambda e, tt_=tt_, sl=sl: e.scalar_tensor_tensor(out=xst[:, sl, :], in0=h_v[:, tt_, :], scalar=ss[:, 2, tt_:tt_ + 1],
                                                                            in1=gfin[:], op0=ALU.mult, op1=ALU.mult),
                     [("h", tt_, 0), ("h", tt_, 1), ("ss", 2, tt_), ("gfin",)], [("xst", sl)])
                dma("sp", out[s, tt_ * 128:(tt_ + 1) * 128, :], xst[:, sl, :], [("xst", sl)], [("out", s, tt_)])

        def dump(nm, ap2d, reads):
            if nm in dbg_out:
                dma("sp", dbg_out[nm], ap2d, reads, [("dbg", nm)])

        MIXN = ["u", "vT", "AB", "ycat", "XV", "XS", "s5s", "win", "dft", "wout", "lagb", "pcpb"]
        import os as _os
        dma("pool", win_v, w_in.rearrange("(k p) c -> p k c", p=128), [], [("win", kt) for kt in range(8)])
        P.phase = 'setup'
        s5_setup()
        barrier(["SU", "XT"] + [n for n in MIXN if n != "win"])
        nseq = 2 if upto == "all" else 1
        for s in range(nseq):
            P.phase = 'mix_front%d' % s
            mix_front(s, "all")
            if upto == "front":
                dump("u", arena[:, 0:4 * L], [("u", f, b_) for f in range(4) for b_ in range(4)])
                dump("vT", arena[:, 4 * L:8 * L], [("vT", f, b_) for f in range(4) for b_ in range(4)])
                dump("win", arena[:, WOFF // 2:WOFF // 2 + 8 * DM], [("win", k_) for k_ in range(8)])
                dump("XT", XT[:].rearrange("p k t -> p (k t)"), [("XT", t_) for t_ in range(NTT)])
                break
            P.phase = 's5_states%d' % s
            s5_states(s)
            P.phase = 's5_scan%d' % s
            s5_scan(s)
            P.phase = 'fourier%d' % s
            fourier(s)
            barrier(["AB", "dft", "lagb", "pcpb"])
            P.phase = 's5_out%d' % s
            s5_out(s)
            barrier(["XV", "XS", "wout"])
            wout_load(s)
            P.phase = 'glu%d' % s
            glu(s)
            if upto == "mix":
                dump("ycat", arena[:, R1 // 2:R1 // 2 + 8 * L], [("ycat", f, b_) for f in range(8) for b_ in range(4)])
                dump("X", X_v, [("XS",)])
                break
            barrier(["u", "vT", "AB", "lagb", "pcpb", "h", "win", "dft"])
            P.phase = 'wout_phase%d' % s
            wout_phase(s)
            if upto == "wout":
                dump("h", arena[:, 0:32 * KB].bitcast(F32), [("h", t_, hf) for t_ in range(NTT) for hf in range(2)])
                break
            barrier(MIXN + FFN_NAMES)
            P.phase = 'ffn%d' % s
            ffn(s)
            barrier(MIXN + FFN_NAMES + ["h", "ostg"])
        P.op("sp", None, [("out", s_, t_) for s_ in range(nseq) for t_ in range(NTT)] + [("dbg", nm) for nm in dbg_out], [("done",)])

        allsems = list(csem.values()) + rings["sp"] + rings["pool"]
        with nc.Block() as block0:
            def _clr(e):
                for sm in allsems:
                    e.sem_clear(sm)
            block0.sync(_clr)
        with nc.Block() as block:
            P.emit(nc, block, csem, rings)
    return nc, P


def _in_maps(inputs):
    global _CONSTS
    if _CONSTS is None:
        _CONSTS = _host_consts()
    f = lambda a: np.ascontiguousarray(np.asarray(a, dtype=np.float32))
    shared = {
        "g_mix": f(inputs["g_mix"][0]), "w_in": f(inputs["w_in"][0]),
        "lam_re": f(inputs["ssm_lam_re"][0]), "lam_im": f(inputs["ssm_lam_im"][0]), "log_dt": f(inputs["ssm_log_dt"][0]),
        "b_re": f(inputs["ssm_b_re"][0]), "b_im": f(inputs["ssm_b_im"][0]), "c_re": f(inputs["ssm_c_re"][0]), "c_im": f(inputs["ssm_c_im"][0]),
        "ssm_d": f(inputs["ssm_d"][0]), "w_glu": f(inputs["w_glu"][0]), "b_glu": f(inputs["b_glu"][0]),
        "w_fourier": f(inputs["w_fourier"][0]), "w_out": f(inputs["w_out"][0]), "g_ffn": f(inputs["g_ffn"][0]),
        "w_up": f(inputs["w_up"][0]), "conv_w": f(inputs["conv_w"][0]), "conv_b": f(inputs["conv_b"][0]),
        "w_down": f(inputs["w_down"][0]), "g_final": f(inputs["g_final"]),
    }
    c = _CONSTS
    shared.update({"identb": c["identb"], "identf": c["identf"], "cdft": c["cdft"], "sdft": c["sdft"],
                   "dftT": c["dftT"].reshape(16, 128, 2 * 16 * 128), "msk": c["msk"], "bdmask": c["bdmask"]})
    xs = f(inputs["x"])
    maps = []
    for c_ in range(NCORES):
        m = dict(shared)
        m["x"] = xs[2 * c_:2 * c_ + 2]
        maps.append(m)
    return maps


def kernel(**inputs):
    nc, _ = build()
    res = run_bass_kernel_spmd(nc, _in_maps(inputs), core_ids=list(range(NCORES)))
    return np.concatenate([np.asarray(r["out"], dtype=np.float32) for r in res.results], axis=0)
```

```python
import math
from contextlib import ExitStack

import numpy as np
import ml_dtypes
import concourse.bass as bass
import concourse.mybir as mybir
from concourse.bass_utils import run_bass_kernel_spmd

F32 = mybir.dt.float32
BF16 = mybir.dt.bfloat16
AF = mybir.ActivationFunctionType
ALU = mybir.AluOpType

NCORES = 8
L = 2048
DM = 1024
NTT = 16
DFF = 2816
NFT = 22
TC = 16
NCH = L // TC
EPS = 1e-6
FFN_PARTS = [4, 4, 4, 4, 3, 3]
MAGIC = 12582912.0


class _Op:
    __slots__ = ("eng", "fn", "deps", "dma", "idx", "sem", "val", "prev", "has_dep", "phase")


class Prog:
    ENGS = ("pe", "act", "dve", "pool", "sp")

    def __init__(self):
        self.ops = []
        self.lastw = {}
        self.readers = {}
        self.bufbar = {}
        self.phase = "init"
        self.scopes = False

    def op(self, eng, fn, reads=(), writes=(), dma=False):
        psr = [r for r in reads if r[0] == "ps"]
        if psr:
            reads = [r for r in reads if r[0] != "ps"]
            writes = list(writes) + [r for r in psr if r not in writes]
        i = len(self.ops)
        deps = set()
        for r in reads:
            w = self.lastw.get(r)
            if w is None:
                w = self.bufbar.get(r[0])
            if w is not None:
                deps.add(w)
        for w_ in writes:
            lw = self.lastw.get(w_)
            if lw is None:
                lw = self.bufbar.get(w_[0])
            if lw is not None:
                deps.add(lw)
            deps.update(self.readers.get(w_, ()))
        o = _Op()
        o.eng, o.fn, o.deps, o.dma, o.idx = eng, fn, deps, dma, i
        o.phase = self.phase
        self.ops.append(o)
        for r in reads:
            self.readers.setdefault(r, []).append(i)
        for w_ in writes:
            self.lastw[w_] = i
            self.readers[w_] = []
        return i

    def barrier(self, eng, fn, names):
        names = set(names)
        deps = set()
        for r in list(self.lastw.keys()):
            if r[0] in names:
                deps.add(self.lastw.pop(r))
        for r in list(self.readers.keys()):
            if r[0] in names:
                deps.update(self.readers.pop(r))
        for n in names:
            if n in self.bufbar:
                deps.add(self.bufbar[n])
        i = len(self.ops)
        o = _Op()
        o.eng, o.fn, o.deps, o.dma, o.idx = eng, fn, deps, False, i
        o.phase = self.phase
        self.ops.append(o)
        for n in names:
            self.bufbar[n] = i
        return i

    def emit(self, nc, block, csem, rings):
        ops = self.ops
        for o in ops:
            o.has_dep = False
        for o in ops:
            for d in o.deps:
                ops[d].has_dep = True
        cnt = {e: 0 for e in self.ENGS}
        dcnt = {e: 0 for e in self.ENGS}
        for o in ops:
            o.prev = None
            if o.dma:
                ring = rings[o.eng]
                j = dcnt[o.eng]
                dcnt[o.eng] += 1
                R = len(ring)
                o.sem = ring[j % R]
                o.val = 16 * (j // R + 1)
                if j >= R:
                    o.prev = (ring[j % R], 16 * (j // R))
            elif o.has_dep:
                cnt[o.eng] += 1
                o.sem = csem[o.eng]
                o.val = cnt[o.eng]
        self.counts = (cnt, dcnt)

        def run(engname, e):
            known = {}
            cur = [None]
            cur_id = [None]
            for o in ops:
                if o.eng != engname:
                    continue
                if self.scopes and o.phase != cur[0]:
                    if cur[0] is not None:
                        nc.leave_named_scope(cur[0], cur_id[0], False)
                    cur_id[0] = nc.enter_named_scope(o.phase, False)[0]
                    cur[0] = o.phase
                waits = {}
                for d in o.deps:
                    p = ops[d]
                    if engname == "pe" and p.eng == "pe" and not p.dma:
                        continue
                    k = id(p.sem)
                    if k not in waits or waits[k][1] < p.val:
                        waits[k] = (p.sem, p.val)
                if o.prev is not None:
                    k = id(o.prev[0])
                    if k not in waits or waits[k][1] < o.prev[1]:
                        waits[k] = o.prev
                wl = []
                for k, (sm, v) in waits.items():
                    if known.get(k, 0) >= v:
                        continue
                    known[k] = v
                    wl.append((sm, v))
                if o.fn is None:
                    for sm, v in wl:
                        e.wait_ge(sm, v)
                    continue
                for sm, v in wl[:-1]:
                    e.wait_ge(sm, v)
                ins = o.fn(e)
                if wl:
                    ins._wait_ge(wl[-1][0], wl[-1][1])
                if o.dma:
                    ins.then_inc(o.sem, 16)
                elif o.has_dep:
                    ins.then_inc(o.sem, 1)
            if self.scopes and cur[0] is not None:
                nc.leave_named_scope(cur[0], cur_id[0], False)

        block.tensor(lambda e: run("pe", e))
        block.scalar(lambda e: run("act", e))
        block.vector(lambda e: run("dve", e))
        block.gpsimd(lambda e: run("pool", e))
        block.sync(lambda e: run("sp", e))


def _host_consts():
    c = {}
    c["identb"] = np.eye(128, dtype=np.float32).astype(ml_dtypes.bfloat16)
    c["identf"] = np.eye(128, dtype=np.float32)
    n = np.arange(128)
    ang = 2 * np.pi * np.outer(n, n) / 128.0
    c["cdft"] = (np.cos(ang) / math.sqrt(128.0)).astype(np.float32)
    c["sdft"] = (np.sin(ang) / math.sqrt(128.0)).astype(np.float32)
    l = np.arange(L)
    lk = np.outer(l, l) % L
    angL = 2 * np.pi * lk / L
    CL = (np.cos(angL) / math.sqrt(L)).astype(np.float32)
    SL = (np.sin(angL) / math.sqrt(L)).astype(np.float32)

    def lay(M):
        return M.reshape(16, 128, 16, 128).transpose(2, 1, 0, 3)
    c["dftT"] = np.ascontiguousarray(np.stack([lay(CL), lay(SL)], axis=2)).astype(ml_dtypes.bfloat16)
    p = np.arange(128)
    msk = np.zeros((128, 8), np.float32)
    msk[:, 0] = (p < 64)
    msk[:, 1] = (p >= 64)
    msk[:, 2] = ((p // 16) % 2 == 0)
    msk[:, 3] = ((p // 16) % 2 == 1)
    msk[:, 4] = -1.0 * (p < 64)
    msk[:, 5] = -1.0 * (p >= 64)
    c["msk"] = msk
    bd = (p[:, None] // 16 == p[None, :] // 16).astype(np.float32)
    c["bdmask"] = bd
    return c


_CONSTS = None


def build(upto="all", debug=(), scopes=False):
    nc = bass.Bass("TRN2", target_bir_lowering=False)
    P = Prog()
    P.scopes = scopes
    dbg_out = {}

    def din(name, shape, dt=F32):
        return nc.dram_tensor(name, list(shape), dt, kind="ExternalInput").ap()

    x = din("x", [2, L, DM])
    g_mix = din("g_mix", [DM]); w_in = din("w_in", [DM, DM])
    lam_re = din("lam_re", [2, 32, 64]); lam_im = din("lam_im", [2, 32, 64]); log_dt = din("log_dt", [2, 32])
    b_re = din("b_re", [2, 32, 64, 16]); b_im = din("b_im", [2, 32, 64, 16])
    c_re = din("c_re", [2, 32, 16, 64]); c_im = din("c_im", [2, 32, 16, 64])
    ssm_d = din("ssm_d", [512]); w_glu = din("w_glu", [512, 512]); b_glu = din("b_glu", [512])
    w_fourier = din("w_fourier", [4, 128, 128]); w_out = din("w_out", [DM, DM]); g_ffn = din("g_ffn", [DM])
    w_up = din("w_up", [DM, 2 * DFF]); conv_w = din("conv_w", [3, 2 * DFF]); conv_b = din("conv_b", [2 * DFF])
    w_down = din("w_down", [DFF, DM]); g_final = din("g_final", [DM])
    identb_d = din("identb", [128, 128], BF16); identf_d = din("identf", [128, 128])
    cdft_d = din("cdft", [128, 128]); sdft_d = din("sdft", [128, 128])
    dftT_d = din("dftT", [16, 128, 2 * 16 * 128], BF16)
    msk_d = din("msk", [128, 8]); bdmask_d = din("bdmask", [128, 128])
    out = nc.dram_tensor("out", [2, L, DM], F32, kind="ExternalOutput").ap()
    lreim_s = nc.dram_tensor("lreim_s", [4, 2, 128, 16 * 2 * 128], BF16).ap()
    pcp_s = nc.dram_tensor("pcp_s", [4, 2, 128, 4 * 2 * 16 * 32], BF16).ap()
    lag_s = nc.dram_tensor("lag_s", [4, 128, 31 * 128], BF16).ap()

    dbg_shapes = {"u": ([128, 4 * L], BF16), "vT": ([128, 4 * L], BF16), "ycat": ([128, 8 * L], BF16),
                  "h": ([128, NTT * DM], F32), "XT": ([128, 8 * L], BF16), "win": ([128, 8 * DM], BF16), "X": ([128, 2 * 2 * 16 * NCH], BF16)}
    for nm in debug:
        shp, dtp = dbg_shapes[nm]
        dbg_out[nm] = nc.dram_tensor("dbg_" + nm, shp, dtp, kind="ExternalOutput").ap()

    es = ExitStack()
    with es:
        def sb(name, shape, dt):
            return es.enter_context(nc.sbuf_tensor(name, list(shape), dt))

        KB = 1024
        ARENA_B = 144 * KB
        arena = sb("arena", [128, ARENA_B // 2], BF16)

        def av(off_b, size_b, dt=BF16):
            v = arena[:, off_b // 2:(off_b + size_b) // 2]
            return v if dt == BF16 else v.bitcast(dt)

        R0, R1, R2 = 0, 64 * KB, 96 * KB
        h_v = av(R0, 64 * KB, F32).rearrange("p (t d) -> p t d", t=NTT)
        u_v = av(R0, 16 * KB).rearrange("p (f t) -> p f t", f=4)
        vT_v = av(R0 + 16 * KB, 16 * KB).rearrange("p (f t) -> p f t", f=4)
        AB_v = av(R0 + 32 * KB, 32 * KB).rearrange("p (l a c) -> p l a c", l=16, a=2)
        ycat_v = av(R1, 32 * KB).rearrange("p (f t) -> p f t", f=8)
        X_v = av(R2, 16 * KB)
        X5 = X_v.rearrange("p (r d q c) -> p r d q c", r=2, d=2, q=16)
        s5s_v = av(R2 + 16 * KB, 16 * KB).rearrange("p (b k r c) -> p b k r c", b=2, k=16, r=2)
        WOFF = R2 + 32 * KB
        win_v = av(WOFF, 16 * KB).rearrange("p (k c) -> p k c", k=8)
        dft_v = av(WOFF, 16 * KB).rearrange("p (b a l c) -> p b a l c", b=2, a=2, l=16)
        wout_v = av(R2, 16 * KB).rearrange("p (k c) -> p k c", k=8)
        lagb_v = av(R0 + 32 * KB, 16 * KB).rearrange("p (b l c) -> p b l c", b=2, l=32)
        pcpb_v = av(R0 + 48 * KB, 16 * KB).rearrange("p (b q r k c) -> p b q r k c", b=2, q=4, r=2, k=16)
        FB = R1
        maxnp = max(FFN_PARTS)
        wup_b = maxnp * 2 * 128 * 8 * 2
        wup_v = [av(FB + i * wup_b, wup_b).rearrange("p (k g c) -> p k g c", k=8, g=2) for i in range(2)]
        o1 = FB + 2 * wup_b
        wdn_b = maxnp * DM * 2
        wdn_v = [av(o1 + i * wdn_b, wdn_b).rearrange("p (t d) -> p t d", t=maxnp) for i in range(2)]
        o2 = o1 + 2 * wdn_b
        act_b = maxnp * L * 2
        act_v = [av(o2 + i * act_b, act_b).rearrange("p (t n) -> p t n", t=maxnp) for i in range(2)]
        o3 = o2 + 2 * act_b
        assert o3 <= ARENA_B, (o3, ARENA_B)
        FFN_NAMES = ["wup", "wdn", "act", "acc", "gact", "xst", "xsb"]
        SU0 = 0
        su_off = [SU0]

        def su(size_b, dt=F32):
            o = su_off[0]
            su_off[0] += size_b
            assert su_off[0] <= WOFF, su_off[0]
            return av(o, size_b, dt)

        XT = sb("XT", [128, 8, L], BF16)
        xst = sb("xst", [128, 2, DM], F32)
        xsb = sb("xsb", [128, 3, DM], BF16)
        acc_v = xst[:].rearrange("p a (b n) -> p (a b) n", b=2)
        gact4w = xsb[:].rearrange("p a n -> p (a n)")[:, 0:2048].rearrange("p (b n) -> p b n", b=4)
        gact4 = [gact4w[:, i, :] for i in range(4)]
        identb = sb("identb_sb", [128, 128], BF16)
        identf = sb("identf_sb", [128, 128], F32)
        msk = sb("msk_sb", [128, 8], F32)
        bdmask = sb("bdmask_sb", [128, 128], F32)
        g8 = sb("g8", [128, 2, 8], F32)
        gfin = sb("gfin", [128, DM], F32)
        ss = sb("ss", [128, 3, NTT], F32)
        cwsw = sb("cwsw", [128, 4, 256], BF16)
        bglu4 = sb("bglu4", [128, 4], F32)
        cw3 = sb("cw3", [128, 3, 44], F32)
        cb1 = sb("cb1", [128, 44], F32)
        d4 = sb("d4", [128, 4], F32)
        a16 = sb("a16", [128, 2, 2, 16], F32)
        a16n = sb("a16n", [128, 2, 2, 16], F32)
        scur = sb("scur", [128, 2, 2, 2, 16], F32)
        stmp = sb("stmp", [128, 2, 2, 2, 16], F32)
        yfs = sb("yfs", [128, 2, 512], BF16)
        junk = yfs[:].rearrange("p a n -> p (a n)")
        bar = sb("bar", [128, 64], F32)
        wglu_v = sb("wglu_sb", [128, 4, 512], BF16)
        _xt2 = XT[:].rearrange("p k t -> p (k t)")
        wf = _xt2[:, 0:1024].bitcast(F32).rearrange("p (h c) -> p h c", h=4)
        cs = _xt2[:, 1024:1536].bitcast(F32).rearrange("p (a c) -> p a c", a=2)
        ps = es.enter_context(nc.psum_tensor("ps", [128, 8, 512], F32))

        csem = {e: es.enter_context(nc.semaphore("c_" + e)) for e in ("pe", "act", "dve", "pool")}
        rings = {"sp": [es.enter_context(nc.semaphore(f"dsp{i}")) for i in range(24)],
                 "pool": [es.enter_context(nc.semaphore(f"dpl{i}")) for i in range(12)],
                 "act": [], "pe": [], "dve": []}
        nbar = [0]

        def barrier(names):
            i = nbar[0]
            nbar[0] += 1
            P.barrier("pool", lambda e, i=i: e.memset(bar[:, i:i + 1], 0.0), names)

        def dma(q, out_ap, in_ap, reads, writes, slow=False):
            if slow:
                P.op(q, lambda e: e.dma_start(out=out_ap, in_=in_ap, allow_slow_non_contiguous=True), reads, writes, dma=True)
            else:
                P.op(q, lambda e: e.dma_start(out=out_ap, in_=in_ap), reads, writes, dma=True)

        psb = [ps[:, b, :] for b in range(8)]
        psrot = [0]

        def nextbank():
            b = psrot[0] % 8
            psrot[0] += 1
            return b

        def PSR(b, n=1):
            return [("ps", b + i) for i in range(n)]

        sel_lo, sel_hi, ev_m, od_m, nsel_lo, nsel_hi = (msk[:, i:i + 1] for i in range(6))

        dma("sp", identb[:], identb_d, [], [("identb",)])
        dma("sp", identf[:], identf_d, [], [("identf",)])
        dma("sp", msk[:], msk_d, [], [("msk",)])
        dma("sp", bdmask[:], bdmask_d, [], [("bdmask",)])
        dma("sp", g8[:, 0, :], g_mix.rearrange("(k p) -> p k", p=128), [], [("g8", 0)], slow=True)
        dma("sp", g8[:, 1, :], g_ffn.rearrange("(k p) -> p k", p=128), [], [("g8", 1)], slow=True)
        dma("sp", bglu4[:], b_glu.rearrange("(k p) -> p k", p=128), [], [("bglu4",)], slow=True)
        dma("sp", d4[:], ssm_d.rearrange("(k p) -> p k", p=128), [], [("d4",)], slow=True)
        dma("sp", cb1[:], conv_b.rearrange("(k p) -> p k", p=128), [], [("cb1",)], slow=True)
        for k in range(3):
            dma("sp", cw3[:, k, :], conv_w[k].rearrange("(k p) -> p k", p=128), [], [("cw3", k)], slow=True)
        dma("sp", gfin[:], g_final.partition_broadcast(128), [], [("gfin",)])
        for kt in range(4):
            dma("pool", wglu_v[:, kt, :], w_glu[kt * 128:(kt + 1) * 128, :], [], [("wglu", kt)])
        dma("sp", wf[:], w_fourier.rearrange("h c j -> c h j"), [], [("SU", "wf")])
        dma("sp", cs[:, 0, :], cdft_d, [], [("SU", "cs", 0)])
        dma("sp", cs[:, 1, :], sdft_d, [], [("SU", "cs", 1)])

        for hh in range(4):
            b = nextbank()
            for a in range(2):
                P.op("pe", lambda e, hh=hh, a=a, b=b: e.matmul(psb[b][:, a * 128:(a + 1) * 128], cs[:, a, :], wf[:, hh, :],
                                                             start=True, stop=True),
                     [("SU", "cs", a), ("SU", "wf")], [("ps", b)])
            P.op("act", lambda e, hh=hh, b=b: e.activation(out=cwsw[:, hh, 0:128], in_=psb[b][:, 0:128], func=AF.Copy),
                 [("ps", b)], [("cwsw", hh, 0)])
            P.op("act", lambda e, hh=hh, b=b: e.activation(out=cwsw[:, hh, 128:256], in_=psb[b][:, 128:256], func=AF.Copy,
                                                           scale=-1.0),
                 [("ps", b)], [("cwsw", hh, 1)])

        def s5_setup():
            S = lambda *a: ("SU",) + a
            cnt = [0]

            def tile(shape, dt=F32):
                n = 1
                for s_ in shape:
                    n *= s_
                v = su(n * (4 if dt == F32 else 2), dt)
                cnt[0] += 1
                if len(shape) == 1:
                    return v, S(cnt[0])
                names = "abcde"[:len(shape)]
                pat = "p (" + " ".join(names) + ") -> p " + " ".join(names)
                return v.rearrange(pat, **{nm: s_ for nm, s_ in zip(names, shape)}), S(cnt[0])

            import os as _os2
            LEVEL = int(_os2.environ.get("K_LEVEL", "9"))

            def tt(eng, o, a, b_, op, rs, ws):
                P.op(eng, lambda e: e.tensor_tensor(out=o, in0=a, in1=b_, op=op), rs, ws)

            def ts(eng, o, a, s1, op0, rs, ws, s2=None, op1=None):
                if op1 is None:
                    P.op(eng, lambda e: e.tensor_scalar(out=o, in0=a, scalar1=s1, scalar2=None, op0=op0), rs, ws)
                else:
                    P.op(eng, lambda e: e.tensor_scalar(out=o, in0=a, scalar1=s1, scalar2=s2, op0=op0, op1=op1), rs, ws)

            def stt(o, a, sc, b_, op0, op1, rs, ws):
                P.op("dve", lambda e: e.scalar_tensor_tensor(out=o, in0=a, scalar=sc, in1=b_, op0=op0, op1=op1), rs, ws)

            def act(o, a, func, rs, ws, **kw):
                P.op("act", lambda e: e.activation(out=o, in_=a, func=func, **kw), rs, ws)

            M = ("msk",)
            LIN, rLIN = tile([2, 128])
            LDT, rLDT = tile([64])
            for t_, src in ((0, lam_re), (1, lam_im)):
                for hf in range(2):
                    dma("sp", LIN[0:64, t_, hf * 64:(hf + 1) * 64], src.rearrange("d g p -> (d g) p"), [], [rLIN + (t_, hf)])
            dma("sp", LDT, log_dt.rearrange("d g -> (d g)").partition_broadcast(128), [], [rLDT])
            BR2, rBR = tile([2, 32, 16]); BI2, rBI = tile([2, 32, 16])
            for (T_, r_, src) in ((BR2, rBR, b_re), (BI2, rBI, b_im)):
                for hf in range(2):
                    dma("sp", T_[hf * 64:(hf + 1) * 64], src.rearrange("d g p h -> p d g h"), [], [r_ + (hf,)])
            CIN, rCIN = tile([2, 8, 128])
            for t_, src in ((0, c_re), (1, c_im)):
                for hf in range(2):
                    dma("sp", CIN[:, t_, :, hf * 64:(hf + 1) * 64], src.rearrange("d g h p -> (d g h) p").rearrange("(r q) p -> q r p", q=128),
                        [], [rCIN + (t_, hf)])
            CR2, rCR = tile([1024]); CI2, rCI = tile([1024])
            for t_, (T_, r_) in enumerate(((CR2, rCR), (CI2, rCI))):
                for r4 in range(2):
                    b = nextbank()
                    for i in range(4):
                        r = r4 * 4 + i
                        P.op("pe", lambda e, b=b, i=i, t_=t_, r=r: e.transpose(psb[b][:, i * 128:(i + 1) * 128], CIN[:, t_, r, :], identf[:]),
                             [rCIN + (t_, 0), rCIN + (t_, 1), ("identf",)], [("ps", b)])
                    act(T_[:, r4 * 512:(r4 + 1) * 512], psb[b], AF.Copy, [("ps", b)], [r_ + (r4,)])
            LAM, rLAM = tile([2, 64])
            b = nextbank()
            for t_ in range(2):
                P.op("pe", lambda e, b=b, t_=t_: e.transpose(psb[b][:, t_ * 64:(t_ + 1) * 64], LIN[0:64, t_, :], identf[0:64, 0:64]),
                     [rLIN + (t_, 0), rLIN + (t_, 1), ("identf",)], [("ps", b)])
            act(LAM.rearrange("p a b -> p (a b)"), psb[b][:, 0:128], AF.Copy, [("ps", b)], [rLAM])
            LAMR, LAMI = LAM[:, 0, :], LAM[:, 1, :]
            if LEVEL <= 1:
                return
            sm, rsm = tile([20, 64])
            k_ = [0]

            def new():
                i = k_[0]
                k_[0] += 1
                return sm[:, i, :], rsm + (i,)
            dt_, rdt = new(); lr, rlr = new(); li, rli = new(); ea, rea = new(); tA, rtA = new(); nn, rnn = new()
            ri, rri = new(); s1, rs1 = new(); ab, rab = new(); c1, rc1 = new(); den, rden = new(); t0, rt0 = new()
            wr, rwr = new(); qr, rqr = new(); qi, rqi = new(); t1_, rt1_ = new(); t2_, rt2_ = new()
            act(dt_, LDT, AF.Exp, [rLDT], [rdt])
            tt("dve", lr, LAMR, dt_, ALU.mult, [rLAM, rdt], [rlr])
            tt("dve", li, LAMI, dt_, ALU.mult, [rLAM, rdt], [rli])
            act(ea, lr, AF.Exp, [rlr], [rea])
            ts("dve", tA, li, 1.0 / (2 * math.pi), ALU.mult, [rli], [rtA], s2=MAGIC, op1=ALU.add)
            ts("dve", nn, tA, -MAGIC, ALU.add, [rtA], [rnn])
            stt(ri, nn, -2 * math.pi, li, ALU.mult, ALU.add, [rnn, rli], [rri])
            act(s1, ri, AF.Sin, [rri], [rs1])
            act(ab, ri, AF.Abs, [rri], [rab])
            act(c1, ab, AF.Sin, [rab], [rc1], scale=-1.0, bias=math.pi / 2)
            ER, rER = tile([17, 64]); EI, rEI = tile([17, 64])
            P.op("dve", lambda e: e.memset(ER[:, 0, :], 1.0), [], [rER + (0,)])
            P.op("dve", lambda e: e.memset(EI[:, 0, :], 0.0), [], [rEI + (0,)])
            tt("dve", ER[:, 1, :], ea, c1, ALU.mult, [rea, rc1], [rER + (1,)])
            tt("dve", EI[:, 1, :], ea, s1, ALU.mult, [rea, rs1], [rEI + (1,)])
            ta, rta = tile([8, 64]); tb, rtb = tile([8, 64])
            n = 1
            while n < 16:
                lo, hi = 1, n + 1
                src_r = [rER + (i,) for i in range(lo, hi)] + [rEI + (i,) for i in range(lo, hi)]
                ern = ER[:, n:n + 1, :].broadcast_to([128, n, 64]) if n > 1 else ER[:, n:n + 1, :]
                ein = EI[:, n:n + 1, :].broadcast_to([128, n, 64]) if n > 1 else EI[:, n:n + 1, :]
                tt("dve", ta[:, 0:n, :], ER[:, lo:hi, :], ern, ALU.mult, src_r, [rta])
                tt("dve", tb[:, 0:n, :], EI[:, lo:hi, :], ein, ALU.mult, src_r, [rtb])
                tt("dve", ER[:, n + 1:2 * n + 1, :], ta[:, 0:n, :], tb[:, 0:n, :], ALU.subtract, [rta, rtb],
                   [rER + (i,) for i in range(n + 1, 2 * n + 1)])
                tt("dve", ta[:, 0:n, :], ER[:, lo:hi, :], ein, ALU.mult, src_r, [rta])
                tt("dve", tb[:, 0:n, :], EI[:, lo:hi, :], ern, ALU.mult, src_r, [rtb])
                tt("dve", EI[:, n + 1:2 * n + 1, :], ta[:, 0:n, :], tb[:, 0:n, :], ALU.add, [rta, rtb],
                   [rEI + (i,) for i in range(n + 1, 2 * n + 1)])
                n *= 2
            rERall = [rER + (i,) for i in range(17)]
            rEIall = [rEI + (i,) for i in range(17)]
            tt("dve", den, LAMR, LAMR, ALU.mult, [rLAM], [rden])
            tt("dve", t0, LAMI, LAMI, ALU.mult, [rLAM], [rt0])
            tt("dve", den, den, t0, ALU.add, [rden, rt0], [rden])
            P.op("dve", lambda e: e.reciprocal(out=den, in_=den), [rden], [rden])
            ts("dve", wr, ER[:, 1, :], -1.0, ALU.add, [rER + (1,)], [rwr])
            tt("dve", t1_, wr, LAMR, ALU.mult, [rwr, rLAM], [rt1_])
            tt("dve", t2_, EI[:, 1, :], LAMI, ALU.mult, [rEI + (1,), rLAM], [rt2_])
            tt("dve", t1_, t1_, t2_, ALU.add, [rt1_, rt2_], [rt1_])
            tt("dve", qr, t1_, den, ALU.mult, [rt1_, rden], [rqr])
            tt("dve", t1_, EI[:, 1, :], LAMR, ALU.mult, [rEI + (1,), rLAM], [rt1_])
            tt("dve", t2_, wr, LAMI, ALU.mult, [rwr, rLAM], [rt2_])
            tt("dve", t1_, t1_, t2_, ALU.subtract, [rt1_, rt2_], [rt1_])
            tt("dve", qi, t1_, den, ALU.mult, [rt1_, rden], [rqi])
            GR, rGR = tile([16, 64]); GI, rGI = tile([16, 64]); tg, rtg = tile([16, 64]); th, rth = tile([16, 64])
            qrb = qr.unsqueeze(1).broadcast_to([128, 16, 64]); qib = qi.unsqueeze(1).broadcast_to([128, 16, 64])
            tt("dve", tg, ER[:, 0:16, :], qrb, ALU.mult, rERall + [rqr], [rtg])
            tt("dve", th, EI[:, 0:16, :], qib, ALU.mult, rEIall + [rqi], [rth])
            tt("dve", GR, tg, th, ALU.subtract, [rtg, rth], [rGR])
            tt("dve", tg, ER[:, 0:16, :], qib, ALU.mult, rERall + [rqi], [rtg])
            tt("dve", th, EI[:, 0:16, :], qrb, ALU.mult, rEIall + [rqr], [rth])
            tt("dve", GI, tg, th, ALU.add, [rtg, rth], [rGI])
            AL, rAL = tile([16, 64]); BE, rBE = tile([16, 64])
            ts("dve", AL, GR, sel_lo, ALU.mult, [rGR, M], [rAL])
            stt(AL, GI, sel_hi, AL, ALU.mult, ALU.add, [rGI, rAL, M], [rAL])
            ts("dve", BE, GR, sel_hi, ALU.mult, [rGR, M], [rBE])
            stt(BE, GI, nsel_lo, BE, ALU.mult, ALU.add, [rGI, rBE, M], [rBE])
            for i_, (T_, r_) in enumerate(((ER, rER), (EI, rEI))):
                e16 = T_[:, 16, :].rearrange("p (d q two) -> p d q two", d=2, two=2)
                ts("dve", a16[:, i_], e16[:, :, :, 0], sel_lo, ALU.mult, [r_ + (16,), M], [("a16", i_)])
                stt(a16[:, i_], e16[:, :, :, 1], sel_hi, a16[:, i_], ALU.mult, ALU.add, [r_ + (16,), ("a16", i_), M], [("a16", i_)])
            ts("dve", a16n[:, 0], a16[:, 1], -1.0, ALU.mult, [("a16", 1)], [("a16n", 0)])
            ts("dve", a16n[:, 1], a16[:, 1], 1.0, ALU.mult, [("a16", 1)], [("a16n", 1)])
            PC0, rPC0 = tile([1024])
            rCRall = [rCR + (0,), rCR + (1,)]; rCIall = [rCI + (0,), rCI + (1,)]
            ts("dve", PC0, CR2, sel_lo, ALU.mult, rCRall + [M], [rPC0])
            stt(PC0, CI2, nsel_hi, PC0, ALU.mult, ALU.add, rCIall + [rPC0, M], [rPC0])
            if LEVEL <= 2:
                return
            L0, rL0 = tile([4, 128])
            AL4 = AL.rearrange("p k (d g) -> p k d g", d=2); BE4 = BE.rearrange("p k (d g) -> p k d g", d=2)
            ER4 = ER.rearrange("p k (d g) -> p k d g", d=2); EI4 = EI.rearrange("p k (d g) -> p k d g", d=2)
            CR4 = CR2.rearrange("p (d g h) -> p d g h", d=2, g=32); CI4 = CI2.rearrange("p (d g h) -> p d g h", d=2, g=32)
            t1, rt1 = tile([16, 8, 16]); t2, rt2 = tile([16, 8, 16]); t3, rt3 = tile([16, 8, 16])
            t4, rt4 = tile([16, 8, 16])
            t5, rt5 = CIN.rearrange("p a r c -> p (a r c)").rearrange("p (k g h) -> p k g h", k=16, g=8), S("t5alias")
            LGS, rLGS = tile([1, 15, 128], BF16)
            LRS, rLRS = tile([1, 16, 2, 128], BF16)
            PCS, rPCS = tile([1, 4, 2, 16, 32], BF16)
            L0B, rL0B = tile([4, 128], BF16)
            tl0, rtl0 = tile([128])
            it = 0
            for d in range(2):
                for ft in range(4):
                    sl = 0
                    it += 1
                    gs = slice(ft * 8, ft * 8 + 8)
                    bk = lambda v: v.unsqueeze(3).broadcast_to([128, 16, 8, 16])
                    bh = lambda v: v.unsqueeze(1).broadcast_to([128, 16, 8, 16])
                    tt("dve", t1, bk(AL4[:, :, d, gs]), bh(BR2[:, d, gs, :]), ALU.mult, [rAL, rBR + (0,), rBR + (1,)], [rt1])
                    tt("dve", t2, bk(BE4[:, :, d, gs]), bh(BI2[:, d, gs, :]), ALU.mult, [rBE, rBI + (0,), rBI + (1,)], [rt2])
                    tt("dve", t1, t1, t2, ALU.add, [rt1, rt2], [rt1])
                    t1f = t1.rearrange("p k g h -> p k (g h)")
                    for t4_ in range(4):
                        b = nextbank()
                        for i in range(4):
                            tau = t4_ * 4 + i
                            P.op("pe", lambda e, b=b, i=i, tau=tau, d=d, ft=ft, t1f=t1f: e.matmul(
                                psb[b][:, i * 128:(i + 1) * 128], t1f[:, tau, :], PC0[:, d * 512 + ft * 128:d * 512 + (ft + 1) * 128],
                                start=True, stop=True), [rt1, rPC0], [("ps", b)])
                        for i in range(4):
                            tau = t4_ * 4 + i
                            if tau == 0:
                                if d == 0:
                                    tt("dve", L0[:, ft, :], psb[b][:, 0:128], bdmask[:], ALU.mult, [("ps", b), ("bdmask",)], [rL0 + (ft,)])
                                else:
                                    tt("dve", tl0, psb[b][:, 0:128], bdmask[:], ALU.mult, [("ps", b), ("bdmask",)], [rtl0])
                                    tt("dve", L0[:, ft, :], L0[:, ft, :], tl0, ALU.add, [rL0 + (ft,), rtl0], [rL0 + (ft,)])
                                    stt(L0[:, ft, :], identf[:], d4[:, ft:ft + 1], L0[:, ft, :], ALU.mult, ALU.add,
                                        [("identf",), ("d4",), rL0 + (ft,)], [rL0 + (ft,)])
                                    ts("dve", L0B[:, ft, :], L0[:, ft, :], 1.0, ALU.mult, [rL0 + (ft,)], [rL0B + (ft,)])
                                    dma("sp", lag_s[ft][:, 0:128], L0B[:, ft, :], [rL0B + (ft,)], [("lag_s", ft, 0)])
                            else:
                                tt("dve", LGS[:, sl, tau - 1, :], psb[b][:, i * 128:(i + 1) * 128],
                                   bdmask[:], ALU.mult, [("ps", b), ("bdmask",)], [rLGS + (sl, tau)])
                    base = (1 + 15 * d) * 128
                    dma("sp", lag_s[ft][:, base:base + 15 * 128], LGS[:, sl].rearrange("p l c -> p (l c)"),
                        [rLGS + (sl, tau) for tau in range(1, 16)], [("lag_s", ft, 1 + d)])
                    if LEVEL <= 3:
                        continue
                    for t4_ in range(4):
                        b = nextbank()
                        for i in range(4):
                            k = t4_ * 4 + i
                            P.op("pe", lambda e, b=b, i=i, k=k, t1f=t1f: e.transpose(psb[b][:, i * 128:(i + 1) * 128], t1f[:, k, :], identf[:]),
                                 [rt1, ("identf",)], [("ps", b)])
                        src = psb[b].rearrange("p (k r c) -> p k r c", k=4, r=2)
                        dst = LRS[:, sl, t4_ * 4:t4_ * 4 + 4, :, :]
                        if True:
                            P.op("act", lambda e, src=src, dst=dst: e.activation(out=dst[:, :, :, 0:64], in_=src, func=AF.Copy, scale=ev_m),
                                 [("ps", b), M], [rLRS + (sl, t4_, 0)])
                        else:
                            ts("dve", dst[:, :, :, 0:64], src, ev_m, ALU.mult, [("ps", b), M], [rLRS + (sl, t4_, 0)])
                        ts("dve", dst[:, :, :, 64:128], src, od_m, ALU.mult, [("ps", b), M], [rLRS + (sl, t4_, 1)])
                    dma("sp", lreim_s[ft, d], LRS[:, sl].rearrange("p k r c -> p (k r c)"),
                        [rLRS + (sl, a_, b_) for a_ in range(4) for b_ in range(2)], [("lreim_s", ft, d)])
                    if LEVEL <= 4:
                        continue
                    crb = bh(CR4[:, d, gs, :]); cib = bh(CI4[:, d, gs, :])
                    erb = bk(ER4[:, 1:17, d, gs]); eib = bk(EI4[:, 1:17, d, gs])
                    tt("pool", t2, crb, erb, ALU.mult, rCRall + rERall, [rt2])
                    tt("pool", t3, cib, eib, ALU.mult, rCIall + rEIall, [rt3])
                    tt("pool", t2, t2, t3, ALU.subtract, [rt2, rt3], [rt2])
                    tt("dve", t4, crb, eib, ALU.mult, rCRall + rEIall, [rt4])
                    tt("dve", t5, cib, erb, ALU.mult, rCIall + rERall, [rt5])
                    tt("pool", t4, t4, t5, ALU.add, [rt4, rt5], [rt4])
                    for (eng, T_, r_, reim, sels) in (("pool", t2, rt2, 0, (sel_lo, sel_hi)), ("dve", t4, rt4, 1, (nsel_lo, nsel_hi))):
                        for gp in range(2):
                            src = T_.rearrange("p k (q two) h -> p q k two h", two=2)[:, :, :, gp, :]
                            dst = PCS[:, sl, :, reim, :, gp * 16:(gp + 1) * 16]
                            P.op(eng, lambda e, src=src, dst=dst, sc=sels[gp]: e.tensor_scalar(out=dst, in0=src, scalar1=sc, scalar2=0.0,
                                                                                               op0=ALU.mult, op1=ALU.add),
                                 [r_, M], [rPCS + (sl, reim, gp)])
                    dma("sp", pcp_s[ft, d], PCS[:, sl].rearrange("p q r k c -> p (q r k c)"),
                        [rPCS + (sl, a_, b_) for a_ in range(2) for b_ in range(2)], [("pcp_s", ft, d)])

        def norm_stage(src_ap, tt_, ni, src_res, scale_out=True):
            P.op("act", lambda e: e.activation(out=junk, in_=src_ap, func=AF.Square, accum_out=ss[:, ni, tt_:tt_ + 1]),
                 src_res, [("yfs", 0), ("yfs", 1), ("ss", ni, tt_)])
            P.op("act", lambda e: e.activation(out=ss[:, ni, tt_:tt_ + 1], in_=ss[:, ni, tt_:tt_ + 1], func=AF.Sqrt,
                                               scale=1.0 / DM, bias=EPS),
                 [("ss", ni, tt_)], [("ss", ni, tt_)])
            P.op("dve", lambda e: e.reciprocal(out=ss[:, ni, tt_:tt_ + 1], in_=ss[:, ni, tt_:tt_ + 1]),
                 [("ss", ni, tt_)], [("ss", ni, tt_)])
            if not scale_out:
                return
            sl = tt_ % 3
            P.op("dve", lambda e: e.tensor_scalar(out=xsb[:, sl, :], in0=src_ap, scalar1=ss[:, ni, tt_:tt_ + 1], scalar2=None,
                                                  op0=ALU.mult),
                 src_res + [("ss", ni, tt_)], [("xsb", sl)])

        def transpose_stage(tt_, gi):
            sl = tt_ % 3
            b = nextbank()
            pT = psb[b].bitcast(BF16)[:, 0:1024].rearrange("p (k t) -> p k t", k=8)
            for kt in range(8):
                P.op("pe", lambda e, kt=kt: e.transpose(pT[:, kt, :], xsb[:, sl, kt * 128:(kt + 1) * 128], identb[:]),
                     [("xsb", sl), ("identb",)], [("ps", b)])
            P.op("dve", lambda e: e.tensor_tensor(out=XT[:, :, tt_ * 128:(tt_ + 1) * 128], in0=pT,
                                                  in1=g8[:, gi, :].unsqueeze(2).broadcast_to([128, 8, 128]), op=ALU.mult),
                 [("ps", b), ("g8", gi)], [("XT", tt_)])

        uJC = [u_v[:, ft, :].rearrange("p (j c) -> p j c", j=TC) for ft in range(4)]

        def mix_front(s, mode="all"):
            if mode != "a" and s > 0:
                dma("pool", win_v, w_in.rearrange("(k p) c -> p k c", p=128), [], [("win", kt) for kt in range(8)])
            pending = []

            def proj_unit(fc, blk):
                b = nextbank()
                for kt in range(8):
                    P.op("pe", lambda e, fc=fc, kt=kt, b=b, blk=blk: e.matmul(
                        psb[b], win_v[:, kt, fc * 128:(fc + 1) * 128], XT[:, kt, blk * 512:(blk + 1) * 512],
                        start=(kt == 0), stop=(kt == 7)),
                        [("win", kt)] + [("XT", t4) for t4 in range(blk * 4, blk * 4 + 4)], [("ps", b)])
                if fc < 4:
                    dst = uJC[fc][:, :, blk * 32:(blk + 1) * 32]
                    src = psb[b].rearrange("p (c j) -> p j c", j=TC)
                else:
                    dst = vT_v[:, fc % 4, blk * 512:(blk + 1) * 512]
                    src = psb[b]
                rn = ("u" if fc < 4 else "vT", fc % 4, blk)
                P.op("act", lambda e, src=src, dst=dst: e.activation(out=dst, in_=src, func=AF.Copy), [("ps", b)], [rn])

            SK = 2
            if mode == "b":
                for blk in range(4):
                    for fc in range(8):
                        proj_unit(fc, blk)
                return
            for it in range(NTT + SK):
                if it < NTT:
                    sl = it % 2
                    dma("sp", xst[:, sl, :], x[s, it * 128:(it + 1) * 128, :], [], [("xst", sl)])
                    norm_stage(xst[:, sl, :], it, 0, [("xst", sl)])
                t2 = it - SK
                if t2 >= 0:
                    transpose_stage(t2, 0)
                    if t2 % 4 == 3 and mode == "all":
                        pending.extend((fc, t2 // 4) for fc in range(8))
                for _ in range(2):
                    if pending:
                        proj_unit(*pending.pop(0))
            while pending:
                proj_unit(*pending.pop(0))

        def fourier(s):
            for lt in range(16):
                b0 = (nextbank() // 2) * 2
                psrot[0] = b0 + 2
                for hh in range(4):
                    bb = b0 + hh // 2
                    P.op("pe", lambda e, hh=hh, bb=bb, lt=lt: e.matmul(
                        ps[:, bb, (hh % 2) * 256:(hh % 2) * 256 + 256], vT_v[:, hh, lt * 128:(lt + 1) * 128], cwsw[:, hh, :],
                        start=True, stop=True),
                        [("vT", hh, lt // 4), ("cwsw", hh, 0), ("cwsw", hh, 1)], [("ps", bb)])
                for bb in (b0, b0 + 1):
                    src = ps[:, bb, :].rearrange("p (h a c) -> p h a c", h=2, a=2)
                    dstv = AB_v[:, lt, :, (bb - b0) * 256:(bb - b0) * 256 + 256].rearrange("p a (h c) -> p h a c", h=2)
                    P.op("act", lambda e, src=src, dstv=dstv: e.activation(out=dstv, in_=src, func=AF.Copy),
                         [("ps", bb)], [("AB", lt, bb - b0)])
            for kt in range(16):
                sl = kt % 2
                dma("sp", dft_v[:, sl].rearrange("p a l c -> p (a l c)"), dftT_d[kt], [], [("dft", sl)])
                b = nextbank()
                n = 0
                for a in range(2):
                    for lt in range(16):
                        P.op("pe", lambda e, a=a, lt=lt, b=b, sl=sl, n=n: e.matmul(
                            psb[b], dft_v[:, sl, a, lt, :], AB_v[:, lt, a, :], start=(n == 0), stop=(n == 31)),
                            [("dft", sl), ("AB", lt, 0), ("AB", lt, 1)], [("ps", b)])
                        n += 1
                P.op("act", lambda e, b=b, sl=sl: e.activation(out=yfs[:, sl, :], in_=psb[b], func=AF.Copy), [("ps", b)], [("yfs", sl)])
                b2 = nextbank()
                pT = psb[b2].bitcast(BF16)[:, 0:512].rearrange("p (f t) -> p f t", f=4)
                for f in range(4):
                    P.op("pe", lambda e, f=f, sl=sl, pT=pT: e.transpose(pT[:, f, :], yfs[:, sl, f * 128:(f + 1) * 128], identb[:]),
                         [("yfs", sl), ("identb",)], [("ps", b2)])
                P.op("act", lambda e, pT=pT, kt=kt: e.activation(out=ycat_v[:, 4:8, kt * 128:(kt + 1) * 128], in_=pT, func=AF.Copy),
                     [("ps", b2)], [("ycat", 4 + f, kt // 4) for f in range(4)])


        def s5_prefetch(s):
            for buf, (ft, d) in enumerate(((0, 0), (0, 1))):
                dma("sp", s5s_v[:, buf].rearrange("p k r c -> p (k r c)"), lreim_s[ft, d], [("lreim_s", ft, d)], [("s5s", buf)])

        def s5_states(s):
            it = 0
            for ft in range(4):
                for d in range(2):
                    buf = it % 2
                    it += 1
                    if it > 2:
                        dma("sp", s5s_v[:, buf].rearrange("p k r c -> p (k r c)"), lreim_s[ft, d], [("lreim_s", ft, d)], [("s5s", buf)])
                    b4 = (nextbank() // 4) * 4
                    psrot[0] = b4 + 4
                    for reim in range(2):
                        for i in range(TC):
                            k = (TC - 1 - i) if d == 0 else i
                            for q in range(4):
                                b = b4 + q
                                P.op("pe", lambda e, b=b, reim=reim, i=i, k=k, q=q, ft=ft, buf=buf: e.matmul(
                                    psb[b][:, reim * 128:(reim + 1) * 128], s5s_v[32 * q:32 * q + 32, buf, k, reim, :],
                                    uJC[ft][32 * q:32 * q + 32, i, :], start=(i == 0), stop=(i == TC - 1), tile_position=(32 * q, 0)),
                                    [("s5s", buf)] + [("u", ft, bl) for bl in range(4)], [("ps", b)])
                    for q in range(4):
                        Q = ft * 4 + q
                        b = b4 + q
                        P.op("act", lambda e, b=b, d=d, Q=Q: e.activation(out=X5[:, :, d, Q, :], in_=psb[b][:, 0:256].rearrange("p (r c) -> p r c", r=2),
                                                                          func=AF.Copy), [("ps", b)], [("XV",), ("XS",)])

        def s5_scan(s):
            P.op("dve", lambda e: e.memset(scur[:, 0], 0.0), [], [("scur", 0)])
            arb = a16[:, 0].unsqueeze(1).broadcast_to([128, 2, 2, 16])
            for st in range(NCH - 1):
                cu, nx = st % 2, (st + 1) % 2
                col = bass.AP(X5.tensor, X5.offset + st, [list(X5.ap[0]), [4096, 2], [2048 + (NCH - 1) - 2 * st, 2], [128, 16]])
                P.op("dve", lambda e, cu=cu: e.tensor_tensor(out=stmp[:, 0], in0=scur[:, cu], in1=arb, op=ALU.mult),
                     [("scur", cu), ("a16", 0)], [("stmp", 0)])
                P.op("dve", lambda e, cu=cu: e.tensor_tensor(out=stmp[:, 1, 0], in0=scur[:, cu, 1], in1=a16n[:, 0], op=ALU.mult),
                     [("scur", cu), ("a16n", 0)], [("stmp", 1, 0)])
                P.op("dve", lambda e, cu=cu: e.tensor_tensor(out=stmp[:, 1, 1], in0=scur[:, cu, 0], in1=a16n[:, 1], op=ALU.mult),
                     [("scur", cu), ("a16n", 1)], [("stmp", 1, 1)])
                P.op("dve", lambda e: e.tensor_tensor(out=stmp[:, 0], in0=stmp[:, 0], in1=stmp[:, 1], op=ALU.add),
                     [("stmp", 0), ("stmp", 1, 0), ("stmp", 1, 1)], [("stmp", 0)])
                P.op("dve", lambda e, nx=nx, col=col: e.tensor_tensor(out=scur[:, nx], in0=stmp[:, 0], in1=col, op=ALU.add),
                     [("stmp", 0), ("XV",)], [("scur", nx)])
                P.op("pool", lambda e, nx=nx, col=col: e.tensor_copy(out=col, in_=scur[:, nx]), [("scur", nx)], [("XS",)])

        def s5_out(s):
            it = 0
            for ft in range(4):
                lb = ft % 2
                dma("sp", lagb_v[:, lb, 0:31, :].rearrange("p l c -> p (l c)"), lag_s[ft],
                    [("lag_s", ft, 0), ("lag_s", ft, 1), ("lag_s", ft, 2)], [("lagb", lb)])
                B0 = (ft % 2) * 4
                Yb = [ps[:, B0 + jb, :].rearrange("p (j c) -> p j c", j=4) for jb in range(4)]
                for jb in range(4):
                    j0 = 4 * jb
                    mm = [(0, j0, j0 + 4, 0)]
                    for tau in range(1, TC):
                        lo, hi = max(tau, j0), j0 + 4
                        if lo < hi:
                            mm.append((tau, lo, hi, -tau))
                    for tau in range(1, TC):
                        lo, hi = j0, min(TC - tau, j0 + 4)
                        if lo < hi:
                            mm.append((TC - 1 + tau, lo, hi, tau))
                    for n, (lg, lo, hi, sh) in enumerate(mm):
                        P.op("pe", lambda e, lg=lg, lo=lo, hi=hi, sh=sh, jb=jb, j0=j0, n=n, lb=lb, ft=ft, Yb=Yb: e.matmul(
                            Yb[jb][:, lo - j0:hi - j0, :], lagb_v[:, lb, lg, :], uJC[ft][:, lo + sh:hi + sh, :], start=(n == 0), stop=False,
                            skip_group_check=True),
                            [("lagb", lb)] + [("u", ft, bl) for bl in range(4)], [("ps", B0 + jb)])
                for d in range(2):
                    pb_ = it % 2
                    it += 1
                    dma("sp", pcpb_v[:, pb_].rearrange("p q r k c -> p (q r k c)"), pcp_s[ft, d], [("pcp_s", ft, d)], [("pcpb", pb_)])
                    for j in range(TC):
                        kk = j if d == 0 else TC - 1 - j
                        for reim in range(2):
                            for q in range(4):
                                Q = ft * 4 + q
                                if d == 0:
                                    rhs = X5[:, reim, 0, Q, 0:NCH - 1]
                                    oc0 = 1
                                else:
                                    rhs = X5[:, reim, 1, Q, 1:NCH]
                                    oc0 = 0
                                last = (d == 1 and j == TC - 1 and reim == 1)
                                oap = ps[32 * q:32 * q + 32, B0 + j // 4, (j % 4) * 128 + oc0:(j % 4) * 128 + oc0 + NCH - 1]
                                P.op("pe", lambda e, oap=oap, pb_=pb_, q=q, reim=reim, kk=kk, rhs=rhs, last=last: e.matmul(
                                    oap, pcpb_v[:, pb_, q, reim, kk, :], rhs, start=False, stop=last, tile_position=(0, 32 * q),
                                    skip_group_check=True),
                                    [("pcpb", pb_), ("XS",)], [("ps", B0 + j // 4)])
                zJC = ycat_v[:, ft, :].rearrange("p (c j) -> p j c", j=TC)
                for jb in range(4):
                    P.op("act", lambda e, jb=jb, Yb=Yb, zJC=zJC: e.activation(out=zJC[:, 4 * jb:4 * jb + 4, :], in_=Yb[jb], func=AF.Gelu_apprx_tanh),
                         [("ps", B0 + jb)], [("ycat", ft, bl) for bl in range(4)])

        def glu(s):
            for blk in range(4):
                cols = slice(blk * 512, (blk + 1) * 512)
                bs = []
                for m in range(4):
                    b = nextbank()
                    bs.append(b)
                    for kt in range(4):
                        P.op("pe", lambda e, b=b, m=m, kt=kt, cols=cols: e.matmul(psb[b], wglu_v[:, kt, m * 128:(m + 1) * 128], ycat_v[:, kt, cols],
                                                                                  start=(kt == 0), stop=(kt == 3)),
                             [("wglu", kt), ("ycat", kt, blk)], [("ps", b)])
                for m in range(4):
                    sl = m % 2
                    P.op("act", lambda e, b=bs[m], m=m, sl=sl: e.activation(out=xst[:, sl, 0:512], in_=psb[b], func=AF.Sigmoid, bias=bglu4[:, m:m + 1]),
                         [("ps", bs[m]), ("bglu4",)], [("xst", sl)])
                    P.op("pool", lambda e, m=m, sl=sl, cols=cols: e.tensor_tensor(out=ycat_v[:, m, cols], in0=ycat_v[:, m, cols], in1=xst[:, sl, 0:512], op=ALU.mult),
                         [("xst", sl), ("ycat", m, blk)], [("ycat", m, blk)])

        def wout_load(s):
            dma("pool", wout_v, w_out.rearrange("(k p) c -> p k c", p=128), [], [("wout", kt) for kt in range(8)])

        def wout_phase(s):
            SK = 2
            for it in range(NTT + SK):
                if it < NTT:
                    tt_ = it
                    sl = tt_ % 2
                    dma("sp", xst[:, sl, :], x[s, tt_ * 128:(tt_ + 1) * 128, :], [], [("xst", sl)])
                    for hf in range(2):
                        b = nextbank()
                        for kt in range(8):
                            P.op("pe", lambda e, kt=kt, b=b, hf=hf, tt_=tt_: e.matmul(
                                psb[b], ycat_v[:, kt, tt_ * 128:(tt_ + 1) * 128], wout_v[:, kt, hf * 512:(hf + 1) * 512],
                                start=(kt == 0), stop=(kt == 7)),
                                [("ycat", kt, tt_ // 4), ("wout", kt)], [("ps", b)])
                        P.op("dve", lambda e, b=b, hf=hf, tt_=tt_, sl=sl: e.tensor_tensor(
                            out=h_v[:, tt_, hf * 512:(hf + 1) * 512], in0=psb[b], in1=xst[:, sl, hf * 512:(hf + 1) * 512], op=ALU.add),
                            [("ps", b), ("xst", sl)], [("h", tt_, hf)])
                    norm_stage(h_v[:, tt_, :], tt_, 1, [("h", tt_, 0), ("h", tt_, 1)])
                if it - SK >= 0:
                    transpose_stage(it - SK, 1)

        def ffn(s):
            starts = [sum(FFN_PARTS[:i]) for i in range(len(FFN_PARTS))]

            def load_wup(pi):
                npt, t0, wb = FFN_PARTS[pi], starts[pi], pi % 2
                for gv in range(2):
                    c0 = gv * DFF + t0 * 128
                    for kt in range(8):
                        dma("pool", wup_v[wb][:, kt, gv, 0:npt * 128], w_up[kt * 128:(kt + 1) * 128, c0:c0 + npt * 128], [], [("wup", wb, kt, gv)])

            def load_wdn(pi):
                npt, t0, wb = FFN_PARTS[pi], starts[pi], pi % 2
                dma("pool", wdn_v[wb][:, 0:npt, :], w_down[t0 * 128:(t0 + npt) * 128, :].rearrange("(t p) d -> p t d", p=128), [], [("wdn", wb)])

            def up_tile(pi, tl):
                t0, wb = starts[pi], pi % 2
                ftile = t0 + tl
                for gv in range(2):
                    B0 = 4 * gv
                    ch = gv * NFT + ftile
                    for blk in range(4):
                        for kt in range(8):
                            P.op("pe", lambda e, B0=B0, blk=blk, kt=kt, wb=wb, gv=gv, tl=tl: e.matmul(
                                ps[:, B0 + blk, :], wup_v[wb][:, kt, gv, tl * 128:(tl + 1) * 128], XT[:, kt, blk * 512:(blk + 1) * 512],
                                start=(kt == 0), stop=(kt == 7)),
                                [("wup", wb, kt, gv)] + [("XT", t4) for t4 in range(blk * 4, blk * 4 + 4)], [("ps", B0 + blk)])
                    row = ps[:, B0:B0 + 4, :].rearrange("p b n -> p (b n)")
                    for blk in range(4):
                        a_ = acc_v[:, blk, :]
                        c_lo, c_hi = blk * 512, (blk + 1) * 512
                        P.op("act", lambda e, a_=a_, row=row, c_lo=c_lo, c_hi=c_hi, ch=ch: e.activation(
                            out=a_, in_=row[:, c_lo:c_hi], func=AF.Identity, scale=cw3[:, 1, ch:ch + 1], bias=cb1[:, ch:ch + 1]),
                            [("ps", B0 + blk), ("cw3", 1), ("cb1",)], [("acc", blk)])
                    for blk in range(4):
                        a_ = acc_v[:, blk, :]
                        ra = ("acc", blk)
                        c_lo, c_hi = blk * 512, (blk + 1) * 512
                        rd = [("ps", B0 + blk)]
                        lo = 1 if blk == 0 else 0
                        P.op("dve", lambda e, a_=a_, row=row, c_lo=c_lo, c_hi=c_hi, ch=ch, lo=lo: e.scalar_tensor_tensor(
                            out=a_[:, lo:512], in0=row[:, c_lo + lo - 1:c_hi - 1], scalar=cw3[:, 0, ch:ch + 1], in1=a_[:, lo:512],
                            op0=ALU.mult, op1=ALU.add),
                            rd + ([("ps", B0 + blk - 1)] if blk > 0 else []) + [ra, ("cw3", 0)], [ra])
                        hi = 511 if blk == 3 else 512
                        P.op("dve", lambda e, a_=a_, row=row, c_lo=c_lo, c_hi=c_hi, ch=ch, hi=hi: e.scalar_tensor_tensor(
                            out=a_[:, 0:hi], in0=row[:, c_lo + 1:c_lo + hi + 1], scalar=cw3[:, 2, ch:ch + 1], in1=a_[:, 0:hi],
                            op0=ALU.mult, op1=ALU.add),
                            rd + ([("ps", B0 + blk + 1)] if blk < 3 else []) + [ra, ("cw3", 2)], [ra])
                        if gv == 0:
                            P.op("act", lambda e, a_=a_, blk=blk: e.activation(out=gact4[blk], in_=a_, func=AF.Gelu_apprx_tanh),
                                 [ra], [("gact", blk)])
                        else:
                            P.op("pool", lambda e, a_=a_, blk=blk, wb=wb, tl=tl, c_lo=c_lo, c_hi=c_hi: e.tensor_tensor(
                                out=act_v[wb][:, tl, c_lo:c_hi], in0=a_, in1=gact4[blk], op=ALU.mult),
                                [ra, ("gact", blk)], [("act", wb, tl, blk)])

            def down(pis, last=False):
                tiles = [(pi % 2, tl) for pi in pis for tl in range(FFN_PARTS[pi])]
                ostg = wup_v[0].rearrange("p k g c -> p (k g c)")[:, 0:6144].bitcast(F32).rearrange("p (b n) -> p b n", b=3)
                ores = [("wup", 0, kt, gv) for kt in range(8) for gv in range(2)]
                for tt_ in range(NTT):
                    for hf in range(2):
                        b = nextbank()
                        for n_, (wb, tl) in enumerate(tiles):
                            P.op("pe", lambda e, b=b, tl=tl, tt_=tt_, hf=hf, wb=wb, n_=n_: e.matmul(
                                psb[b], act_v[wb][:, tl, tt_ * 128:(tt_ + 1) * 128], wdn_v[wb][:, tl, hf * 512:(hf + 1) * 512],
                                start=(n_ == 0), stop=(n_ == len(tiles) - 1)),
                                [("act", wb, tl, tt_ // 4), ("wdn", wb)], [("ps", b)])
                        P.op("dve", lambda e, b=b, tt_=tt_, hf=hf: e.tensor_tensor(
                            out=h_v[:, tt_, hf * 512:(hf + 1) * 512], in0=psb[b], in1=h_v[:, tt_, hf * 512:(hf + 1) * 512], op=ALU.add),
                            [("ps", b), ("h", tt_, hf)], [("h", tt_, hf)])
                    if last:
                        norm_stage(h_v[:, tt_, :], tt_, 2, [("h", tt_, 0), ("h", tt_, 1)], scale_out=False)
                        ob = tt_ % 3
                        P.op("dve", lambda e, tt_=tt_, ob=ob: e.scalar_tensor_tensor(out=ostg[:, ob, :], in0=h_v[:, tt_, :], scalar=ss[:, 2, tt_:tt_ + 1],
                                                                                    in1=gfin[:], op0=ALU.mult, op1=ALU.mult),
                             [("h", tt_, 0), ("h", tt_, 1), ("ss", 2, tt_), ("gfin",)], ores + [("ostg", ob)])
                        dma("sp", out[s, tt_ * 128:(tt_ + 1) * 128, :], ostg[:, ob, :], [("ostg", ob)], [("out", s, tt_)])

            NP = len(FFN_PARTS)
            assert NP % 2 == 0
            load_wup(0); load_wup(1); load_wdn(0); load_wdn(1)
            for pi in range(NP):
                for tl in range(FFN_PARTS[pi]):
                    up_tile(pi, tl)
                    if tl == 0:
                        if pi % 2 == 0 and pi >= 2:
                            load_wup(pi + 1); load_wdn(pi); load_wdn(pi + 1)
                        if pi % 2 == 1 and pi + 1 < NP:
                            load_wup(pi + 1)
                if pi % 2 == 1:
                    down([pi - 1, pi], last=(pi == NP - 1))

        def final(s):
            for tt_ in range(NTT):
                norm_stage(h_v[:, tt_, :], tt_, 2, [("h", tt_, 0), ("h", tt_, 1)], scale_out=False)
                sl = tt_ % 2
                P.op("dve", lambda e, tt_=tt_, sl=sl: e.scalar_tensor_tensor(out=xst[:, sl, :], in0=h_v[:, tt_, :], scalar=ss[:, 2, tt_:tt_ + 1],
                                                                            in1=gfin[:], op0=ALU.mult, op1=ALU.mult),
                     [("h", tt_, 0), ("h", tt_, 1), ("ss", 2, tt_), ("gfin",)], [("xst", sl)])
                dma("sp", out[s, tt_ * 128:(tt_ + 1) * 128, :], xst[:, sl, :], [("xst", sl)], [("out", s, tt_)])

        def dump(nm, ap2d, reads):
            if nm in dbg_out:
                dma("sp", dbg_out[nm], ap2d, reads, [("dbg", nm)])

        MIXN = ["u", "vT", "AB", "ycat", "XV", "XS", "s5s", "win", "dft", "wout", "lagb", "pcpb"]
        import os as _os
        dma("pool", win_v, w_in.rearrange("(k p) c -> p k c", p=128), [], [("win", kt) for kt in range(8)])
        P.phase = 'setup'
        s5_setup()
        barrier(["SU", "XT"] + [n for n in MIXN if n != "win"])
        nseq = 2 if upto == "all" else 1
        for s in range(nseq):
            P.phase = 'mix_front%d' % s
            s5_prefetch(s)
            mix_front(s, "all")
            if upto == "front":
                dump("u", arena[:, 0:4 * L], [("u", f, b_) for f in range(4) for b_ in range(4)])
                dump("vT", arena[:, 4 * L:8 * L], [("vT", f, b_) for f in range(4) for b_ in range(4)])
                dump("win", arena[:, WOFF // 2:WOFF // 2 + 8 * DM], [("win", k_) for k_ in range(8)])
                dump("XT", XT[:].rearrange("p k t -> p (k t)"), [("XT", t_) for t_ in range(NTT)])
                break
            P.phase = 's5_states%d' % s
            s5_states(s)
            P.phase = 's5_scan%d' % s
            s5_scan(s)
            P.phase = 'fourier%d' % s
            fourier(s)
            barrier(["AB", "dft", "lagb", "pcpb"])
            P.phase = 's5_out%d' % s
            s5_out(s)
            barrier(["XV", "XS", "wout"])
            wout_load(s)
            P.phase = 'glu%d' % s
            glu(s)
            if upto == "mix":
                dump("ycat", arena[:, R1 // 2:R1 // 2 + 8 * L], [("ycat", f, b_) for f in range(8) for b_ in range(4)])
                dump("X", X_v, [("XS",)])
                break
            barrier(["u", "vT", "AB", "lagb", "pcpb", "h", "win", "dft"])
            P.phase = 'wout_phase%d' % s
            wout_phase(s)
            if upto == "wout":
                dump("h", arena[:, 0:32 * KB].bitcast(F32), [("h", t_, hf) for t_ in range(NTT) for hf in range(2)])
                break
            barrier(MIXN + FFN_NAMES)
            P.phase = 'ffn%d' % s
            ffn(s)
            barrier(MIXN + FFN_NAMES + ["h", "ostg"])
        P.op("sp", None, [("out", s_, t_) for s_ in range(nseq) for t_ in range(NTT)] + [("dbg", nm) for nm in dbg_out], [("done",)])

        allsems = list(csem.values()) + rings["sp"] + rings["pool"]
        with nc.Block() as block0:
            def _clr(e):
                for sm in allsems:
                    e.sem_clear(sm)
            block0.sync(_clr)
        with nc.Block() as block:
            P.emit(nc, block, csem, rings)
    return nc, P


def _in_maps(inputs):
    global _CONSTS
    if _CONSTS is None:
        _CONSTS = _host_consts()
    f = lambda a: np.ascontiguousarray(np.asarray(a, dtype=np.float32))
    shared = {
        "g_mix": f(inputs["g_mix"][0]), "w_in": f(inputs["w_in"][0]),
        "lam_re": f(inputs["ssm_lam_re"][0]), "lam_im": f(inputs["ssm_lam_im"][0]), "log_dt": f(inputs["ssm_log_dt"][0]),
        "b_re": f(inputs["ssm_b_re"][0]), "b_im": f(inputs["ssm_b_im"][0]), "c_re": f(inputs["ssm_c_re"][0]), "c_im": f(inputs["ssm_c_im"][0]),
        "ssm_d": f(inputs["ssm_d"][0]), "w_glu": f(inputs["w_glu"][0]), "b_glu": f(inputs["b_glu"][0]),
        "w_fourier": f(inputs["w_fourier"][0]), "w_out": f(inputs["w_out"][0]), "g_ffn": f(inputs["g_ffn"][0]),
        "w_up": f(inputs["w_up"][0]), "conv_w": f(inputs["conv_w"][0]), "conv_b": f(inputs["conv_b"][0]),
        "w_down": f(inputs["w_down"][0]), "g_final": f(inputs["g_final"]),
    }
    c = _CONSTS
    shared.update({"identb": c["identb"], "identf": c["identf"], "cdft": c["cdft"], "sdft": c["sdft"],
                   "dftT": c["dftT"].reshape(16, 128, 2 * 16 * 128), "msk": c["msk"], "bdmask": c["bdmask"]})
    xs = f(inputs["x"])
    maps = []
    for c_ in range(NCORES):
        m = dict(shared)
        m["x"] = xs[2 * c_:2 * c_ + 2]
        maps.append(m)
    return maps


def kernel(**inputs):
    nc, _ = build()
    res = run_bass_kernel_spmd(nc, _in_maps(inputs), core_ids=list(range(NCORES)))
    return np.concatenate([np.asarray(r["out"], dtype=np.float32) for r in res.results], axis=0)
```
